# Optimizing a Trainium2 kernel written in Bass

```python
import jax
import jax.numpy as jnp
from jax import lax
import numpy as np

D_MODEL = 1024
BATCH = 8
SEQ = 2048
DEPTH = 2

GRID_W = 64
CTX_LEN = 256
QBLK = 128
ROPE_THETA = 10000.0
EPS = 1e-6

POOL_GROUPS = 4
POOL_CH = 64
POOL_DIM = POOL_GROUPS * POOL_CH
POOL_WINDOWS = (2, 4, 8, 16)

HEAD_DIM = 64
GQA_HEADS = 12
GQA_KV_HEADS = 4
GQA_GROUP = GQA_HEADS // GQA_KV_HEADS
GQA_Q_DIM = GQA_HEADS * HEAD_DIM
GQA_KV_DIM = GQA_KV_HEADS * HEAD_DIM
L0_IN = POOL_DIM + GQA_Q_DIM + 2 * GQA_KV_DIM
L0_MIX = POOL_DIM + GQA_Q_DIM

MLA_HEADS = 8
MLA_NOPE = 64
MLA_ROPE = 32
MLA_QK = MLA_NOPE + MLA_ROPE
MLA_V = 64
MLA_Q_LORA = 384
MLA_KV_LORA = 256

NA_HEADS = 8
NA_HEAD_DIM = 64
NA_DIM = NA_HEADS * NA_HEAD_DIM
NA_ROWS = 8
NA_COLS = 16

L1_IN = MLA_Q_LORA + MLA_KV_LORA + MLA_ROPE + 3 * NA_DIM
L1_MIX = MLA_HEADS * MLA_V + NA_DIM

FFN_DIM = -(-8 * D_MODEL // (3 * 256)) * 256

kernel_name = "hybrid_pool_gqa_mla_na_dit_trunk"


def rmsnorm(x, g):
    xf = x.astype(jnp.float32)
    y = xf * lax.rsqrt(jnp.mean(xf * xf, axis=-1, keepdims=True) + EPS)
    return y.astype(x.dtype) * g


def rope_1d(x, pos):
    d = x.shape[-1]
    half = d // 2
    inv = (ROPE_THETA ** (-np.arange(half, dtype=np.float32) * 2.0 / d)).astype(np.float32)
    ang = pos.astype(jnp.float32)[:, None] * inv[None, :]
    shape = (1, x.shape[1]) + (1,) * (x.ndim - 3) + (half,)
    cos = jnp.cos(ang).reshape(shape)
    sin = jnp.sin(ang).reshape(shape)
    xf = x.astype(jnp.float32)
    x1, x2 = xf[..., :half], xf[..., half:]
    return jnp.concatenate([x1 * cos - x2 * sin, x1 * sin + x2 * cos], axis=-1).astype(x.dtype)


def axial_rope(x, row, col):
    half = x.shape[-1] // 2
    return jnp.concatenate([rope_1d(x[..., :half], row), rope_1d(x[..., half:], col)], axis=-1)


def split_cols(p, sizes):
    return jnp.split(p, [int(o) for o in np.cumsum(sizes)[:-1]], axis=-1)


def block_attention(q, k, v, scale):
    B, T = q.shape[:2]
    qb = jnp.moveaxis(q.reshape((B, T // QBLK, QBLK) + q.shape[2:]), 1, 0)

    def one_block(q_blk):
        s = jnp.einsum("bqhgd,bkhd->bhgqk", q_blk, k).astype(jnp.float32) * scale
        p = jax.nn.softmax(s, axis=-1).astype(v.dtype)
        return jnp.einsum("bhgqk,bkhd->bqhgd", p, v)

    o = lax.map(one_block, qb)
    return jnp.moveaxis(o, 0, 1).reshape((B, T) + o.shape[3:])


def multiscale_pool(u, w_grp, ch_scale):
    B, T, _ = u.shape
    ug = u.reshape(B, T, POOL_GROUPS, POOL_CH).astype(jnp.float32)
    csum = jnp.concatenate([jnp.zeros((B, 1, POOL_GROUPS, POOL_CH), jnp.float32),
                            jnp.cumsum(ug, axis=1)], axis=1)
    t = np.arange(T)
    lo = np.stack([np.clip(t - w // 2, 0, T) for w in POOL_WINDOWS], axis=-1).astype(np.int32)
    hi = np.stack([np.clip(t - w // 2 + w, 0, T) for w in POOL_WINDOWS], axis=-1).astype(np.int32)
    grp = np.arange(POOL_GROUPS, dtype=np.int32)[None, :]
    win_sum = csum[:, hi, grp] - csum[:, lo, grp]
    count = jnp.asarray((hi - lo).astype(np.float32))[None, :, :, None]
    d = (win_sum / count - ug).astype(u.dtype)
    y = jnp.einsum("btgc,gcd->btgd", d, w_grp).reshape(B, T, POOL_DIM)
    return y * ch_scale


def neighbourhood_attention(q, k, v, k_ctx, v_ctx, rpb, scale):
    B, S, H, dh = q.shape
    rows = S // GRID_W
    kr = min(NA_ROWS, rows)
    kc = NA_COLS
    cols = np.arange(GRID_W)
    key_col = (np.clip(cols - kc // 2, 0, GRID_W - kc)[:, None] + np.arange(kc)[None, :]).astype(np.int32)
    dc_idx = (key_col - cols[:, None] + (NA_COLS - 1)).astype(np.int32)
    n_win = kr * kc
    q_rows = jnp.moveaxis(q.reshape(B, rows, GRID_W, H, dh), 1, 0)

    def one_row(args):
        r, q_r = args
        r0 = jnp.clip(r - kr // 2, 0, rows - kr)
        key_row = r0 + jnp.arange(kr, dtype=jnp.int32)
        idx = (key_row[None, :, None] * GRID_W + key_col[:, None, :]).reshape(GRID_W, n_win)
        k_w = jnp.take(k, idx, axis=1)
        v_w = jnp.take(v, idx, axis=1)
        bias = rpb[:, key_row - r + (NA_ROWS - 1)][:, :, dc_idx]
        bias = jnp.transpose(bias, (0, 2, 1, 3)).reshape(H, GRID_W, n_win)
        s_win = jnp.einsum("bqhd,bqkhd->bhqk", q_r, k_w).astype(jnp.float32) * scale + bias.astype(jnp.float32)
        s_ctx = jnp.einsum("bqhd,bchd->bhqc", q_r, k_ctx).astype(jnp.float32) * scale
        p = jax.nn.softmax(jnp.concatenate([s_win, s_ctx], axis=-1), axis=-1).astype(v.dtype)
        return (jnp.einsum("bhqk,bqkhd->bqhd", p[..., :n_win], v_w)
                + jnp.einsum("bhqc,bchd->bqhd", p[..., n_win:], v_ctx))

    o = lax.map(one_row, (jnp.arange(rows, dtype=jnp.int32), q_rows))
    return jnp.moveaxis(o, 0, 1).reshape(B, S, H, dh)


def adaln(cond, w, b):
    d = w.shape[0]
    m = jax.nn.silu(cond) @ w + b
    return [m[:, None, i * d:(i + 1) * d] for i in range(6)]


def modulate(x, g, shift, scale):
    return rmsnorm(x, g) * (1.0 + scale) + shift


def swiglu(h, wg, wu, wd):
    return (jax.nn.silu(h @ wg) * (h @ wu)) @ wd


def mixer_pool_gqa(h, hc, row, col, need_ctx, w_in, pool_w, pool_scale, q_gain, k_gain, w_out):
    B, S, _ = h.shape
    Tc = hc.shape[1]
    sizes = (POOL_DIM, GQA_Q_DIM, GQA_KV_DIM, GQA_KV_DIM)
    a, q, k, v = split_cols(h @ w_in, sizes)
    ac, qc, kc, vc = split_cols(hc @ w_in, sizes)
    q = axial_rope(rmsnorm(q.reshape(B, S, GQA_KV_HEADS, GQA_GROUP, HEAD_DIM), q_gain), row, col)
    k = axial_rope(rmsnorm(k.reshape(B, S, GQA_KV_HEADS, HEAD_DIM), k_gain), row, col)
    v = v.reshape(B, S, GQA_KV_HEADS, HEAD_DIM)
    kc = rmsnorm(kc.reshape(B, Tc, GQA_KV_HEADS, HEAD_DIM), k_gain)
    vc = vc.reshape(B, Tc, GQA_KV_HEADS, HEAD_DIM)
    scale = HEAD_DIM ** -0.5
    attn = block_attention(q, jnp.concatenate([kc, k], axis=1), jnp.concatenate([vc, v], axis=1),
                           scale).reshape(B, S, GQA_Q_DIM)
    out = jnp.concatenate([multiscale_pool(a, pool_w, pool_scale), attn], axis=-1) @ w_out
    if not need_ctx:
        return out, None
    qc = rmsnorm(qc.reshape(B, Tc, GQA_KV_HEADS, GQA_GROUP, HEAD_DIM), q_gain)
    attn_c = block_attention(qc, kc, vc, scale).reshape(B, Tc, GQA_Q_DIM)
    out_c = jnp.concatenate([multiscale_pool(ac, pool_w, pool_scale), attn_c], axis=-1) @ w_out
    return out, out_c


def mixer_mla_na(h, hc, row, col, need_ctx, w_in, q_a_gain, kv_a_gain, w_uq, w_ukv,
                 mla_q_gain, mla_k_gain, na_q_gain, na_k_gain, na_rpb, w_out):
    B, S, _ = h.shape
    Tc = hc.shape[1]
    sizes = (MLA_Q_LORA, MLA_KV_LORA, MLA_ROPE, NA_DIM, NA_DIM, NA_DIM)
    cq, ckv, kr, nq, nk, nv = split_cols(h @ w_in, sizes)
    cqc, ckvc, krc, nqc, nkc, nvc = split_cols(hc @ w_in, sizes)

    def mla_q(cq_, T):
        q_ = (rmsnorm(cq_, q_a_gain) @ w_uq).reshape(B, T, MLA_HEADS, MLA_QK)
        return rmsnorm(q_, mla_q_gain)

    def mla_kv(ckv_, kr_, T):
        kv = (rmsnorm(ckv_, kv_a_gain) @ w_ukv).reshape(B, T, MLA_HEADS, MLA_NOPE + MLA_V)
        k_rope = jnp.broadcast_to(kr_[:, :, None, :], (B, T, MLA_HEADS, MLA_ROPE))
        k_ = jnp.concatenate([kv[..., :MLA_NOPE], k_rope], axis=-1)
        return rmsnorm(k_, mla_k_gain), kv[..., MLA_NOPE:]

    def rope_tail(t):
        return jnp.concatenate([t[..., :MLA_NOPE], axial_rope(t[..., MLA_NOPE:], row, col)], axis=-1)

    mq = rope_tail(mla_q(cq, S))
    mk, mv = mla_kv(ckv, kr, S)
    mk = rope_tail(mk)
    mkc, mvc = mla_kv(ckvc, krc, Tc)
    mla_scale = MLA_QK ** -0.5
    o_mla = block_attention(mq[:, :, :, None], jnp.concatenate([mkc, mk], axis=1),
                            jnp.concatenate([mvc, mv], axis=1), mla_scale).reshape(B, S, MLA_HEADS * MLA_V)

    q_na = rmsnorm(nq.reshape(B, S, NA_HEADS, NA_HEAD_DIM), na_q_gain)
    k_na = rmsnorm(nk.reshape(B, S, NA_HEADS, NA_HEAD_DIM), na_k_gain)
    v_na = nv.reshape(B, S, NA_HEADS, NA_HEAD_DIM)
    k_nac = rmsnorm(nkc.reshape(B, Tc, NA_HEADS, NA_HEAD_DIM), na_k_gain)
    v_nac = nvc.reshape(B, Tc, NA_HEADS, NA_HEAD_DIM)
    na_scale = NA_HEAD_DIM ** -0.5
    o_na = neighbourhood_attention(q_na, k_na, v_na, k_nac, v_nac, na_rpb, na_scale).reshape(B, S, NA_DIM)

    out = jnp.concatenate([o_mla, o_na], axis=-1) @ w_out
    if not need_ctx:
        return out, None
    mqc = mla_q(cqc, Tc)
    o_mla_c = block_attention(mqc[:, :, :, None], mkc, mvc, mla_scale).reshape(B, Tc, MLA_HEADS * MLA_V)
    q_nac = rmsnorm(nqc.reshape(B, Tc, NA_HEADS, NA_HEAD_DIM), na_q_gain)
    o_na_c = block_attention(q_nac[:, :, :, None], k_nac, v_nac, na_scale).reshape(B, Tc, NA_DIM)
    out_c = jnp.concatenate([o_mla_c, o_na_c], axis=-1) @ w_out
    return out, out_c


def setup_inputs(seed: int = 0) -> dict:
    key = jax.random.key(seed)
    ks = iter(jax.random.split(key, 48))

    def nrm(shape, s):
        return jax.random.normal(next(ks), shape, jnp.float32) * s

    def gain(n):
        return 1.0 + nrm((n,), 0.1)

    D = D_MODEL
    inp = {}
    inp["x"] = nrm((BATCH, SEQ, D), 1.0)
    inp["c"] = nrm((BATCH, D), 1.0)
    inp["ctx"] = nrm((BATCH, CTX_LEN, D), 1.0)
    inp["c_ctx"] = nrm((D,), 1.0)
    inp["l0_ada_w"] = nrm((D, 6 * D), 0.5 * D ** -0.5)
    inp["l0_ada_b"] = nrm((6 * D,), 0.02)
    inp["l0_norm_mix"] = gain(D)
    inp["l0_norm_ffn"] = gain(D)
    inp["l0_w_in"] = nrm((D, L0_IN), D ** -0.5)
    inp["l0_pool_w"] = nrm((POOL_GROUPS, POOL_CH, POOL_CH), POOL_CH ** -0.5)
    inp["l0_pool_scale"] = gain(POOL_DIM)
    inp["l0_q_gain"] = gain(HEAD_DIM)
    inp["l0_k_gain"] = gain(HEAD_DIM)
    inp["l0_w_out"] = nrm((L0_MIX, D), L0_MIX ** -0.5)
    inp["l0_ffn_w_gate"] = nrm((D, FFN_DIM), D ** -0.5)
    inp["l0_ffn_w_up"] = nrm((D, FFN_DIM), D ** -0.5)
    inp["l0_ffn_w_down"] = nrm((FFN_DIM, D), FFN_DIM ** -0.5)
    inp["l1_ada_w"] = nrm((D, 6 * D), 0.5 * D ** -0.5)
    inp["l1_ada_b"] = nrm((6 * D,), 0.02)
    inp["l1_norm_mix"] = gain(D)
    inp["l1_norm_ffn"] = gain(D)
    inp["l1_w_in"] = nrm((D, L1_IN), D ** -0.5)
    inp["l1_mla_q_a_gain"] = gain(MLA_Q_LORA)
    inp["l1_mla_kv_a_gain"] = gain(MLA_KV_LORA)
    inp["l1_mla_w_uq"] = nrm((MLA_Q_LORA, MLA_HEADS * MLA_QK), MLA_Q_LORA ** -0.5)
    inp["l1_mla_w_ukv"] = nrm((MLA_KV_LORA, MLA_HEADS * (MLA_NOPE + MLA_V)), MLA_KV_LORA ** -0.5)
    inp["l1_mla_q_gain"] = gain(MLA_QK)
    inp["l1_mla_k_gain"] = gain(MLA_QK)
    inp["l1_na_q_gain"] = gain(NA_HEAD_DIM)
    inp["l1_na_k_gain"] = gain(NA_HEAD_DIM)
    inp["l1_na_rpb"] = nrm((NA_HEADS, 2 * NA_ROWS - 1, 2 * NA_COLS - 1), 0.1)
    inp["l1_w_out"] = nrm((L1_MIX, D), L1_MIX ** -0.5)
    inp["l1_ffn_w_gate"] = nrm((D, FFN_DIM), D ** -0.5)
    inp["l1_ffn_w_up"] = nrm((D, FFN_DIM), D ** -0.5)
    inp["l1_ffn_w_down"] = nrm((FFN_DIM, D), FFN_DIM ** -0.5)
    return inp


def reference(x, c, ctx, c_ctx,
              l0_ada_w, l0_ada_b, l0_norm_mix, l0_norm_ffn, l0_w_in, l0_pool_w, l0_pool_scale,
              l0_q_gain, l0_k_gain, l0_w_out, l0_ffn_w_gate, l0_ffn_w_up, l0_ffn_w_down,
              l1_ada_w, l1_ada_b, l1_norm_mix, l1_norm_ffn, l1_w_in, l1_mla_q_a_gain, l1_mla_kv_a_gain,
              l1_mla_w_uq, l1_mla_w_ukv, l1_mla_q_gain, l1_mla_k_gain, l1_na_q_gain, l1_na_k_gain,
              l1_na_rpb, l1_w_out, l1_ffn_w_gate, l1_ffn_w_up, l1_ffn_w_down):
    S = x.shape[1]
    tok = jnp.arange(S, dtype=jnp.int32)
    row = tok // GRID_W
    col = tok % GRID_W
    layers = [
        (l0_ada_w, l0_ada_b, l0_norm_mix, l0_norm_ffn, l0_ffn_w_gate, l0_ffn_w_up, l0_ffn_w_down,
         mixer_pool_gqa, (l0_w_in, l0_pool_w, l0_pool_scale, l0_q_gain, l0_k_gain, l0_w_out)),
        (l1_ada_w, l1_ada_b, l1_norm_mix, l1_norm_ffn, l1_ffn_w_gate, l1_ffn_w_up, l1_ffn_w_down,
         mixer_mla_na, (l1_w_in, l1_mla_q_a_gain, l1_mla_kv_a_gain, l1_mla_w_uq, l1_mla_w_ukv,
                        l1_mla_q_gain, l1_mla_k_gain, l1_na_q_gain, l1_na_k_gain, l1_na_rpb, l1_w_out)),
    ]
    xc = ctx
    for i in range(DEPTH):
        ada_w, ada_b, n_mix, n_ffn, wg, wu, wd, mixer, mparams = layers[i]
        need_ctx = i < DEPTH - 1
        sh1, sc1, g1, sh2, sc2, g2 = adaln(c, ada_w, ada_b)
        csh1, csc1, cg1, csh2, csc2, cg2 = adaln(c_ctx[None, :], ada_w, ada_b)
        h = modulate(x, n_mix, sh1, sc1)
        hc = modulate(xc, n_mix, csh1, csc1)
        o, oc = mixer(h, hc, row, col, need_ctx, *mparams)
        x = x + g1 * o
        x = x + g2 * swiglu(modulate(x, n_ffn, sh2, sc2), wg, wu, wd)
        if need_ctx:
            xc = xc + cg1 * oc
            xc = xc + cg2 * swiglu(modulate(xc, n_ffn, csh2, csc2), wg, wu, wd)
    return x
```

```python
import numpy as np
import ml_dtypes
import concourse.bass as bass
import concourse.mybir as mybir
from concourse.bass_utils import run_bass_kernel_spmd

F32 = mybir.dt.float32
BF16 = mybir.dt.bfloat16
ALU = mybir.AluOpType
AF = mybir.ActivationFunctionType
AX = mybir.AxisListType

D = 1024
S = 2048
CTX = 256
NT = S + CTX
FFN = 2816
NJ = FFN // 128
EPS = 1e-6
STAGE = "full"
Q_PERM = [0, 3, 1, 4, 2, 5, 6, 9, 7, 10, 8, 11]


class Res:
    __slots__ = ("name", "w", "r", "excl")

    def __init__(self, name):
        self.name = name
        self.w = None
        self.r = {}
        self.excl = False


class Op:
    __slots__ = ("eng", "fn", "deps", "idx", "signal", "semval", "dma", "dsem", "dval", "dprev", "selfsig", "bg")


class Prog:
    ENG = ("pe", "act", "dve", "pool", "sp")

    def __init__(self, nc, esems, dsems, streams):
        self.nc = nc
        self.h = {"pe": nc.tensor, "act": nc.scalar, "dve": nc.vector, "pool": nc.gpsimd, "sp": nc.sync}
        self.esem = esems
        self.dsems = dsems
        self.streams = streams
        self.rr = {k: 0 for k in streams}
        self.dtot = [0] * len(dsems)
        self.dlast = [None] * len(dsems)
        self.ops = []
        self.last = {e: None for e in self.ENG}
        self.pending = {e: [] for e in self.ENG}
        self.resd = {}

    def res(self, *key):
        r = self.resd.get(key)
        if r is None:
            r = Res(key)
            self.resd[key] = r
        return r

    def op(self, eng, fn, reads=(), writes=(), dma=0, stream="misc", extra=(), selfsig=False, bg=False):
        o = Op()
        o.eng = eng
        o.fn = fn
        o.dma = dma
        o.signal = False
        o.semval = 0
        o.selfsig = selfsig
        o.bg = bg
        o.idx = len(self.ops)
        deps = {}

        def add(d, kind):
            if d is None:
                return
            if d.dma == 0 and d.eng == eng:
                if eng == "pe" or eng == "sp":
                    return
            key = ("d", d.idx) if d.dma else ("e", d.eng)
            cur = deps.get(key)
            if cur is None or d.idx > cur.idx:
                deps[key] = d

        for r in reads:
            add(r.w, "raw")
            if r.excl:
                for rd in r.r.values():
                    if rd.eng != eng:
                        add(rd, "rar")
        for w in writes:
            add(w.w, "waw")
            for rd in w.r.values():
                add(rd, "war")
        for d in extra:
            add(d, "raw")
        for d in self.pending[eng]:
            add(d, "raw")
        self.pending[eng] = []
        for r in reads:
            key = ("d", o.idx) if dma else ("e", eng)
            r.r[key] = o
        for w in writes:
            w.w = o
            w.r = {}
        o.deps = list(deps.values())
        for d in o.deps:
            d.signal = True
        if dma:
            lst = self.streams[stream]
            si = lst[self.rr[stream] % len(lst)]
            self.rr[stream] += 1
            o.dsem = si
            o.dprev = self.dtot[si]
            self.dtot[si] += 16 * dma
            o.dval = self.dtot[si]
            self.dlast[si] = o
        self.ops.append(o)
        self.last[eng] = o
        return o

    def barrier(self):
        nc = self.nc
        extra = [self.last[e] for e in ("pe", "act", "dve", "pool") if self.last[e] is not None]
        extra += [d for d in self.dlast if d is not None and not d.bg]
        sem = self.esem["sp"]
        b = self.op("sp", lambda: nc.sync.sem_inc(sem, 1), extra=extra, selfsig=True)
        b.signal = True
        for e in ("pe", "act", "dve", "pool"):
            self.pending[e].append(b)

    def emit(self):
        cnt = {e: 0 for e in self.ENG}
        for o in self.ops:
            if o.dma == 0 and o.signal:
                cnt[o.eng] += 1
                o.semval = cnt[o.eng]
        waited = {e: {} for e in self.ENG}
        nwait = 0
        for o in self.ops:
            E = self.h[o.eng]
            wd = waited[o.eng]
            for d in o.deps:
                if d.dma:
                    key, sem, val = ("d", d.dsem), self.dsems[d.dsem], d.dval
                else:
                    key, sem, val = ("e", d.eng), self.esem[d.eng], d.semval
                if wd.get(key, 0) < val:
                    E.wait_ge(sem, val)
                    wd[key] = val
                    nwait += 1
            if o.dma:
                key = ("d", o.dsem)
                if o.dprev > 0 and wd.get(key, 0) < o.dprev:
                    E.wait_ge(self.dsems[o.dsem], o.dprev)
                    wd[key] = o.dprev
                for ins in o.fn():
                    ins.then_inc(self.dsems[o.dsem], 16)
            else:
                ins = o.fn()
                if o.signal and not o.selfsig:
                    ins.then_inc(self.esem[o.eng], 1)
        E = self.h["sp"]
        for i, t in enumerate(self.dtot):
            if t > 0 and waited["sp"].get(("d", i), 0) < t:
                E.wait_ge(self.dsems[i], t)
        return nwait


class Arena:
    def __init__(self, ap_f32, nbytes):
        self.base = ap_f32
        self.nbytes = nbytes
        self.top = 0
        self.peak = 0

    def alloc(self, shape, dt):
        esz = 2 if dt == BF16 else 4
        n = 1
        for d in shape[1:]:
            n *= d
        nb = (n * esz + 63) // 64 * 64
        off = self.top
        self.top += nb
        self.peak = max(self.peak, self.top)
        assert self.top <= self.nbytes, f"SBUF arena overflow: {self.top} > {self.nbytes}"
        self.last_off = off
        return self.view_at(off, shape, dt)

    def view_at(self, off, shape, dt):
        esz = 2 if dt == BF16 else 4
        n = 1
        for d in shape[1:]:
            n *= d
        nb = (n * esz + 63) // 64 * 64
        assert off % 4 == 0
        ap = self.base[0:shape[0], off // 4:(off + nb) // 4]
        if dt == BF16:
            ap = ap.bitcast(BF16)
        ap = ap[:, 0:n]
        if len(shape) == 3:
            ap = ap.rearrange("p (a b) -> p a b", a=shape[1])
        elif len(shape) == 4:
            ap = ap.rearrange("p (a b c) -> p a b c", a=shape[1], b=shape[2])
        elif len(shape) == 5:
            ap = ap.rearrange("p (a b c d) -> p a b c d", a=shape[1], b=shape[2], c=shape[3])
        return ap

    def scope(self):
        ar = self

        class _S:
            def __enter__(self_):
                self_.m = ar.top
                return self_

            def __exit__(self_, *a):
                ar.top = self_.m
                return False

            def enter_context(self_, x):
                return x
        return _S()


def _rope_tables(hd, n_tiles=16):
    half = hd // 2
    q = half // 2
    inv = (10000.0 ** (-np.arange(q, dtype=np.float32) * 2.0 / half)).astype(np.float32)
    tok = np.arange(S)
    row = (tok // 64).astype(np.float32)
    col = (tok % 64).astype(np.float32)
    ar = row[:, None] * inv[None, :]
    ac = col[:, None] * inv[None, :]
    cr, sr, cc, sc = np.cos(ar), np.sin(ar), np.cos(ac), np.sin(ac)
    cos = np.concatenate([cr, cr, cc, cc], axis=1).astype(np.float32)
    sin = np.concatenate([-sr, sr, -sc, sc], axis=1).astype(np.float32)
    cos = cos.reshape(n_tiles, 128, hd).transpose(1, 0, 2)
    sin = sin.reshape(n_tiles, 128, hd).transpose(1, 0, 2)
    return np.ascontiguousarray(cos), np.ascontiguousarray(sin)


def _band_tables():
    T = 384
    out = np.zeros((128, 20, 128), np.float32)
    for g, w in enumerate((2, 4, 8, 16)):
        M = np.zeros((T, T), np.float32)
        for t in range(T):
            lo = min(max(t - w // 2, 0), T)
            hi = min(max(t - w // 2 + w, 0), T)
            M[lo:hi, t] += 1.0 / (hi - lo)
            M[t, t] -= 1.0
        out[:, g * 5 + 0, :] = M[128:256, 128:256]
        out[:, g * 5 + 1, :] = M[0:128, 128:256]
        out[:, g * 5 + 2, :] = M[256:384, 128:256]
        out[:, g * 5 + 3, :] = M[0:128, 0:128]
        out[:, g * 5 + 4, :] = M[256:384, 256:384]
        assert np.array_equal(M[128:256, 0:128], out[:, g * 5 + 2, :])
        assert np.array_equal(M[128:256, 256:384], out[:, g * 5 + 1, :])
    return out


NA_NEG = -240000.0


def _na_tables():
    rows = 32
    masks = []
    keyof = {}
    plan = []
    qc = np.arange(64)
    c0 = np.clip(qc - 8, 0, 48)
    kc = np.arange(64)
    colvalid = (kc[:, None] >= c0[None, :]) & (kc[:, None] < c0[None, :] + 16)
    for i in range(16):
        r0 = [int(np.clip(2 * i + b - 4, 0, rows - 8)) for b in (0, 1)]
        jlo = min(r0) // 2
        jhi = (max(r0) + 7) // 2
        lst = []
        for j in range(jlo, jhi + 1):
            m = np.zeros((128, 128), np.float32)
            for a in (0, 1):
                for b in (0, 1):
                    kr = 2 * j + a
                    ok = (kr >= r0[b]) and (kr < r0[b] + 8)
                    blk = colvalid if ok else np.zeros((64, 64), bool)
                    m[a * 64:(a + 1) * 64, b * 64:(b + 1) * 64] = np.where(blk, 0.0, NA_NEG)
            if (m == NA_NEG).all():
                continue
            kb = m.tobytes()
            if kb not in keyof:
                keyof[kb] = len(masks)
                masks.append(m)
            lst.append((j, keyof[kb]))
        plan.append(lst)
    return plan, np.stack(masks, axis=1)


NA_PLAN, NA_MASKS = _na_tables()
NMASK = NA_MASKS.shape[1]


def _j2x8():
    m = np.zeros((128, 128), np.float32)
    for a in (0, 1):
        for k in range(64):
            m[a * 64 + k, a * 64 + 63 - k] = 8.0
    return m


CB_IDENT, CB_ONES, CB_J2, CB_BAND, CB_MASK = 0, 128, 256, 384, 384 + 2560
CB_N = CB_MASK + NMASK * 128
CF_IDENT, CF_COS0, CF_SIN0, CF_COS1, CF_SIN1 = 0, 128, 128 + 1024, 128 + 2048, 128 + 2048 + 512
CF_N = CF_SIN1 + 512
G_Q0, G_K0, G_QA, G_KVA, G_MQ, G_MK, G_NQ, G_NK, G_N = 0, 64, 128, 512, 768, 864, 960, 1024, 1088
V_ADAB, V_NMIX0, V_NFFN0, V_NMIX1, V_NFFN1, V_PSC, V_N = 0, 96, 104, 112, 120, 128, 130


def _const_arrays():
    cb = np.zeros((128, CB_N), np.float32)
    cb[:, CB_IDENT:CB_IDENT + 128] = np.eye(128, dtype=np.float32)
    cb[:, CB_ONES:CB_ONES + 128] = 1.0 / 1024.0
    cb[:, CB_J2:CB_J2 + 128] = _j2x8()
    cb[:, CB_BAND:CB_BAND + 2560] = _band_tables().reshape(128, 2560)
    cb[:, CB_MASK:] = NA_MASKS.reshape(128, NMASK * 128)
    cf = np.zeros((128, CF_N), np.float32)
    cf[:, CF_IDENT:CF_IDENT + 128] = np.eye(128, dtype=np.float32)
    c0, s0 = _rope_tables(64)
    c1, s1 = _rope_tables(32)
    cf[:, CF_COS0:CF_COS0 + 1024] = c0.reshape(128, 1024)
    cf[:, CF_SIN0:CF_SIN0 + 1024] = s0.reshape(128, 1024)
    cf[:, CF_COS1:CF_COS1 + 512] = c1.reshape(128, 512)
    cf[:, CF_SIN1:CF_SIN1 + 512] = s1.reshape(128, 512)
    return cb.astype(ml_dtypes.bfloat16), cf


class Builder:
    def __init__(self, stage):
        self.stage = stage
        self.nc = bass.Bass("TRN2", target_bir_lowering=False)

    def mm(self, out, lhsT, rhs, start, stop, r, w):
        nc = self.nc
        return self.P.op("pe", lambda: nc.tensor.matmul(out, lhsT=lhsT, rhs=rhs, start=start, stop=stop), r, w)

    def tr(self, out, in_, ident, r, w):
        nc = self.nc
        return self.P.op("pe", lambda: nc.tensor.transpose(out, in_, ident), r, w)

    def act(self, out, in_, func, r, w, bias=None, scale=None):
        nc = self.nc
        kw = {}
        if bias is not None:
            kw["bias"] = bias
        if scale is not None:
            kw["scale"] = scale
        return self.P.op("act", lambda: nc.scalar.activation(out=out, in_=in_, func=func, **kw), r, w)

    def _ve(self, eng):
        return {"dve": self.nc.vector, "pool": self.nc.gpsimd}[eng]

    def cp(self, eng, out, in_, r, w):
        nc = self.nc
        if eng == "act":
            return self.P.op("act", lambda: nc.scalar.copy(out=out, in_=in_), r, w)
        E = self._ve(eng)
        return self.P.op(eng, lambda: E.tensor_copy(out=out, in_=in_), r, w)

    def tt(self, eng, out, in0, in1, op, r, w):
        E = self._ve(eng)
        return self.P.op(eng, lambda: E.tensor_tensor(out=out, in0=in0, in1=in1, op=op), r, w)

    def stt(self, eng, out, in0, scalar, in1, op0, op1, r, w):
        E = self._ve(eng)
        return self.P.op(eng, lambda: E.scalar_tensor_tensor(out=out, in0=in0, scalar=scalar, in1=in1, op0=op0, op1=op1), r, w)

    def ts(self, eng, out, in0, s1, s2, op0, op1, r, w):
        E = self._ve(eng)
        if s2 is None:
            return self.P.op(eng, lambda: E.tensor_scalar(out=out, in0=in0, scalar1=s1, scalar2=None, op0=op0), r, w)
        return self.P.op(eng, lambda: E.tensor_scalar(out=out, in0=in0, scalar1=s1, scalar2=s2, op0=op0, op1=op1), r, w)

    def red(self, out, in_, r, w):
        nc = self.nc
        return self.P.op("dve", lambda: nc.vector.reduce_sum(out=out, in_=in_, axis=AX.X), r, w)

    def rcp(self, out, in_, r, w):
        nc = self.nc
        return self.P.op("dve", lambda: nc.vector.reciprocal(out=out, in_=in_), r, w)

    def memset(self, eng, ap, val, w):
        E = self._ve(eng)
        return self.P.op(eng, lambda: E.memset(ap, val), (), w)

    def dma(self, eng, pairs, r, w, stream="misc", bg=False):
        E = self.P.h[eng]
        return self.P.op(eng, lambda: [E.dma_start(out=o, in_=i) for (o, i) in pairs], r, w, dma=len(pairs), stream=stream, bg=bg)

    def build(self):
        nc = self.nc
        stage = self.stage
        dbg = stage != "full"
        di = {}

        def inp(name, shape, dt=F32):
            di[name] = nc.dram_tensor(name, list(shape), dt, kind="ExternalInput").ap()
            return di[name]

        x_d = inp("x", [S, D])
        ctx_d = inp("ctx", [CTX, D])
        cond_d = inp("condT", [128, 16])
        vecs_d = inp("vecs", [128, V_N])
        gains_d = inp("gains", [1, G_N])
        cb_d = inp("cbf", [128, CB_N], BF16)
        cf_d = inp("cf32", [128, CF_N])
        rpb_d = inp("rpbR", [8 * 15 * 127 + 128])
        adaw_d = [inp("l0_ada_w", [D, 6 * D]), inp("l1_ada_w", [D, 6 * D])]
        w_in_kv0 = inp("l0_w_in_kv", [D, 768])
        w_in_q0 = inp("l0_w_in_q", [D, 768])
        poolw_d = inp("l0_pool_w", [4, 64, 64])
        wout_d = [inp("l0_w_out_p", [D, D]), inp("l1_w_out", [D, D])]
        w_in1 = inp("l1_w_in", [D, 2208])
        wuq_d = inp("l1_mla_w_uq", [384, 768])
        wukv_d = inp("l1_mla_w_ukv", [256, 1024])
        wg_d = [inp("l0_ffn_w_gate", [D, FFN]), inp("l1_ffn_w_gate", [D, FFN])]
        wu_d = [inp("l0_ffn_w_up", [D, FFN]), inp("l1_ffn_w_up", [D, FFN])]
        wd_d = [inp("l0_ffn_w_down", [FFN, D]), inp("l1_ffn_w_down", [FFN, D])]
        nout = NT if dbg else S
        y_d = nc.dram_tensor("y", [nout, D], F32, kind="ExternalOutput").ap()
        wgs = [nc.dram_tensor(f"wgs{l}", [NJ, 128, 8, 128], BF16).ap() for l in range(2)]
        wus = [nc.dram_tensor(f"wus{l}", [NJ, 128, 8, 128], BF16).ap() for l in range(2)]
        wds = [nc.dram_tensor(f"wds{l}", [8, 128, NJ, 128], BF16).ap() for l in range(2)]

        from contextlib import ExitStack
        with ExitStack() as es:
            def sb(name, shape, dt):
                return AR.alloc(list(shape), dt)

            esems = {e: es.enter_context(nc.semaphore("s_" + e)) for e in Prog.ENG}
            NDS = 30
            dsems = [es.enter_context(nc.semaphore(f"d{i}")) for i in range(NDS)]
            streams = {"misc": [0, 1, 2, 3], "w": [4, 5, 6, 7], "gu": [8, 9, 10, 11], "wd": [12, 13, 14],
                       "xin": [15, 16], "out": [17, 18], "bg": [19, 20, 21, 22, 23, 24], "ada": [25, 26], "c": [27, 28, 29]}
            P = Prog(nc, esems, dsems, streams)
            self.P = P
            R = P.res

            ARENA_BYTES = 207 * 1024
            AR = Arena(es.enter_context(nc.sbuf_tensor("arena", [128, ARENA_BYTES // 4], F32)), ARENA_BYTES)
            self.AR = AR
            ps = es.enter_context(nc.psum_tensor("ps", [128, 8, 512], F32))
            PB = [R("psb", b) for b in range(8)]
            for r_ in PB:
                r_.excl = True

            def psb(b, n=512):
                return ps[:, b, 0:n]

            def psb16(b):
                return ps[:, b, :].bitcast(BF16)

            xT = sb("xT", [128, 8, NT], F32)
            mixT = sb("mixT", [128, 8, NT], BF16)
            self.mix_off = AR.last_off
            vecs = sb("vecs", [128, V_N], F32)
            mvec = sb("mvec", [128, 96, 2], F32)
            avec = sb("avec", [128, 2, 2, 2, 8], F32)
            identb = sb("identb", [128, 128], BF16)
            onesm = sb("onesm", [128, 128], BF16)
            identf = sb("identf", [128, 128], F32)

            def xres(dc, blk):
                return R("xT", dc, blk)

            def blks_of(t0, n):
                return sorted(set(min(t // 512, 4) for t in (t0, t0 + n - 1)))

            def xr(t0, n, dcs=range(8)):
                return [xres(dc, b) for dc in dcs for b in blks_of(t0, n)]

            def mres(c, blk):
                return R("mixT", c, blk)

            bgq = []

            def convert_ffn(l):
                wgv = wg_d[l].rearrange("(c p) (j n) -> j p c n", p=128, n=128)
                wuv = wu_d[l].rearrange("(c p) (j n) -> j p c n", p=128, n=128)
                wdv = wd_d[l].rearrange("(j p) (c n) -> c p j n", p=128, n=128)

                def mk(dst, src, res):
                    return lambda: self.dma("pool", [(dst, src)], (), [res], stream="bg", bg=True)
                for j in range(NJ):
                    bgq.append(mk(wgs[l][j], wgv[j], R("wgs", l, j)))
                    bgq.append(mk(wus[l][j], wuv[j], R("wus", l, j)))
                for c in range(8):
                    bgq.append(mk(wds[l][c], wdv[c], R("wds", l, c)))

            def bg_step(k):
                for _ in range(k):
                    if bgq:
                        bgq.pop(0)()

            rc = R("consts")
            self.dma("sp", [(vecs[:, :], vecs_d), (identb[:, :], cb_d[:, CB_IDENT:CB_IDENT + 128]),
                            (onesm[:, :], cb_d[:, CB_ONES:CB_ONES + 128]), (identf[:, :], cf_d[:, CF_IDENT:CF_IDENT + 128]),
                            ], (), [rc], stream="c")

            with AR.scope() as s0:
                condT = s0.enter_context(AR.alloc([128, 16], F32))
                scT = s0.enter_context(AR.alloc([128, 8, 2], BF16))
                adaw = [s0.enter_context(AR.alloc([128, 8, 1024], BF16)) for i in range(2)]
                rcond = R("condT")
                self.dma("sp", [(condT[:, :], cond_d)], (), [rcond], stream="c")
                self.act(scT[:, :, :], condT[:, :].rearrange("p (c k) -> p k c", c=2), AF.Silu, [rcond], [R("scT")])
                k = 0
                for l in range(2):
                    src = adaw_d[l].rearrange("(c p) n -> p c n", p=128)
                    for v in range(6):
                        slot = k % 2
                        k += 1
                        ra = R("adaw", slot)
                        self.dma("pool", [(adaw[slot][:, :, :], src[:, :, v * 1024:(v + 1) * 1024])], (), [ra], stream="ada")
                        for m in range(8):
                            col = (l * 48 + v * 8 + m) * 2
                            for c in range(8):
                                self.mm(ps[:, 7, col:col + 2], adaw[slot][:, c, m * 128:(m + 1) * 128], scT[:, c, :],
                                        c == 0, c == 7, [ra, R("scT")], [PB[7]])
                rmv = R("mvec")
                self.tt("dve", mvec[:, :, :], ps[:, 7, 0:192].rearrange("p (a b) -> p a b", b=2),
                        vecs[:, V_ADAB:V_ADAB + 96].unsqueeze(2).to_broadcast([128, 96, 2]), ALU.add, [PB[7], rc], [rmv])
                for l in range(2):
                    for f in range(2):
                        v = 1 + 3 * f
                        ng = vecs[:, V_NMIX0 + 16 * l + 8 * f: V_NMIX0 + 16 * l + 8 * f + 8]
                        for cnd in range(2):
                            self.stt("dve", avec[:, l, f, cnd, :], mvec[:, l * 48 + v * 8: l * 48 + v * 8 + 8, cnd], 1.0, ng,
                                     ALU.add, ALU.mult, [rmv, rc], [R("avec")])

                def MV(l, v, cnd, c):
                    return mvec[:, l * 48 + v * 8 + c, cnd:cnd + 1]

                def AV(l, f, cnd, c):
                    return avec[:, l, f, cnd, c:c + 1]
                self.MV, self.AV = MV, AV
                RMOD = [rmv, R("avec")]

                xin = [s0.enter_context(AR.alloc([128, D], F32)) for i in range(2)]
                for tt in range(18):
                    slot = tt % 2
                    rx = R("xin", slot)
                    src = x_d[tt * 128:(tt + 1) * 128, :] if tt < 16 else ctx_d[(tt - 16) * 128:(tt - 15) * 128, :]
                    self.dma("sp", [(xin[slot][:, :], src)], (), [rx], stream="xin")
                    b0 = 2 * (tt % 2)
                    for c in range(8):
                        self.tr(ps[:, b0 + c // 4, (c % 4) * 128:(c % 4 + 1) * 128], xin[slot][:, c * 128:(c + 1) * 128], identf[:, :],
                                [rx, rc], [PB[b0 + c // 4]])
                    dst = xT[:, :, tt * 128:(tt + 1) * 128]
                    for hh in range(2):
                        self.cp("act" if hh == 0 else "dve", dst[:, hh * 4:(hh + 1) * 4, :],
                                ps[:, b0 + hh, :].rearrange("p (c n) -> p c n", n=128), [PB[b0 + hh]], xr(tt * 128, 128, range(hh * 4, hh * 4 + 4)))
                P.barrier()
            import os as _os
            if stage != "load" and not _os.environ.get("K_SKIP_CONVERT"):
                convert_ffn(0)

            MIX_OFF = self.mix_off

            def modulate_g(t0, n, l, f, cnd, hT, rh, tmp, on_dve=False):
                sqb, sdt, tm = tmp
                rsq = [R("m_sq", i) for i in range(2)]
                for c in range(8):
                    i = c % 2
                    if on_dve:
                        self.tt("dve", sqb[i][:, 0:n], xT[:, c, t0:t0 + n], xT[:, c, t0:t0 + n], ALU.mult, xr(t0, n, [c]), [rsq[i]])
                    else:
                        self.act(sqb[i][:, 0:n], xT[:, c, t0:t0 + n], AF.Square, xr(t0, n, [c]), [rsq[i]])
                    self.mm(psb(7, n), onesm[:, :], sqb[i][:, 0:n], c == 0, c == 7, [rsq[i], rc], [PB[7]])
                    yield
                self.act(sdt[:, 0:n], psb(7, n), AF.Ln, [PB[7]], [R("m_sd")], bias=EPS, scale=1.0)
                yield
                self.act(sdt[:, 0:n], sdt[:, 0:n], AF.Exp, [R("m_sd")], [R("m_sd")], scale=-0.5)
                yield
                shift_v = 0 if f == 0 else 3
                for c in range(8):
                    i = c % 2
                    rt = R("m_tm", i)
                    self.stt("dve", tm[i][:, 0:n], xT[:, c, t0:t0 + n], self.AV(l, f, cnd, c), sdt[:, 0:n], ALU.mult, ALU.mult,
                             xr(t0, n, [c]) + [R("m_sd")] + RMOD, [rt])
                    yield
                    if on_dve:
                        self.ts("dve", hT[:, c, 0:n], tm[i][:, 0:n], self.MV(l, shift_v, cnd, c), None, ALU.add, None, [rt] + RMOD, [rh])
                    else:
                        self.act(hT[:, c, 0:n], tm[i][:, 0:n], AF.Identity, [rt] + RMOD, [rh], bias=self.MV(l, shift_v, cnd, c))
                    yield

            def modulate(*a):
                for _ in modulate_g(*a):
                    pass

            def mod_tmp():
                return ([AR.alloc([128, 512], BF16) for i in range(2)], AR.alloc([128, 512], F32),
                        [AR.alloc([128, 512], F32) for i in range(2)])

            def hn_alloc(w):
                t2_ = AR.alloc([128, w], F32)
                sq_ = AR.view_at(AR.last_off, [128, w], BF16)
                return (sq_, AR.alloc([128, w], F32), t2_, AR.alloc([128, 16], F32), AR.alloc([128, 16], F32))

            def headnorm_g(segs, H, Dh, gain_ap, out_bf, tmp, rout, rg, rope=None, tag=""):
                sqt, kg, t2, ss, sd = tmp
                rsq, rkg, rt2, rss, rsd = R("hn_sq" + tag), R("hn_kg" + tag), R("hn_sq" + tag), R("hn_ss" + tag), R("hn_sd" + tag)
                kgv = kg[:, 0:H * Dh].rearrange("p (h d) -> p h d", d=Dh)
                sqv = sqt[:, 0:H * Dh].rearrange("p (h d) -> p h d", d=Dh)
                for (src, h0, h1, rsrc) in segs:
                    self.act(sqv[:, h0:h1, :], src, AF.Square, rsrc, [rsq])
                    yield
                    self.tt("dve", kgv[:, h0:h1, :], src, gain_ap.unsqueeze(1).to_broadcast([128, h1 - h0, Dh]), ALU.mult, rsrc + [rg], [rkg])
                    yield
                yield "psum_done"
                self.red(ss[:, 0:H], sqv, [rsq], [rss])
                yield
                self.act(sd[:, 0:H], ss[:, 0:H], AF.Ln, [rss], [rsd], bias=EPS, scale=1.0 / Dh)
                self.act(sd[:, 0:H], sd[:, 0:H], AF.Exp, [rsd], [rsd], scale=-0.5)
                yield
                rsb = sd[:, 0:H].unsqueeze(2).to_broadcast([128, H, Dh])
                if rope is None:
                    self.tt("dve", out_bf, kgv, rsb, ALU.mult, [rkg, rsd], rout)
                    yield
                    return
                cos, sin, r0, Dr, rtab = rope
                q4 = Dr // 4
                self.tt("dve", kgv, kgv, rsb, ALU.mult, [rkg, rsd], [rkg])
                yield
                knr = kgv[:, :, r0:r0 + Dr].rearrange("p h (a b c) -> p h a b c", a=2, b=2)
                t2v = t2[:, 0:H * Dr].rearrange("p (h a b c) -> p h a b c", a=2, b=2, c=q4)
                sinv = sin.rearrange("p (a b c) -> p a b c", a=2, b=2)
                cosv = cos.unsqueeze(1).to_broadcast([128, H, Dr])
                for b in range(2):
                    self.tt("dve", t2v[:, :, :, b, :], knr[:, :, :, 1 - b, :], sinv[:, :, b, :].unsqueeze(1).to_broadcast([128, H, 2, q4]),
                            ALU.mult, [rkg, rtab], [rt2])
                    yield
                if r0 > 0:
                    self.cp("act", out_bf[:, :, 0:r0], kgv[:, :, 0:r0], [rkg], rout)
                self.tt("dve", kgv[:, :, r0:r0 + Dr], kgv[:, :, r0:r0 + Dr], cosv, ALU.mult, [rkg, rtab], [rkg])
                yield
                self.tt("dve", out_bf[:, :, r0:r0 + Dr], kgv[:, :, r0:r0 + Dr], t2[:, 0:H * Dr].rearrange("p (h d) -> p h d", d=Dr),
                        ALU.add, [rkg, rt2], rout)
                yield

            def headnorm(*a, **k):
                k.setdefault("tag", "0")
                for _ in headnorm_g(*a, **k):
                    pass

            SB_S = [0, 1, 2]
            OB_S = [3, 4]
            state = {"s": 0, "o": 0, "pt": 0}

            def v_lhsT(V, kt, hl):
                pr_, odd = hl // 2, hl % 2
                return V[:, kt, pr_, odd:odd + 2, :].rearrange("p a b -> p (a b)"), odd

            def v_store(V, tt, src_ps, npair, rsrc, rdst, d0=0, dstep=64):
                sv = src_ps.rearrange("p (a b) d -> p a b d", b=2)
                for odd in range(2):
                    self.cp("act", V[:, tt, :, 2 * odd, :], sv[:, :, odd, d0:d0 + 64], rsrc, rdst)

            def attention(jobs, n, scale, PT, recs, stepper=None):
                for job in jobs:
                    keys = job["keys"]
                    nk = len(keys)
                    ob = OB_S[state["o"] % 2]
                    state["o"] += 1
                    sl = []

                    def emit_s(i):
                        b = SB_S[state["s"] % 3]
                        state["s"] += 1
                        kl, vl, rkv = keys[i]
                        self.mm(psb(b, n), kl, job["q"], True, True, rkv + job["rq"], [PB[b]])
                        sl.append(b)
                    emit_s(0)
                    if nk > 1:
                        emit_s(1)
                    for i in range(nk):
                        pi = state["pt"] % 3
                        state["pt"] += 1
                        rp = R("PT", pi)
                        self.act(PT[pi][:, 0:n], psb(sl[i], n), AF.Exp, [PB[sl[i]]], [rp], scale=scale)
                        if i + 2 < nk:
                            emit_s(i + 2)
                        self.mm(psb(ob, n), keys[i][1], PT[pi][:, 0:n], i == 0, i == nk - 1, [rp] + keys[i][2], [PB[ob]])
                        if stepper is not None:
                            stepper()
                    finish_head(ob, n, job, recs)

            def finish_head(ob, n, job, recs):
                oh = job["ohalf"]
                po = slice(oh * 64, oh * 64 + 64)
                psum_ = slice((1 - oh) * 64, (1 - oh) * 64 + 64)
                rr = R("recs")
                self.act(recs[psum_, 0:n], ps[psum_, ob, 0:n], AF.Ln, [PB[ob]], [rr])
                self.act(recs[psum_, 0:n], recs[psum_, 0:n], AF.Exp, [rr], [rr], scale=-1.0)
                self.tt("dve", job["dst"], ps[po, ob, 0:n], recs[psum_, 0:n], ALU.mult, [PB[ob], rr], job["rdst"])

            def transposes_to(src_bf, nblk, width, rsrc, bank):
                pb = psb16(bank)
                for i in range(nblk):
                    self.tr(pb[0:width, i * 128:(i + 1) * 128], src_bf[:, i * width:(i + 1) * width], identb[:, :], rsrc + [rc], [PB[bank]])
                return pb

            def load_w(dst3, src2, rw, eng="pool"):
                self.dma(eng, [(dst3, src2.rearrange("(c p) n -> p c n", p=128))], (), [rw], stream="w")

            def final_proj(l, groups):
                with AR.scope():
                    wo = AR.alloc([128, 8, D], BF16)
                    rwo = R("wo")
                    load_w(wo[:, :, :], wout_d[l], rwo)
                    k = 0
                    for (t0, n, cnd) in groups:
                        for dc in range(8):
                            b = k % 8
                            k += 1
                            for mc in range(8):
                                self.mm(psb(b, n), wo[:, mc, dc * 128:(dc + 1) * 128], mixT[:, mc, t0:t0 + n], mc == 0, mc == 7,
                                        [rwo] + [mres(mc, bb) for bb in blks_of(t0, n)], [PB[b]])
                            self.stt("dve", xT[:, dc, t0:t0 + n], psb(b, n), self.MV(l, 2, cnd, dc), xT[:, dc, t0:t0 + n], ALU.mult, ALU.add,
                                     [PB[b]] + RMOD + xr(t0, n, [dc]), xr(t0, n, [dc]))
                    P.barrier()

            def ffn(l, groups):
                with AR.scope():
                    hT = AR.view_at(MIX_OFF, [128, 8, 1024], BF16)
                    gu = [AR.view_at(MIX_OFF + 16384 + i * 4096, [128, 2, 8, 128], BF16) for i in range(4)]
                    actT = AR.alloc([128, NJ, 1024], BF16)
                    wdr = [AR.alloc([128, NJ, 128], BF16) for i in range(3)]
                    sg = [AR.alloc([128, 1024], F32) for i in range(2)]
                    tmp = mod_tmp()
                    kg = 0
                    kd = 0
                    if l == 0:
                        bg_step(len(bgq))
                        if stage not in ("l0mix", "l0"):
                            convert_ffn(1)
                    else:
                        bg_step(len(bgq))
                    for (t0, n, cnd) in groups:
                        rh = R("f_hT")
                        for h0 in range(0, n, 512):
                            hn = min(512, n - h0)
                            modulate(t0 + h0, hn, l, 1, cnd, hT[:, :, h0:h0 + hn], rh, tmp)
                        nh = (n + 511) // 512
                        CUT = _os.environ.get("K_CUT", "")
                        if CUT == "m":
                            return
                        for j in range(NJ):
                            if CUT == "A1" and j == 1:
                                return
                            if CUT == "A3" and j == 3:
                                return
                            bg_step(1)
                            slot = kg % 4
                            rg = R("f_gu", slot)
                            self.dma("sp", [(gu[slot][:, 0, :, :], wgs[l][j]), (gu[slot][:, 1, :, :], wus[l][j])],
                                     [R("wgs", l, j), R("wus", l, j)], [rg], stream="gu")
                            set_ = (kg % 2) * 4
                            kg += 1
                            for gi in range(2):
                                for hh in range(nh):
                                    hn = min(512, n - hh * 512)
                                    b = set_ + gi * 2 + hh
                                    for c in range(8):
                                        self.mm(psb(b, hn), gu[slot][:, gi, c, :], hT[:, c, hh * 512:hh * 512 + hn], c == 0, c == 7, [rg, rh], [PB[b]])
                            si = j % 2
                            rs_ = R("f_sg", si)
                            for hh in range(nh):
                                hn = min(512, n - hh * 512)
                                self.act(sg[si][:, hh * 512:hh * 512 + hn], psb(set_ + hh, hn), AF.Silu, [PB[set_ + hh]], [rs_])
                                self.tt("dve", actT[:, j, hh * 512:hh * 512 + hn], psb(set_ + 2 + hh, hn), sg[si][:, hh * 512:hh * 512 + hn], ALU.mult,
                                        [PB[set_ + 2 + hh], rs_], [R("f_actT", j)])
                        if CUT == "A":
                            return
                        for dc in range(8):
                            if CUT == "B1" and dc == 1:
                                return
                            slot = kd % 3
                            rw = R("f_wd", slot)
                            self.dma("sp", [(wdr[slot][:, :, :], wds[l][dc])], [R("wds", l, dc)], [rw], stream="wd")
                            set_ = (kd % 4) * 2
                            kd += 1
                            for hh in range(nh):
                                hn = min(512, n - hh * 512)
                                b = set_ + hh
                                for j in range(NJ):
                                    self.mm(psb(b, hn), wdr[slot][:, j, :], actT[:, j, hh * 512:hh * 512 + hn], j == 0, j == NJ - 1,
                                            [rw, R("f_actT", j)], [PB[b]])
                                tt0 = t0 + hh * 512
                                self.stt("dve", xT[:, dc, tt0:tt0 + hn], psb(b, hn), self.MV(l, 5, cnd, dc), xT[:, dc, tt0:tt0 + hn], ALU.mult, ALU.add,
                                         [PB[b]] + RMOD + xr(tt0, hn, [dc]), xr(tt0, hn, [dc]))
                    P.barrier()

            LAT512 = [(g * 512, 512, 0) for g in range(4)]
            CTXG = (S, CTX, 1)

            def tile_of(t):
                return t // 128

            def tiles_of(groups):
                return [(gi, t0, n, cnd, ti) for gi, (t0, n, cnd) in enumerate(groups) for ti in range(n // 128)]

            def run_pass1(groups, l, hTs, mtmp, projA, postB):
                tl = tiles_of(groups)
                first_of = {}
                for k_, t_ in enumerate(tl):
                    first_of.setdefault(t_[0], k_)
                modgen = {}

                def start_mod(gi):
                    if gi < len(groups) and gi not in modgen:
                        t0, n, cnd = groups[gi]
                        bg_step(4)
                        modgen[gi] = modulate_g(t0, n, l, 0, cnd, hTs[gi % len(hTs)], R("hT", gi % len(hTs)), mtmp)

                def finish_mod(gi):
                    start_mod(gi)
                    for _ in modgen[gi]:
                        pass

                def A(k):
                    gi, t0, n, cnd, ti = tl[k]
                    if ti == 0:
                        finish_mod(gi)
                    projA(tl[k], hTs[gi % len(hTs)], R("hT", gi % len(hTs)), k % 2)
                active = []
                flags = {}

                def adv(g_):
                    try:
                        v = next(g_)
                        if v == "psum_done":
                            flags[id(g_)] = True
                    except StopIteration:
                        flags[id(g_)] = True
                        if g_ in active:
                            active.remove(g_)

                def adv_mod():
                    for mg in modgen.values():
                        try:
                            next(mg)
                        except StopIteration:
                            pass

                def step_all(until_len):
                    while len(active) > until_len:
                        for g_ in list(active):
                            adv(g_)
                        adv_mod()
                A(0)
                if len(tl) > 1:
                    A(1)
                for k in range(len(tl)):
                    gi, t0, n, cnd, ti = tl[k]
                    if ti == 0:
                        start_mod(gi + 1)
                    gk = postB(tl[k], k % 2)
                    active.append(gk)
                    step_all(1)
                    while not flags.get(id(gk), False):
                        adv(gk)
                        adv_mod()
                    if k + 2 < len(tl):
                        A(k + 2)
                step_all(0)

            def run_pass2(groups, l, hT, mtmp, tile_g, make_jobs, attend, rate=1):
                rh = R("hT", 0)

                def qproc_g(gi):
                    t0, n, cnd = groups[gi]
                    bg_step(4)
                    yield from modulate_g(t0, n, l, 0, cnd, hT, rh, mtmp, on_dve=True)
                    for ti in range(n // 128):
                        yield from tile_g((gi, t0, n, cnd, ti), hT, rh)
                for _ in qproc_g(0):
                    pass
                for gi in range(len(groups)):
                    gen = qproc_g(gi + 1) if gi + 1 < len(groups) else None

                    def stepper(gen=gen):
                        if gen is None:
                            return
                        for _ in range(rate):
                            try:
                                next(gen)
                            except StopIteration:
                                return
                    attend(make_jobs(gi), groups[gi], stepper)
                    if gen is not None:
                        for _ in gen:
                            pass

            def with_hooks(njobs, hooks):
                pos = [(h + 1) * njobs // (len(hooks) + 1) for h in range(len(hooks))]
                st = {"h": 0}

                def before(j):
                    while st["h"] < len(hooks) and pos[st["h"]] <= j:
                        hooks[st["h"]]()
                        st["h"] += 1

                def flush():
                    while st["h"] < len(hooks):
                        hooks[st["h"]]()
                        st["h"] += 1
                return before, flush

            def attention_h(jobs, n, scale, PT, recs, hooks):
                before, flush = with_hooks(len(jobs), hooks)
                for j, job in enumerate(jobs):
                    before(j)
                    attention([job], n, scale, PT, recs)
                flush()

            def layer0():
                with AR.scope():
                    T = AR.alloc
                    gains = T([128, 128], F32)
                    cos0 = T([128, 16, 64], F32)
                    sin0 = T([128, 16, 64], F32)
                    kT = T([128, 2, NT], BF16)
                    Vp = T([128, 18, 2, 3, 64], BF16)
                    hT = T([128, 8, 512], BF16)
                    mtmp = mod_tmp()
                    hn_tmp = hn_alloc(768)
                    qkbf = T([128, 768], BF16)
                    rl0 = R("l0c")
                    rg0 = R("gains0")
                    self.dma("sp", [(cos0[:, :, :], cf_d[:, CF_COS0:CF_COS0 + 1024].rearrange("p (t d) -> p t d", d=64)),
                                    (sin0[:, :, :], cf_d[:, CF_SIN0:CF_SIN0 + 1024].rearrange("p (t d) -> p t d", d=64))], (), [rl0], stream="c")
                    self.dma("sp", [(gains[:, :], gains_d[:, 0:128].partition_broadcast(128))], (), [rg0], stream="c")
                    self.memset("pool", Vp[:, :, :, 1, :], 1.0, [R("Vp1")])
                    groups = LAT512 + [CTXG]
                    with AR.scope():
                        band = T([128, 20, 128], BF16)
                        wblk = T([128, 2, 128], BF16)
                        dsb = T([128, 2, 128], BF16)
                        hT2 = T([128, 8, 512], BF16)
                        hnP = [hn_alloc(256) for i in range(2)]
                        qkP = [T([128, 256], BF16) for i in range(2)]
                        a_sb = AR.view_at(MIX_OFF + 2 * NT * 2, [128, 18, 256], BF16)
                        wkv = AR.view_at(MIX_OFF + 4 * NT * 2, [128, 8, 768], BF16)
                        rband = R("band")
                        self.dma("sp", [(band[:, :, :], cb_d[:, CB_BAND:CB_BAND + 2560].rearrange("p (t d) -> p t d", d=128))], (), [rband], stream="c")
                        self.memset("pool", wblk[:, :, :], 0.0, [R("wblk")])
                        self.dma("pool", [(wblk[(g % 2) * 64:(g % 2 + 1) * 64, g // 2, (g % 2) * 64:(g % 2 + 1) * 64], poolw_d[g]) for g in range(4)],
                                 (), [R("wblk")], stream="w")
                        rwkv = R("wkv")
                        load_w(wkv[:, :, :], w_in_kv0, rwkv)

                        def projA(tile, hTg, rh, par):
                            gi, t0, n, cnd, ti = tile
                            for nb, (c0, c1) in enumerate(((0, 512), (512, 768))):
                                b = 2 * par + nb
                                for c in range(8):
                                    self.mm(ps[:, b, 0:c1 - c0], hTg[:, c, ti * 128:(ti + 1) * 128], wkv[:, c, c0:c1], c == 0, c == 7, [rh, rwkv], [PB[b]])

                        def postB(tile, par):
                            gi, t0, n, cnd, ti = tile
                            tt = tile_of(t0) + ti
                            b0, b1 = 2 * par, 2 * par + 1
                            self.cp("act", a_sb[:, tt, :], ps[:, b0, 0:256], [PB[b0]], [R("a_sb", tt)])
                            yield
                            v_store(Vp, tt, ps[:, b1, 0:256].rearrange("p (h d) -> p h d", d=64), 2, [PB[b1]], [R("Vp", tt)])
                            yield
                            rope = None if cnd == 1 else (cos0[:, tt, :], sin0[:, tt, :], 0, 64, rl0)
                            kout = qkP[par][:, 0:256].rearrange("p (h d) -> p h d", d=64)
                            yield from headnorm_g([(ps[:, b0, 256:512].rearrange("p (h d) -> p h d", d=64), 0, 4, [PB[b0]])], 4, 64,
                                                  gains[:, 64:128], kout, hnP[par], [R("qkbf", par)], rg0, rope=rope, tag=str(par))
                            pb = transposes_to(qkP[par][:, 0:256], 2, 128, [R("qkbf", par)], 4 + par)
                            yield
                            self.cp("act", kT[:, :, tt * 128:(tt + 1) * 128], pb[:, 0:256].rearrange("p (m n) -> p m n", n=128), [PB[4 + par]], [R("kT", tt)])
                            yield
                        run_pass1(groups, 0, [hT, hT2], mtmp, projA, postB)
                        for tt in range(18):
                            first = tt in (0, 16)
                            last = tt in (15, 17)
                            bA, bB = (5, 6) if tt % 2 == 0 else (0, 1)
                            for g in range(4):
                                terms = [(tt, 3 if first else (4 if last else 0))]
                                if not first:
                                    terms.append((tt - 1, 1))
                                if not last:
                                    terms.append((tt + 1, 2))
                                for k, (ts_, var) in enumerate(terms):
                                    self.mm(ps[(g % 2) * 64:(g % 2 + 1) * 64, bA, (g // 2) * 128:(g // 2 + 1) * 128],
                                            a_sb[:, ts_, g * 64:(g + 1) * 64], band[:, g * 5 + var, :], k == 0, k == len(terms) - 1,
                                            [R("a_sb", ts_), rband], [PB[bA]])
                            self.cp("act", dsb[:, :, :], ps[:, bA, 0:256].rearrange("p (a n) -> p a n", n=128), [PB[bA]], [R("dsb")])
                            for pr in range(2):
                                self.mm(ps[:, bB, pr * 128:(pr + 1) * 128], wblk[:, pr, :], dsb[:, pr, :], True, True, [R("dsb"), R("wblk")], [PB[bB]])
                            for pr in range(2):
                                self.ts("dve", mixT[:, pr, tt * 128:(tt + 1) * 128], ps[:, bB, pr * 128:(pr + 1) * 128], vecs[:, V_PSC + pr:V_PSC + pr + 1],
                                        None, ALU.mult, None, [PB[bB], rc], [mres(pr, min(tt // 4, 4))])
                        P.barrier()
                    with AR.scope():
                        wq = T([128, 8, 768], BF16)
                        qT = [T([128, 6, 2, 512], BF16) for i in range(2)]
                        PT = [T([128, 512], BF16) for i in range(3)]
                        recs = T([128, 512], F32)
                        rwq = R("wq")
                        load_w(wq[:, :, :], w_in_q0, rwq)
                        for i in range(2):
                            self.memset("pool", qT[i][:, :, :, :], 0.0, [R("qT", i)])

                        def tile_g(tile, hTg, rh):
                            gi, t0, n, cnd, ti = tile
                            tt = tile_of(t0) + ti
                            for nb, (c0, c1) in enumerate(((0, 512), (512, 768))):
                                for c in range(8):
                                    self.mm(ps[:, 5 + nb, 0:c1 - c0], hTg[:, c, ti * 128:(ti + 1) * 128], wq[:, c, c0:c1], c == 0, c == 7, [rh, rwq], [PB[5 + nb]])
                                yield
                            rope = None if cnd == 1 else (cos0[:, tt, :], sin0[:, tt, :], 0, 64, rl0)
                            qout = qkbf[:, 0:768].rearrange("p (h d) -> p h d", d=64)
                            yield from headnorm_g([(ps[:, 5, 0:512].rearrange("p (h d) -> p h d", d=64), 0, 8, [PB[5]]),
                                                   (ps[:, 6, 0:256].rearrange("p (h d) -> p h d", d=64), 8, 12, [PB[6]])], 12, 64,
                                                  gains[:, 0:64], qout, hn_tmp, [R("qkbf")], rg0, rope=rope, tag="0")
                            qTg, rq = qT[gi % 2], R("qT", gi % 2)
                            pb = transposes_to(qkbf[:, 0:768], 6, 128, [R("qkbf")], 7)
                            yield
                            for hf in range(2):
                                hp_ = slice(hf * 64, hf * 64 + 64)
                                self.cp("dve", qTg[hp_, :, hf, ti * 128:(ti + 1) * 128], pb[hp_, 0:768].rearrange("p (m n) -> p m n", n=128), [PB[7]], [rq])
                                yield

                        def make_jobs(gi):
                            t0, n, cnd = groups[gi]
                            qTg, rq = qT[gi % 2], R("qT", gi % 2)
                            ktiles = [16, 17] + (list(range(16)) if cnd == 0 else [])
                            jobs = []
                            for hs in range(12):
                                g = Q_PERM[hs] // 3
                                half = hs % 2
                                assert g % 2 == half
                                pr = slice(half * 64, half * 64 + 64)
                                keys = []
                                for kt in ktiles:
                                    vl, oh = v_lhsT(Vp, kt, g)
                                    keys.append((kT[:, g // 2, kt * 128:(kt + 1) * 128], vl, [R("kT", kt), R("Vp", kt), R("Vp1")]))
                                jobs.append(dict(q=qTg[:, hs // 2, half, 0:n], keys=keys, rq=[rq], ohalf=oh,
                                                 dst=mixT[pr, 2 + hs // 2, t0:t0 + n], rdst=[mres(2 + hs // 2, min(t0 // 512, 4))]))
                            return jobs

                        def attend(jobs, grp, stepper):
                            attention(jobs, grp[1], 0.125, PT, recs, stepper)
                        run_pass2(groups, 0, hT, mtmp, tile_g, make_jobs, attend, rate=1)
                        P.barrier()
                final_proj(0, LAT512 + [CTXG])

            def layer1():
                with AR.scope():
                    T = AR.alloc
                    GO = 128
                    gains = T([128, G_N - GO], F32)
                    hT = T([128, 8, 512], BF16)
                    mtmp = mod_tmp()
                    hn_tmp = hn_alloc(384)
                    qkbf = T([128, 384], BF16)
                    hnP = [hn_tmp, hn_alloc(384)]
                    qkP = [qkbf, T([128, 384], BF16)]
                    PT = [T([128, 512], BF16) for i in range(3)]
                    recs = T([128, 512], F32)
                    rg1 = R("gains1")
                    self.dma("sp", [(gains[:, :], gains_d[:, GO:G_N].partition_broadcast(128))], (), [rg1], stream="c")
                    groups_all = LAT512 + [CTXG]

                    with AR.scope():
                        j2 = T([128, 128], BF16)
                        msk = T([128, NMASK, 128], BF16)
                        rna = R("nac")
                        self.dma("sp", [(j2[:, :], cb_d[:, CB_J2:CB_J2 + 128]),
                                        (msk[:, :, :], cb_d[:, CB_MASK:CB_MASK + NMASK * 128].rearrange("p (t d) -> p t d", d=128))], (), [rna], stream="c")
                        XH = T([128, 7, 4, 128], BF16)
                        kTn = T([128, 2, NT], BF16)
                        Vn = T([128, 18, 2, 3, 64], BF16)
                        self.memset("pool", Vn[:, :, :, 1, :], 1.0, [R("Vn1")])
                        for hh in range(2):
                            rxh = R("XH")
                            for off in range(-3, 4):
                                prs = []
                                for a in range(2):
                                    for b in range(2):
                                        dr = 2 * off + a - b
                                        base = (hh * 4 * 15 + dr + 7) * 127
                                        src = bass.AP(rpb_d.tensor, base, [[1, 64], [15 * 127, 4], [1, 64]])
                                        prs.append((XH[a * 64:(a + 1) * 64, off + 3, :, b * 64:(b + 1) * 64], src))
                                self.dma("pool", prs, (), [rxh], stream="w")
                            with AR.scope():
                                wkv = T([128, 8, 512], BF16)
                                hT2 = T([128, 8, 512], BF16)
                                rw = R("wnkv")
                                self.dma("pool", [(wkv[:, :, 0:256], w_in1[:, 1184 + hh * 256:1184 + (hh + 1) * 256].rearrange("(c p) n -> p c n", p=128)),
                                                  (wkv[:, :, 256:512], w_in1[:, 1696 + hh * 256:1696 + (hh + 1) * 256].rearrange("(c p) n -> p c n", p=128))],
                                         (), [rw], stream="w")

                                def projA(tile, hTg, rh, par):
                                    gi, t0, n, cnd, ti = tile
                                    for c in range(8):
                                        self.mm(ps[:, par, :], hTg[:, c, ti * 128:(ti + 1) * 128], wkv[:, c, :], c == 0, c == 7, [rh, rw], [PB[par]])

                                def postB(tile, par):
                                    gi, t0, n, cnd, ti = tile
                                    tt = tile_of(t0) + ti
                                    v_store(Vn, tt, ps[:, par, 256:512].rearrange("p (h d) -> p h d", d=64), 2, [PB[par]], [R("Vn", tt)])
                                    yield
                                    kout = qkP[par][:, 0:256].rearrange("p (h d) -> p h d", d=64)
                                    yield from headnorm_g([(ps[:, par, 0:256].rearrange("p (h d) -> p h d", d=64), 0, 4, [PB[par]])], 4, 64,
                                                          gains[:, G_NK - GO:G_NK - GO + 64], kout, hnP[par], [R("qkbf", par)], rg1, tag=str(par))
                                    pb = transposes_to(qkP[par][:, 0:256], 2, 128, [R("qkbf", par)], 4 + par)
                                    yield
                                    self.cp("act", kTn[:, :, tt * 128:(tt + 1) * 128], pb[:, 0:256].rearrange("p (m n) -> p m n", n=128), [PB[4 + par]], [R("kTn", tt)])
                                    yield
                                run_pass1(groups_all, 1, [hT, hT2], mtmp, projA, postB)
                                P.barrier()
                            with AR.scope():
                                wq = T([128, 8, 256], BF16)
                                qT = [T([128, 2, 2, 512], BF16) for i in range(2)]
                                rw = R("wnq")
                                load_w(wq[:, :, :], w_in1[:, 672 + hh * 256:672 + (hh + 1) * 256], rw)
                                for i in range(2):
                                    self.memset("pool", qT[i][:, :, :, :], 0.0, [R("qTn", i)])

                                def tile_g(tile, hTg, rh):
                                    gi, t0, n, cnd, ti = tile
                                    for c in range(8):
                                        self.mm(ps[:, 5, 0:256], hTg[:, c, ti * 128:(ti + 1) * 128], wq[:, c, :], c == 0, c == 7, [rh, rw], [PB[5]])
                                    yield
                                    qout = qkbf[:, 0:256].rearrange("p (h d) -> p h d", d=64)
                                    yield from headnorm_g([(ps[:, 5, 0:256].rearrange("p (h d) -> p h d", d=64), 0, 4, [PB[5]])], 4, 64,
                                                          gains[:, G_NQ - GO:G_NQ - GO + 64], qout, hn_tmp, [R("qkbf")], rg1, tag="0")
                                    qTg, rq = qT[gi % 2], R("qTn", gi % 2)
                                    pb = transposes_to(qkbf[:, 0:256], 2, 128, [R("qkbf")], 7)
                                    yield
                                    for hf in range(2):
                                        hp_ = slice(hf * 64, hf * 64 + 64)
                                        self.cp("dve", qTg[hp_, :, hf, ti * 128:(ti + 1) * 128], pb[hp_, 0:256].rearrange("p (m n) -> p m n", n=128), [PB[7]], [rq])
                                        yield

                                def make_jobs(gi):
                                    return gi

                                def attend(gi, grp, stepper):
                                    t0, n, cnd = grp
                                    qTg, rq = qT[gi % 2], R("qTn", gi % 2)
                                    for hl in range(4):
                                        h = hh * 4 + hl
                                        half = hl % 2
                                        m = hl // 2
                                        ob = OB_S[state["o"] % 2]
                                        state["o"] += 1
                                        oh = hl % 2
                                        for qt in range(4):
                                            i = gi * 4 + qt
                                            kts = [(16, None), (17, None)] + NA_PLAN[i]
                                            qa = qTg[:, m, half, qt * 128:(qt + 1) * 128]
                                            oc = ps[:, ob, qt * 128:(qt + 1) * 128]
                                            nkt = len(kts)
                                            chunks = [kts[c0:c0 + 4] for c0 in range(0, nkt, 4)]
                                            banks = []
                                            for chunk in chunks:
                                                b = SB_S[state["s"] % 3]
                                                state["s"] += 1
                                                banks.append(b)
                                                for bi, (j, mi) in enumerate(chunk):
                                                    sc = ps[:, b, bi * 128:(bi + 1) * 128]
                                                    self.mm(sc, kTn[:, m, j * 128:(j + 1) * 128], qa, True, mi is None, [R("kTn", j), rq], [PB[b]])
                                                    if mi is not None:
                                                        self.mm(sc, j2[:, :], XH[:, j - i + 3, hl, :], False, False, [rna, rxh], [PB[b]])
                                                        self.mm(sc, identb[:, :], msk[:, mi, :], False, True, [rna, rc], [PB[b]])
                                            done = 0
                                            for chunk, b in zip(chunks, banks):
                                                w_ = len(chunk) * 128
                                                pi = state["pt"] % 3
                                                state["pt"] += 1
                                                rp = R("PT", pi)
                                                self.act(PT[pi][:, 0:w_], psb(b, w_), AF.Exp, [PB[b]], [rp], scale=0.125)
                                                for bi, (j, mi) in enumerate(chunk):
                                                    vl, _ = v_lhsT(Vn, j, hl)
                                                    self.mm(oc, vl, PT[pi][:, bi * 128:(bi + 1) * 128], done == 0, done == nkt - 1,
                                                            [rp, R("Vn", j), R("Vn1")], [PB[ob]])
                                                    done += 1
                                                stepper()
                                        hp = slice((h % 2) * 64, (h % 2) * 64 + 64)
                                        finish_head(ob, 512, dict(ohalf=oh, dst=mixT[hp, 4 + h // 2, t0:t0 + 512], rdst=[mres(4 + h // 2, gi)]), recs)
                                run_pass2(LAT512, 1, hT, mtmp, tile_g, make_jobs, attend, rate=3)
                                P.barrier()

                    with AR.scope():
                        cos1 = T([128, 16, 32], F32)
                        sin1 = T([128, 16, 32], F32)
                        kTm = T([128, 4, NT], BF16)
                        Vm = T([128, 18, 2, 3, 64], BF16)
                        kfullP = [T([128, 4, 96], F32) for i in range(2)]
                        latbP = [T([128, 384], BF16) for i in range(2)]
                        latTP = [T([128, 3, 128], BF16) for i in range(2)]
                        rl1 = R("l1c")
                        self.dma("sp", [(cos1[:, :, :], cf_d[:, CF_COS1:CF_COS1 + 512].rearrange("p (t d) -> p t d", d=32)),
                                        (sin1[:, :, :], cf_d[:, CF_SIN1:CF_SIN1 + 512].rearrange("p (t d) -> p t d", d=32))], (), [rl1], stream="c")
                        self.memset("pool", Vm[:, :, :, 1, :], 1.0, [R("Vm1")])
                        ss1P = [T([128, 1], F32) for i in range(2)]
                        sd1P = [T([128, 1], F32) for i in range(2)]

                        def lora_norm_g(src_ps, width, gain_ap, rsrc, tbank, par):
                            sp_ = str(par)
                            sqv = hnP[par][0][:, 0:width]
                            ss1, sd1, lat_bf, latT = ss1P[par], sd1P[par], latbP[par], latTP[par]
                            self.act(sqv, src_ps, AF.Square, rsrc, [R("hn_sq" + sp_)])
                            yield
                            self.red(ss1[:, :], sqv, [R("hn_sq" + sp_)], [R("ss1", par)])
                            yield
                            self.act(sd1[:, :], ss1[:, :], AF.Ln, [R("ss1", par)], [R("sd1", par)], bias=EPS, scale=1.0 / width)
                            self.act(sd1[:, :], sd1[:, :], AF.Exp, [R("sd1", par)], [R("sd1", par)], scale=-0.5)
                            yield
                            self.stt("dve", lat_bf[:, 0:width], src_ps, sd1[:, 0:1], gain_ap, ALU.mult, ALU.mult, rsrc + [R("sd1", par), rg1], [R("lat_bf", par)])
                            yield "psum_done"
                            nb = width // 128
                            pb = transposes_to(lat_bf[:, 0:width], nb, 128, [R("lat_bf", par)], tbank)
                            yield
                            self.cp("act", latT[:, 0:nb, :], pb[:, 0:width].rearrange("p (m n) -> p m n", n=128), [PB[tbank]], [R("latT", par)])
                            yield

                        def lora_norm(src_ps, width, gain_ap, rsrc, tbank):
                            for _ in lora_norm_g(src_ps, width, gain_ap, rsrc, tbank, 0):
                                pass
                        latT = latTP[0]

                        for hh in range(2):
                            with AR.scope():
                                wkv = T([128, 8, 288], BF16)
                                hT2 = T([128, 8, 512], BF16)
                                wukv = T([128, 2, 512], BF16)
                                rw, rwu = R("wmkv"), R("wukv")
                                load_w(wkv[:, :, :], w_in1[:, 384:672], rw)
                                load_w(wukv[:, :, :], wukv_d[:, hh * 512:(hh + 1) * 512], rwu)

                                def projA(tile, hTg, rh, par):
                                    gi, t0, n, cnd, ti = tile
                                    for c in range(8):
                                        self.mm(ps[:, par, 0:288], hTg[:, c, ti * 128:(ti + 1) * 128], wkv[:, c, :], c == 0, c == 7, [rh, rw], [PB[par]])

                                def postB(tile, par):
                                    gi, t0, n, cnd, ti = tile
                                    tt = tile_of(t0) + ti
                                    kfull = kfullP[par]
                                    rkf = R("kfull", par)
                                    self.cp("act", kfull[:, :, 64:96], ps[:, par, 256:288].unsqueeze(1).to_broadcast([128, 4, 32]), [PB[par]], [rkf])
                                    yield
                                    yield from lora_norm_g(ps[:, par, 0:256], 256, gains[:, G_KVA - GO:G_KVA - GO + 256], [PB[par]], 2 + par, par)
                                    kb = 4 + par
                                    for c in range(2):
                                        self.mm(ps[:, kb, :], latTP[par][:, c, :], wukv[:, c, :], c == 0, c == 1, [R("latT", par), rwu], [PB[kb]])
                                    yield
                                    kvv = ps[:, kb, :].rearrange("p (h d) -> p h d", d=128)
                                    self.cp("act", kfull[:, :, 0:64], kvv[:, :, 0:64], [PB[kb]], [rkf])
                                    yield
                                    v_store(Vm, tt, kvv, 2, [PB[kb]], [R("Vm", tt)], d0=64)
                                    yield
                                    rope = None if cnd == 1 else (cos1[:, tt, :], sin1[:, tt, :], 64, 32, rl1)
                                    kout = qkP[par][:, 0:384].rearrange("p (h d) -> p h d", d=96)
                                    yield from headnorm_g([(kfull[:, :, :], 0, 4, [rkf])], 4, 96, gains[:, G_MK - GO:G_MK - GO + 96], kout, hnP[par],
                                                          [R("qkbf", par)], rg1, rope=rope, tag=str(par))
                                    pb = transposes_to(qkP[par][:, 0:384], 4, 96, [R("qkbf", par)], 6)
                                    yield
                                    self.cp("act", kTm[0:96, :, tt * 128:(tt + 1) * 128], pb[0:96, 0:512].rearrange("p (m n) -> p m n", n=128), [PB[6]], [R("kTm", tt)])
                                    yield
                                run_pass1(groups_all, 1, [hT, hT2], mtmp, projA, postB)
                                P.barrier()
                            with AR.scope():
                                wcq = T([128, 8, 384], BF16)
                                wuq = T([128, 3, 384], BF16)
                                qT = [T([128, 4, 512], BF16) for i in range(2)]
                                rw, rwu = R("wcq"), R("wuq")
                                load_w(wcq[:, :, :], w_in1[:, 0:384], rw)
                                load_w(wuq[:, :, :], wuq_d[:, hh * 384:(hh + 1) * 384], rwu)

                                def tile_g(tile, hTg, rh):
                                    gi, t0, n, cnd, ti = tile
                                    tt = tile_of(t0) + ti
                                    for c in range(8):
                                        self.mm(ps[:, 5, 0:384], hTg[:, c, ti * 128:(ti + 1) * 128], wcq[:, c, :], c == 0, c == 7, [rh, rw], [PB[5]])
                                    yield
                                    yield from lora_norm_g(ps[:, 5, 0:384], 384, gains[:, G_QA - GO:G_QA - GO + 384], [PB[5]], 7, 0)
                                    for c in range(3):
                                        self.mm(ps[:, 6, 0:384], latT[:, c, :], wuq[:, c, :], c == 0, c == 2, [R("latT", 0), rwu], [PB[6]])
                                    yield
                                    qout = qkbf[:, 0:384].rearrange("p (h d) -> p h d", d=96)
                                    yield from headnorm_g([(ps[:, 6, 0:384].rearrange("p (h d) -> p h d", d=96), 0, 4, [PB[6]])], 4, 96,
                                                          gains[:, G_MQ - GO:G_MQ - GO + 96], qout, hn_tmp, [R("qkbf")], rg1,
                                                          rope=(cos1[:, tt, :], sin1[:, tt, :], 64, 32, rl1), tag="0")
                                    qTg, rq = qT[gi % 2], R("qTm", gi % 2)
                                    pb = transposes_to(qkbf[:, 0:384], 4, 96, [R("qkbf")], 7)
                                    yield
                                    self.cp("dve", qTg[0:96, :, ti * 128:(ti + 1) * 128], pb[0:96, 0:512].rearrange("p (m n) -> p m n", n=128), [PB[7]], [rq])
                                    yield

                                def make_jobs(gi):
                                    t0, n, cnd = LAT512[gi]
                                    qTg, rq = qT[gi % 2], R("qTm", gi % 2)
                                    ktiles = [16, 17] + list(range(16))
                                    jobs = []
                                    for hl in range(4):
                                        h = hh * 4 + hl
                                        hp = slice((h % 2) * 64, (h % 2) * 64 + 64)
                                        keys = []
                                        for kt in ktiles:
                                            vl, oh = v_lhsT(Vm, kt, hl)
                                            keys.append((kTm[0:96, hl, kt * 128:(kt + 1) * 128], vl, [R("kTm", kt), R("Vm", kt), R("Vm1")]))
                                        jobs.append(dict(q=qTg[0:96, hl, 0:512], keys=keys, rq=[rq], ohalf=oh,
                                                         dst=mixT[hp, h // 2, t0:t0 + 512], rdst=[mres(h // 2, gi)]))
                                    return jobs

                                def attend(jobs, grp, stepper):
                                    attention(jobs, 512, 96.0 ** -0.5, PT, recs, stepper)
                                run_pass2(LAT512, 1, hT, mtmp, tile_g, make_jobs, attend, rate=2)
                                P.barrier()
                final_proj(1, LAT512)


            if stage != "load":
                layer0()
                if stage != "l0mix":
                    ffn(0, [(0, 1024, 0), (1024, 1024, 0), CTXG])
                    if stage != "l0":
                        layer1()
                        if stage != "l1mix":
                            ffn(1, [(0, 1024, 0), (1024, 1024, 0)])

            with AR.scope() as s1:
                yst = [s1.enter_context(AR.alloc([128, D], F32)) for i in range(2)]
                for tt in range(nout // 128):
                    slot = tt % 2
                    b0 = 2 * (tt % 2)
                    ry = R("yst", slot)
                    for c in range(8):
                        self.tr(ps[:, b0 + c // 4, (c % 4) * 128:(c % 4 + 1) * 128], xT[:, c, tt * 128:(tt + 1) * 128], identf[:, :],
                                xr(tt * 128, 128, [c]) + [rc], [PB[b0 + c // 4]])
                    for hh in range(2):
                        self.cp("act" if hh == 0 else "dve", yst[slot][:, hh * 512:(hh + 1) * 512], ps[:, b0 + hh, :], [PB[b0 + hh]], [ry])
                    self.dma("sp", [(y_d[tt * 128:(tt + 1) * 128, :], yst[slot][:, :])], [ry], [R("ydram", tt)], stream="out")
                self.nwait = P.emit()
                self.nops = len(P.ops)
        return nc


_CONSTS = None


def _prep_shared(inputs):
    global _CONSTS
    if _CONSTS is None:
        _CONSTS = _const_arrays()
    cb, cf = _CONSTS
    f = lambda a: np.ascontiguousarray(np.asarray(a, dtype=np.float32))
    fm = lambda v: f(v).reshape(-1, 128).T
    vecs = np.zeros((128, V_N), np.float32)
    vecs[:, 0:48] = fm(inputs["l0_ada_b"])
    vecs[:, 48:96] = fm(inputs["l1_ada_b"])
    vecs[:, V_NMIX0:V_NMIX0 + 8] = fm(inputs["l0_norm_mix"])
    vecs[:, V_NFFN0:V_NFFN0 + 8] = fm(inputs["l0_norm_ffn"])
    vecs[:, V_NMIX1:V_NMIX1 + 8] = fm(inputs["l1_norm_mix"])
    vecs[:, V_NFFN1:V_NFFN1 + 8] = fm(inputs["l1_norm_ffn"])
    vecs[:, V_PSC:V_PSC + 2] = fm(inputs["l0_pool_scale"])
    gains = np.concatenate([f(inputs[k]).reshape(-1) for k in
                            ("l0_q_gain", "l0_k_gain", "l1_mla_q_a_gain", "l1_mla_kv_a_gain", "l1_mla_q_gain", "l1_mla_k_gain",
                             "l1_na_q_gain", "l1_na_k_gain")]).reshape(1, G_N)
    w_in0 = f(inputs["l0_w_in"])
    qcols = np.concatenate([np.arange(256 + h * 64, 256 + (h + 1) * 64) for h in Q_PERM])
    w_in_q = np.ascontiguousarray(w_in0[:, qcols])
    w_in_kv = np.ascontiguousarray(np.concatenate([w_in0[:, 0:256], w_in0[:, 1024:1536]], axis=1))
    w_out0 = f(inputs["l0_w_out"])
    orow = np.concatenate([np.arange(256)] + [np.arange(256 + h * 64, 256 + (h + 1) * 64) for h in Q_PERM])
    w_out_p = np.ascontiguousarray(w_out0[orow, :])
    rpb = f(inputs["l1_na_rpb"])
    rp = np.zeros((8, 15, 127), np.float32)
    rp[:, :, 48:79] = rpb[:, :, ::-1]
    rpbR = np.concatenate([rp.reshape(-1), np.zeros(128, np.float32)])
    shared = {
        "vecs": vecs, "gains": gains, "cbf": cb, "cf32": cf, "rpbR": rpbR,
        "l0_ada_w": f(inputs["l0_ada_w"]), "l1_ada_w": f(inputs["l1_ada_w"]),
        "l0_w_in_kv": w_in_kv, "l0_w_in_q": w_in_q, "l0_pool_w": f(inputs["l0_pool_w"]),
        "l0_w_out_p": w_out_p, "l1_w_out": f(inputs["l1_w_out"]), "l1_w_in": f(inputs["l1_w_in"]),
        "l1_mla_w_uq": f(inputs["l1_mla_w_uq"]), "l1_mla_w_ukv": f(inputs["l1_mla_w_ukv"]),
    }
    for l in range(2):
        for nm in ("gate", "up", "down"):
            shared[f"l{l}_ffn_w_{nm}"] = f(inputs[f"l{l}_ffn_w_{nm}"])
    return shared


def _run(inputs, stage="full", cores=8, trace=False):
    shared = _prep_shared(inputs)
    x = np.asarray(inputs["x"], dtype=np.float32)
    c = np.asarray(inputs["c"], dtype=np.float32)
    ctx = np.asarray(inputs["ctx"], dtype=np.float32)
    c_ctx = np.asarray(inputs["c_ctx"], dtype=np.float32)
    in_maps = []
    for b in range(cores):
        cond = np.concatenate([c[b].reshape(8, 128).T, c_ctx.reshape(8, 128).T], axis=1)
        m = dict(shared)
        m["x"] = np.ascontiguousarray(x[b])
        m["ctx"] = np.ascontiguousarray(ctx[b])
        m["condT"] = np.ascontiguousarray(cond)
        in_maps.append(m)
    bld = Builder(stage)
    nc = bld.build()
    used = set()
    res = run_bass_kernel_spmd(nc, in_maps, core_ids=list(range(cores)), **({"trace": True} if trace else {}))
    return res, bld


def kernel(**inputs):
    res, _ = _run(inputs, STAGE, 8)
    return np.stack([np.asarray(r["y"], dtype=np.float32) for r in res.results], axis=0)
```

```python
import numpy as np
import ml_dtypes
import concourse.bass as bass
import concourse.mybir as mybir
from concourse.bass_utils import run_bass_kernel_spmd

F32 = mybir.dt.float32
BF16 = mybir.dt.bfloat16
ALU = mybir.AluOpType
AF = mybir.ActivationFunctionType
AX = mybir.AxisListType

D = 1024
S = 2048
CTX = 256
NT = S + CTX
FFN = 2816
NJ = FFN // 128
EPS = 1e-6
STAGE = "full"
Q_PERM = [0, 3, 1, 4, 2, 5, 6, 9, 7, 10, 8, 11]


class Res:
    __slots__ = ("name", "w", "r", "excl")

    def __init__(self, name):
        self.name = name
        self.w = None
        self.r = {}
        self.excl = False


class Op:
    __slots__ = ("eng", "fn", "deps", "idx", "signal", "semval", "dma", "dsem", "dval", "dprev", "selfsig", "bg")


class Prog:
    ENG = ("pe", "act", "dve", "pool", "sp")

    def __init__(self, nc, esems, dsems, streams):
        self.nc = nc
        self.h = {"pe": nc.tensor, "act": nc.scalar, "dve": nc.vector, "pool": nc.gpsimd, "sp": nc.sync}
        self.esem = esems
        self.dsems = dsems
        self.streams = streams
        self.rr = {k: 0 for k in streams}
        self.dtot = [0] * len(dsems)
        self.dlast = [None] * len(dsems)
        self.ops = []
        self.last = {e: None for e in self.ENG}
        self.pending = {e: [] for e in self.ENG}
        self.resd = {}

    def res(self, *key):
        r = self.resd.get(key)
        if r is None:
            r = Res(key)
            self.resd[key] = r
        return r

    def op(self, eng, fn, reads=(), writes=(), dma=0, stream="misc", extra=(), selfsig=False, bg=False):
        o = Op()
        o.eng = eng
        o.fn = fn
        o.dma = dma
        o.signal = False
        o.semval = 0
        o.selfsig = selfsig
        o.bg = bg
        o.idx = len(self.ops)
        deps = {}

        def add(d, kind):
            if d is None:
                return
            if d.dma == 0 and d.eng == eng:
                if eng == "pe" or eng == "sp":
                    return
            key = ("d", d.idx) if d.dma else ("e", d.eng)
            cur = deps.get(key)
            if cur is None or d.idx > cur.idx:
                deps[key] = d

        for r in reads:
            add(r.w, "raw")
            if r.excl:
                for rd in r.r.values():
                    if rd.eng != eng:
                        add(rd, "rar")
        for w in writes:
            add(w.w, "waw")
            for rd in w.r.values():
                add(rd, "war")
        for d in extra:
            add(d, "raw")
        for d in self.pending[eng]:
            add(d, "raw")
        self.pending[eng] = []
        for r in reads:
            key = ("d", o.idx) if dma else ("e", eng)
            r.r[key] = o
        for w in writes:
            w.w = o
            w.r = {}
        o.deps = list(deps.values())
        for d in o.deps:
            d.signal = True
        if dma:
            lst = self.streams[stream]
            si = lst[self.rr[stream] % len(lst)]
            self.rr[stream] += 1
            o.dsem = si
            o.dprev = self.dtot[si]
            self.dtot[si] += 16 * dma
            o.dval = self.dtot[si]
            self.dlast[si] = o
        self.ops.append(o)
        self.last[eng] = o
        return o

    def barrier(self):
        nc = self.nc
        extra = [self.last[e] for e in ("pe", "act", "dve", "pool") if self.last[e] is not None]
        extra += [d for d in self.dlast if d is not None and not d.bg]
        sem = self.esem["sp"]
        b = self.op("sp", lambda: nc.sync.sem_inc(sem, 1), extra=extra, selfsig=True)
        b.signal = True
        for e in ("pe", "act", "dve", "pool"):
            self.pending[e].append(b)

    def emit(self):
        cnt = {e: 0 for e in self.ENG}
        for o in self.ops:
            if o.dma == 0 and o.signal:
                cnt[o.eng] += 1
                o.semval = cnt[o.eng]
        waited = {e: {} for e in self.ENG}
        nwait = 0
        for o in self.ops:
            E = self.h[o.eng]
            wd = waited[o.eng]
            for d in o.deps:
                if d.dma:
                    key, sem, val = ("d", d.dsem), self.dsems[d.dsem], d.dval
                else:
                    key, sem, val = ("e", d.eng), self.esem[d.eng], d.semval
                if wd.get(key, 0) < val:
                    E.wait_ge(sem, val)
                    wd[key] = val
                    nwait += 1
            if o.dma:
                key = ("d", o.dsem)
                if o.dprev > 0 and wd.get(key, 0) < o.dprev:
                    E.wait_ge(self.dsems[o.dsem], o.dprev)
                    wd[key] = o.dprev
                for ins in o.fn():
                    ins.then_inc(self.dsems[o.dsem], 16)
            else:
                ins = o.fn()
                if o.signal and not o.selfsig:
                    ins.then_inc(self.esem[o.eng], 1)
        E = self.h["sp"]
        for i, t in enumerate(self.dtot):
            if t > 0 and waited["sp"].get(("d", i), 0) < t:
                E.wait_ge(self.dsems[i], t)
        return nwait


class Arena:
    def __init__(self, ap_f32, nbytes):
        self.base = ap_f32
        self.nbytes = nbytes
        self.top = 0
        self.peak = 0

    def alloc(self, shape, dt):
        esz = 2 if dt == BF16 else 4
        n = 1
        for d in shape[1:]:
            n *= d
        nb = (n * esz + 63) // 64 * 64
        off = self.top
        self.top += nb
        self.peak = max(self.peak, self.top)
        assert self.top <= self.nbytes, f"SBUF arena overflow: {self.top} > {self.nbytes}"
        self.last_off = off
        return self.view_at(off, shape, dt)

    def view_at(self, off, shape, dt):
        esz = 2 if dt == BF16 else 4
        n = 1
        for d in shape[1:]:
            n *= d
        nb = (n * esz + 63) // 64 * 64
        assert off % 4 == 0
        ap = self.base[0:shape[0], off // 4:(off + nb) // 4]
        if dt == BF16:
            ap = ap.bitcast(BF16)
        ap = ap[:, 0:n]
        if len(shape) == 3:
            ap = ap.rearrange("p (a b) -> p a b", a=shape[1])
        elif len(shape) == 4:
            ap = ap.rearrange("p (a b c) -> p a b c", a=shape[1], b=shape[2])
        elif len(shape) == 5:
            ap = ap.rearrange("p (a b c d) -> p a b c d", a=shape[1], b=shape[2], c=shape[3])
        return ap

    def scope(self):
        ar = self

        class _S:
            def __enter__(self_):
                self_.m = ar.top
                return self_

            def __exit__(self_, *a):
                ar.top = self_.m
                return False

            def enter_context(self_, x):
                return x
        return _S()


def _rope_tables(hd, n_tiles=16):
    half = hd // 2
    q = half // 2
    inv = (10000.0 ** (-np.arange(q, dtype=np.float32) * 2.0 / half)).astype(np.float32)
    tok = np.arange(S)
    row = (tok // 64).astype(np.float32)
    col = (tok % 64).astype(np.float32)
    ar = row[:, None] * inv[None, :]
    ac = col[:, None] * inv[None, :]
    cr, sr, cc, sc = np.cos(ar), np.sin(ar), np.cos(ac), np.sin(ac)
    cos = np.concatenate([cr, cr, cc, cc], axis=1).astype(np.float32)
    sin = np.concatenate([-sr, sr, -sc, sc], axis=1).astype(np.float32)
    cos = cos.reshape(n_tiles, 128, hd).transpose(1, 0, 2)
    sin = sin.reshape(n_tiles, 128, hd).transpose(1, 0, 2)
    return np.ascontiguousarray(cos), np.ascontiguousarray(sin)


def _band_tables():
    T = 384
    out = np.zeros((128, 20, 128), np.float32)
    for g, w in enumerate((2, 4, 8, 16)):
        M = np.zeros((T, T), np.float32)
        for t in range(T):
            lo = min(max(t - w // 2, 0), T)
            hi = min(max(t - w // 2 + w, 0), T)
            M[lo:hi, t] += 1.0 / (hi - lo)
            M[t, t] -= 1.0
        out[:, g * 5 + 0, :] = M[128:256, 128:256]
        out[:, g * 5 + 1, :] = M[0:128, 128:256]
        out[:, g * 5 + 2, :] = M[256:384, 128:256]
        out[:, g * 5 + 3, :] = M[0:128, 0:128]
        out[:, g * 5 + 4, :] = M[256:384, 256:384]
        assert np.array_equal(M[128:256, 0:128], out[:, g * 5 + 2, :])
        assert np.array_equal(M[128:256, 256:384], out[:, g * 5 + 1, :])
    return out


NA_NEG = -240000.0


def _na_tables():
    rows = 32
    masks = []
    keyof = {}
    plan = []
    qc = np.arange(64)
    c0 = np.clip(qc - 8, 0, 48)
    kc = np.arange(64)
    colvalid = (kc[:, None] >= c0[None, :]) & (kc[:, None] < c0[None, :] + 16)
    for i in range(16):
        r0 = [int(np.clip(2 * i + b - 4, 0, rows - 8)) for b in (0, 1)]
        jlo = min(r0) // 2
        jhi = (max(r0) + 7) // 2
        lst = []
        for j in range(jlo, jhi + 1):
            m = np.zeros((128, 128), np.float32)
            for a in (0, 1):
                for b in (0, 1):
                    kr = 2 * j + a
                    ok = (kr >= r0[b]) and (kr < r0[b] + 8)
                    blk = colvalid if ok else np.zeros((64, 64), bool)
                    m[a * 64:(a + 1) * 64, b * 64:(b + 1) * 64] = np.where(blk, 0.0, NA_NEG)
            if (m == NA_NEG).all():
                continue
            kb = m.tobytes()
            if kb not in keyof:
                keyof[kb] = len(masks)
                masks.append(m)
            lst.append((j, keyof[kb]))
        plan.append(lst)
    return plan, np.stack(masks, axis=1)


NA_PLAN, NA_MASKS = _na_tables()
NMASK = NA_MASKS.shape[1]


def _j2x8():
    m = np.zeros((128, 128), np.float32)
    for a in (0, 1):
        for k in range(64):
            m[a * 64 + k, a * 64 + 63 - k] = 8.0
    return m


CB_IDENT, CB_ONES, CB_J2, CB_BAND, CB_MASK = 0, 128, 256, 384, 384 + 2560
CB_N = CB_MASK + NMASK * 128
CF_IDENT, CF_COS0, CF_SIN0, CF_COS1, CF_SIN1 = 0, 128, 128 + 1024, 128 + 2048, 128 + 2048 + 512
CF_N = CF_SIN1 + 512
G_Q0, G_K0, G_QA, G_KVA, G_MQ, G_MK, G_NQ, G_NK, G_N = 0, 64, 128, 512, 768, 864, 960, 1024, 1088
V_ADAB, V_NMIX0, V_NFFN0, V_NMIX1, V_NFFN1, V_PSC, V_N = 0, 96, 104, 112, 120, 128, 130


def _const_arrays():
    cb = np.zeros((128, CB_N), np.float32)
    cb[:, CB_IDENT:CB_IDENT + 128] = np.eye(128, dtype=np.float32)
    cb[:, CB_ONES:CB_ONES + 128] = 1.0 / 1024.0
    cb[:, CB_J2:CB_J2 + 128] = _j2x8()
    cb[:, CB_BAND:CB_BAND + 2560] = _band_tables().reshape(128, 2560)
    cb[:, CB_MASK:] = NA_MASKS.reshape(128, NMASK * 128)
    cf = np.zeros((128, CF_N), np.float32)
    cf[:, CF_IDENT:CF_IDENT + 128] = np.eye(128, dtype=np.float32)
    c0, s0 = _rope_tables(64)
    c1, s1 = _rope_tables(32)
    cf[:, CF_COS0:CF_COS0 + 1024] = c0.reshape(128, 1024)
    cf[:, CF_SIN0:CF_SIN0 + 1024] = s0.reshape(128, 1024)
    cf[:, CF_COS1:CF_COS1 + 512] = c1.reshape(128, 512)
    cf[:, CF_SIN1:CF_SIN1 + 512] = s1.reshape(128, 512)
    return cb.astype(ml_dtypes.bfloat16), cf


class Builder:
    def __init__(self, stage):
        self.stage = stage
        self.nc = bass.Bass("TRN2", target_bir_lowering=False)

    def mm(self, out, lhsT, rhs, start, stop, r, w):
        nc = self.nc
        return self.P.op("pe", lambda: nc.tensor.matmul(out, lhsT=lhsT, rhs=rhs, start=start, stop=stop), r, w)

    def tr(self, out, in_, ident, r, w):
        nc = self.nc
        return self.P.op("pe", lambda: nc.tensor.transpose(out, in_, ident), r, w)

    def act(self, out, in_, func, r, w, bias=None, scale=None):
        nc = self.nc
        kw = {}
        if bias is not None:
            kw["bias"] = bias
        if scale is not None:
            kw["scale"] = scale
        return self.P.op("act", lambda: nc.scalar.activation(out=out, in_=in_, func=func, **kw), r, w)

    def _ve(self, eng):
        return {"dve": self.nc.vector, "pool": self.nc.gpsimd}[eng]

    def cp(self, eng, out, in_, r, w):
        nc = self.nc
        if eng == "act":
            return self.P.op("act", lambda: nc.scalar.copy(out=out, in_=in_), r, w)
        E = self._ve(eng)
        return self.P.op(eng, lambda: E.tensor_copy(out=out, in_=in_), r, w)

    def tt(self, eng, out, in0, in1, op, r, w):
        E = self._ve(eng)
        return self.P.op(eng, lambda: E.tensor_tensor(out=out, in0=in0, in1=in1, op=op), r, w)

    def stt(self, eng, out, in0, scalar, in1, op0, op1, r, w):
        E = self._ve(eng)
        return self.P.op(eng, lambda: E.scalar_tensor_tensor(out=out, in0=in0, scalar=scalar, in1=in1, op0=op0, op1=op1), r, w)

    def ts(self, eng, out, in0, s1, s2, op0, op1, r, w):
        E = self._ve(eng)
        if s2 is None:
            return self.P.op(eng, lambda: E.tensor_scalar(out=out, in0=in0, scalar1=s1, scalar2=None, op0=op0), r, w)
        return self.P.op(eng, lambda: E.tensor_scalar(out=out, in0=in0, scalar1=s1, scalar2=s2, op0=op0, op1=op1), r, w)

    def red(self, out, in_, r, w):
        nc = self.nc
        return self.P.op("dve", lambda: nc.vector.reduce_sum(out=out, in_=in_, axis=AX.X), r, w)

    def rcp(self, out, in_, r, w):
        nc = self.nc
        return self.P.op("dve", lambda: nc.vector.reciprocal(out=out, in_=in_), r, w)

    def memset(self, eng, ap, val, w):
        E = self._ve(eng)
        return self.P.op(eng, lambda: E.memset(ap, val), (), w)

    def dma(self, eng, pairs, r, w, stream="misc", bg=False):
        E = self.P.h[eng]
        return self.P.op(eng, lambda: [E.dma_start(out=o, in_=i) for (o, i) in pairs], r, w, dma=len(pairs), stream=stream, bg=bg)

    def build(self):
        nc = self.nc
        stage = self.stage
        dbg = stage != "full"
        di = {}

        def inp(name, shape, dt=F32):
            di[name] = nc.dram_tensor(name, list(shape), dt, kind="ExternalInput").ap()
            return di[name]

        x_d = inp("x", [S, D])
        ctx_d = inp("ctx", [CTX, D])
        cond_d = inp("condT", [128, 16])
        vecs_d = inp("vecs", [128, V_N])
        gains_d = inp("gains", [1, G_N])
        cb_d = inp("cbf", [128, CB_N], BF16)
        cf_d = inp("cf32", [128, CF_N])
        rpb_d = inp("rpbR", [8 * 15 * 127 + 128])
        adaw_d = [inp("l0_ada_w", [D, 6 * D]), inp("l1_ada_w", [D, 6 * D])]
        w_in_kv0 = inp("l0_w_in_kv", [D, 768])
        w_in_q0 = inp("l0_w_in_q", [D, 768])
        poolw_d = inp("l0_pool_w", [4, 64, 64])
        wout_d = [inp("l0_w_out_p", [D, D]), inp("l1_w_out", [D, D])]
        w_in1 = inp("l1_w_in", [D, 2208])
        wuq_d = inp("l1_mla_w_uq", [384, 768])
        wukv_d = inp("l1_mla_w_ukv", [256, 1024])
        wg_d = [inp("l0_ffn_w_gate", [D, FFN]), inp("l1_ffn_w_gate", [D, FFN])]
        wu_d = [inp("l0_ffn_w_up", [D, FFN]), inp("l1_ffn_w_up", [D, FFN])]
        wd_d = [inp("l0_ffn_w_down", [FFN, D]), inp("l1_ffn_w_down", [FFN, D])]
        nout = NT if dbg else S
        y_d = nc.dram_tensor("y", [nout, D], F32, kind="ExternalOutput").ap()
        wgs = [nc.dram_tensor(f"wgs{l}", [NJ, 128, 8, 128], BF16).ap() for l in range(2)]
        wus = [nc.dram_tensor(f"wus{l}", [NJ, 128, 8, 128], BF16).ap() for l in range(2)]
        wds = [nc.dram_tensor(f"wds{l}", [8, 128, NJ, 128], BF16).ap() for l in range(2)]

        from contextlib import ExitStack
        with ExitStack() as es:
            def sb(name, shape, dt):
                return AR.alloc(list(shape), dt)

            esems = {e: es.enter_context(nc.semaphore("s_" + e)) for e in Prog.ENG}
            NDS = 30
            dsems = [es.enter_context(nc.semaphore(f"d{i}")) for i in range(NDS)]
            streams = {"misc": [0, 1, 2, 3], "w": [4, 5, 6, 7], "gu": [8, 9, 10, 11], "wd": [12, 13, 14],
                       "xin": [15, 16], "out": [17, 18], "bg": [19, 20, 21, 22, 23, 24], "ada": [25, 26], "c": [27, 28, 29]}
            P = Prog(nc, esems, dsems, streams)
            self.P = P
            R = P.res

            ARENA_BYTES = 207 * 1024
            AR = Arena(es.enter_context(nc.sbuf_tensor("arena", [128, ARENA_BYTES // 4], F32)), ARENA_BYTES)
            self.AR = AR
            ps = es.enter_context(nc.psum_tensor("ps", [128, 8, 512], F32))
            PB = [R("psb", b) for b in range(8)]
            for r_ in PB:
                r_.excl = True

            def psb(b, n=512):
                return ps[:, b, 0:n]

            def psb16(b):
                return ps[:, b, :].bitcast(BF16)

            xT = sb("xT", [128, 8, NT], F32)
            mixT = sb("mixT", [128, 8, NT], BF16)
            self.mix_off = AR.last_off
            vecs = sb("vecs", [128, V_N], F32)
            mvec = sb("mvec", [128, 96, 2], F32)
            avec = sb("avec", [128, 2, 2, 2, 8], F32)
            identb = sb("identb", [128, 128], BF16)
            onesm = sb("onesm", [128, 128], BF16)
            identf = sb("identf", [128, 128], F32)

            def xres(dc, blk):
                return R("xT", dc, blk)

            def blks_of(t0, n):
                return sorted(set(min(t // 512, 4) for t in (t0, t0 + n - 1)))

            def xr(t0, n, dcs=range(8)):
                return [xres(dc, b) for dc in dcs for b in blks_of(t0, n)]

            def mres(c, blk):
                return R("mixT", c, blk)

            bgq = []

            def convert_ffn(l):
                wgv = wg_d[l].rearrange("(c p) (j n) -> j p c n", p=128, n=128)
                wuv = wu_d[l].rearrange("(c p) (j n) -> j p c n", p=128, n=128)
                wdv = wd_d[l].rearrange("(j p) (c n) -> c p j n", p=128, n=128)

                def mk(dst, src, res):
                    return lambda: self.dma("pool", [(dst, src)], (), [res], stream="bg", bg=True)
                for j in range(NJ):
                    bgq.append(mk(wgs[l][j], wgv[j], R("wgs", l, j)))
                    bgq.append(mk(wus[l][j], wuv[j], R("wus", l, j)))
                for c in range(8):
                    bgq.append(mk(wds[l][c], wdv[c], R("wds", l, c)))

            def bg_step(k):
                for _ in range(k):
                    if bgq:
                        bgq.pop(0)()

            rc = R("consts")
            self.dma("sp", [(vecs[:, :], vecs_d), (identb[:, :], cb_d[:, CB_IDENT:CB_IDENT + 128]),
                            (onesm[:, :], cb_d[:, CB_ONES:CB_ONES + 128]), (identf[:, :], cf_d[:, CF_IDENT:CF_IDENT + 128]),
                            ], (), [rc], stream="c")

            with AR.scope() as s0:
                condT = s0.enter_context(AR.alloc([128, 16], F32))
                scT = s0.enter_context(AR.alloc([128, 8, 2], BF16))
                adaw = [s0.enter_context(AR.alloc([128, 8, 1024], BF16)) for i in range(2)]
                rcond = R("condT")
                self.dma("sp", [(condT[:, :], cond_d)], (), [rcond], stream="c")
                self.act(scT[:, :, :], condT[:, :].rearrange("p (c k) -> p k c", c=2), AF.Silu, [rcond], [R("scT")])
                k = 0
                for l in range(2):
                    src = adaw_d[l].rearrange("(c p) n -> p c n", p=128)
                    for v in range(6):
                        slot = k % 2
                        k += 1
                        ra = R("adaw", slot)
                        self.dma("pool", [(adaw[slot][:, :, :], src[:, :, v * 1024:(v + 1) * 1024])], (), [ra], stream="ada")
                        for m in range(8):
                            col = (l * 48 + v * 8 + m) * 2
                            for c in range(8):
                                self.mm(ps[:, 7, col:col + 2], adaw[slot][:, c, m * 128:(m + 1) * 128], scT[:, c, :],
                                        c == 0, c == 7, [ra, R("scT")], [PB[7]])
                rmv = R("mvec")
                self.tt("dve", mvec[:, :, :], ps[:, 7, 0:192].rearrange("p (a b) -> p a b", b=2),
                        vecs[:, V_ADAB:V_ADAB + 96].unsqueeze(2).to_broadcast([128, 96, 2]), ALU.add, [PB[7], rc], [rmv])
                for l in range(2):
                    for f in range(2):
                        v = 1 + 3 * f
                        ng = vecs[:, V_NMIX0 + 16 * l + 8 * f: V_NMIX0 + 16 * l + 8 * f + 8]
                        for cnd in range(2):
                            self.stt("dve", avec[:, l, f, cnd, :], mvec[:, l * 48 + v * 8: l * 48 + v * 8 + 8, cnd], 1.0, ng,
                                     ALU.add, ALU.mult, [rmv, rc], [R("avec")])

                def MV(l, v, cnd, c):
                    return mvec[:, l * 48 + v * 8 + c, cnd:cnd + 1]

                def AV(l, f, cnd, c):
                    return avec[:, l, f, cnd, c:c + 1]
                self.MV, self.AV = MV, AV
                RMOD = [rmv, R("avec")]

                xin = [s0.enter_context(AR.alloc([128, D], F32)) for i in range(2)]
                for tt in range(18):
                    slot = tt % 2
                    rx = R("xin", slot)
                    src = x_d[tt * 128:(tt + 1) * 128, :] if tt < 16 else ctx_d[(tt - 16) * 128:(tt - 15) * 128, :]
                    self.dma("sp", [(xin[slot][:, :], src)], (), [rx], stream="xin")
                    b0 = 2 * (tt % 2)
                    for c in range(8):
                        self.tr(ps[:, b0 + c // 4, (c % 4) * 128:(c % 4 + 1) * 128], xin[slot][:, c * 128:(c + 1) * 128], identf[:, :],
                                [rx, rc], [PB[b0 + c // 4]])
                    dst = xT[:, :, tt * 128:(tt + 1) * 128]
                    for hh in range(2):
                        self.cp("act" if hh == 0 else "dve", dst[:, hh * 4:(hh + 1) * 4, :],
                                ps[:, b0 + hh, :].rearrange("p (c n) -> p c n", n=128), [PB[b0 + hh]], xr(tt * 128, 128, range(hh * 4, hh * 4 + 4)))
                P.barrier()
            import os as _os
            if stage != "load" and not _os.environ.get("K_SKIP_CONVERT"):
                convert_ffn(0)

            MIX_OFF = self.mix_off

            def modulate_g(t0, n, l, f, cnd, hT, rh, tmp, on_dve=False):
                sqb, sdt, tm = tmp
                rsq = [R("m_sq", i) for i in range(2)]
                for c in range(8):
                    i = c % 2
                    if on_dve:
                        self.tt("dve", sqb[i][:, 0:n], xT[:, c, t0:t0 + n], xT[:, c, t0:t0 + n], ALU.mult, xr(t0, n, [c]), [rsq[i]])
                    else:
                        self.act(sqb[i][:, 0:n], xT[:, c, t0:t0 + n], AF.Square, xr(t0, n, [c]), [rsq[i]])
                    self.mm(psb(7, n), onesm[:, :], sqb[i][:, 0:n], c == 0, c == 7, [rsq[i], rc], [PB[7]])
                    yield
                self.act(sdt[:, 0:n], psb(7, n), AF.Ln, [PB[7]], [R("m_sd")], bias=EPS, scale=1.0)
                yield
                self.act(sdt[:, 0:n], sdt[:, 0:n], AF.Exp, [R("m_sd")], [R("m_sd")], scale=-0.5)
                yield
                shift_v = 0 if f == 0 else 3
                for c in range(8):
                    i = c % 2
                    rt = R("m_tm", i)
                    self.stt("dve", tm[i][:, 0:n], xT[:, c, t0:t0 + n], self.AV(l, f, cnd, c), sdt[:, 0:n], ALU.mult, ALU.mult,
                             xr(t0, n, [c]) + [R("m_sd")] + RMOD, [rt])
                    yield
                    if on_dve:
                        self.ts("dve", hT[:, c, 0:n], tm[i][:, 0:n], self.MV(l, shift_v, cnd, c), None, ALU.add, None, [rt] + RMOD, [rh])
                    else:
                        self.act(hT[:, c, 0:n], tm[i][:, 0:n], AF.Identity, [rt] + RMOD, [rh], bias=self.MV(l, shift_v, cnd, c))
                    yield

            def modulate(*a):
                for _ in modulate_g(*a):
                    pass

            def mod_tmp():
                return ([AR.alloc([128, 512], BF16) for i in range(2)], AR.alloc([128, 512], F32),
                        [AR.alloc([128, 512], F32) for i in range(2)])

            def hn_alloc(w):
                t2_ = AR.alloc([128, w], F32)
                sq_ = AR.view_at(AR.last_off, [128, w], BF16)
                return (sq_, AR.alloc([128, w], F32), t2_, AR.alloc([128, 16], F32), AR.alloc([128, 16], F32))

            def headnorm_g(segs, H, Dh, gain_ap, out_bf, tmp, rout, rg, rope=None, tag=""):
                sqt, kg, t2, ss, sd = tmp
                rsq, rkg, rt2, rss, rsd = R("hn_sq" + tag), R("hn_kg" + tag), R("hn_sq" + tag), R("hn_ss" + tag), R("hn_sd" + tag)
                kgv = kg[:, 0:H * Dh].rearrange("p (h d) -> p h d", d=Dh)
                sqv = sqt[:, 0:H * Dh].rearrange("p (h d) -> p h d", d=Dh)
                for (src, h0, h1, rsrc) in segs:
                    self.act(sqv[:, h0:h1, :], src, AF.Square, rsrc, [rsq])
                    yield
                    self.tt("dve", kgv[:, h0:h1, :], src, gain_ap.unsqueeze(1).to_broadcast([128, h1 - h0, Dh]), ALU.mult, rsrc + [rg], [rkg])
                    yield
                yield "psum_done"
                self.red(ss[:, 0:H], sqv, [rsq], [rss])
                yield
                self.act(sd[:, 0:H], ss[:, 0:H], AF.Ln, [rss], [rsd], bias=EPS, scale=1.0 / Dh)
                self.act(sd[:, 0:H], sd[:, 0:H], AF.Exp, [rsd], [rsd], scale=-0.5)
                yield
                rsb = sd[:, 0:H].unsqueeze(2).to_broadcast([128, H, Dh])
                if rope is None:
                    self.tt("dve", out_bf, kgv, rsb, ALU.mult, [rkg, rsd], rout)
                    yield
                    return
                cos, sin, r0, Dr, rtab = rope
                q4 = Dr // 4
                self.tt("dve", kgv, kgv, rsb, ALU.mult, [rkg, rsd], [rkg])
                yield
                knr = kgv[:, :, r0:r0 + Dr].rearrange("p h (a b c) -> p h a b c", a=2, b=2)
                t2v = t2[:, 0:H * Dr].rearrange("p (h a b c) -> p h a b c", a=2, b=2, c=q4)
                sinv = sin.rearrange("p (a b c) -> p a b c", a=2, b=2)
                cosv = cos.unsqueeze(1).to_broadcast([128, H, Dr])
                for b in range(2):
                    self.tt("dve", t2v[:, :, :, b, :], knr[:, :, :, 1 - b, :], sinv[:, :, b, :].unsqueeze(1).to_broadcast([128, H, 2, q4]),
                            ALU.mult, [rkg, rtab], [rt2])
                    yield
                if r0 > 0:
                    self.cp("act", out_bf[:, :, 0:r0], kgv[:, :, 0:r0], [rkg], rout)
                self.tt("dve", kgv[:, :, r0:r0 + Dr], kgv[:, :, r0:r0 + Dr], cosv, ALU.mult, [rkg, rtab], [rkg])
                yield
                self.tt("dve", out_bf[:, :, r0:r0 + Dr], kgv[:, :, r0:r0 + Dr], t2[:, 0:H * Dr].rearrange("p (h d) -> p h d", d=Dr),
                        ALU.add, [rkg, rt2], rout)
                yield

            def headnorm(*a, **k):
                k.setdefault("tag", "0")
                for _ in headnorm_g(*a, **k):
                    pass

            SB_S = [0, 1, 2]
            OB_S = [3, 4]
            state = {"s": 0, "o": 0, "pt": 0}

            def v_lhsT(V, kt, hl):
                pr_, odd = hl // 2, hl % 2
                return V[:, kt, pr_, odd:odd + 2, :].rearrange("p a b -> p (a b)"), odd

            def v_store(V, tt, src_ps, npair, rsrc, rdst, d0=0, dstep=64):
                sv = src_ps.rearrange("p (a b) d -> p a b d", b=2)
                for odd in range(2):
                    self.cp("act", V[:, tt, :, 2 * odd, :], sv[:, :, odd, d0:d0 + 64], rsrc, rdst)

            def attention(jobs, n, scale, PT, recs, stepper=None):
                for job in jobs:
                    keys = job["keys"]
                    nk = len(keys)
                    ob = OB_S[state["o"] % 2]
                    state["o"] += 1
                    sl = []

                    def emit_s(i):
                        b = SB_S[state["s"] % 3]
                        state["s"] += 1
                        kl, vl, rkv = keys[i]
                        self.mm(psb(b, n), kl, job["q"], True, True, rkv + job["rq"], [PB[b]])
                        sl.append(b)
                    emit_s(0)
                    if nk > 1:
                        emit_s(1)
                    for i in range(nk):
                        pi = state["pt"] % 3
                        state["pt"] += 1
                        rp = R("PT", pi)
                        self.act(PT[pi][:, 0:n], psb(sl[i], n), AF.Exp, [PB[sl[i]]], [rp], scale=scale)
                        if i + 2 < nk:
                            emit_s(i + 2)
                        self.mm(psb(ob, n), keys[i][1], PT[pi][:, 0:n], i == 0, i == nk - 1, [rp] + keys[i][2], [PB[ob]])
                        if stepper is not None:
                            stepper()
                    finish_head(ob, n, job, recs)

            def finish_head(ob, n, job, recs):
                oh = job["ohalf"]
                po = slice(oh * 64, oh * 64 + 64)
                psum_ = slice((1 - oh) * 64, (1 - oh) * 64 + 64)
                rr = R("recs")
                self.act(recs[psum_, 0:n], ps[psum_, ob, 0:n], AF.Ln, [PB[ob]], [rr])
                self.act(recs[psum_, 0:n], recs[psum_, 0:n], AF.Exp, [rr], [rr], scale=-1.0)
                self.tt("dve", job["dst"], ps[po, ob, 0:n], recs[psum_, 0:n], ALU.mult, [PB[ob], rr], job["rdst"])

            def transposes_to(src_bf, nblk, width, rsrc, bank):
                pb = psb16(bank)
                for i in range(nblk):
                    self.tr(pb[0:width, i * 128:(i + 1) * 128], src_bf[:, i * width:(i + 1) * width], identb[:, :], rsrc + [rc], [PB[bank]])
                return pb

            def load_w(dst3, src2, rw, eng="pool"):
                self.dma(eng, [(dst3, src2.rearrange("(c p) n -> p c n", p=128))], (), [rw], stream="w")

            def final_proj(l, groups):
                with AR.scope():
                    wo = AR.alloc([128, 8, D], BF16)
                    rwo = R("wo")
                    load_w(wo[:, :, :], wout_d[l], rwo)
                    k = 0
                    for (t0, n, cnd) in groups:
                        for dc in range(8):
                            b = k % 8
                            k += 1
                            for mc in range(8):
                                self.mm(psb(b, n), wo[:, mc, dc * 128:(dc + 1) * 128], mixT[:, mc, t0:t0 + n], mc == 0, mc == 7,
                                        [rwo] + [mres(mc, bb) for bb in blks_of(t0, n)], [PB[b]])
                            self.stt("dve", xT[:, dc, t0:t0 + n], psb(b, n), self.MV(l, 2, cnd, dc), xT[:, dc, t0:t0 + n], ALU.mult, ALU.add,
                                     [PB[b]] + RMOD + xr(t0, n, [dc]), xr(t0, n, [dc]))
                    P.barrier()

            def ffn(l, groups):
                with AR.scope():
                    hT = AR.view_at(MIX_OFF, [128, 8, 1024], BF16)
                    gu = [AR.view_at(MIX_OFF + 16384 + i * 4096, [128, 2, 8, 128], BF16) for i in range(4)]
                    actT = AR.alloc([128, NJ, 1024], BF16)
                    wdr = [AR.alloc([128, NJ, 128], BF16) for i in range(3)]
                    sg = [AR.alloc([128, 1024], F32) for i in range(2)]
                    tmp = mod_tmp()
                    kg = 0
                    kd = 0
                    if l == 0:
                        bg_step(len(bgq))
                        if stage not in ("l0mix", "l0"):
                            convert_ffn(1)
                    else:
                        bg_step(len(bgq))
                    for (t0, n, cnd) in groups:
                        rh = R("f_hT")
                        for h0 in range(0, n, 512):
                            hn = min(512, n - h0)
                            modulate(t0 + h0, hn, l, 1, cnd, hT[:, :, h0:h0 + hn], rh, tmp)
                        nh = (n + 511) // 512
                        CUT = _os.environ.get("K_CUT", "")
                        if CUT == "m":
                            return
                        for j in range(NJ):
                            if CUT == "A1" and j == 1:
                                return
                            if CUT == "A3" and j == 3:
                                return
                            bg_step(1)
                            slot = kg % 4
                            rg = R("f_gu", slot)
                            self.dma("sp", [(gu[slot][:, 0, :, :], wgs[l][j]), (gu[slot][:, 1, :, :], wus[l][j])],
                                     [R("wgs", l, j), R("wus", l, j)], [rg], stream="gu")
                            set_ = (kg % 2) * 4
                            kg += 1
                            for gi in range(2):
                                for hh in range(nh):
                                    hn = min(512, n - hh * 512)
                                    b = set_ + gi * 2 + hh
                                    for c in range(8):
                                        self.mm(psb(b, hn), gu[slot][:, gi, c, :], hT[:, c, hh * 512:hh * 512 + hn], c == 0, c == 7, [rg, rh], [PB[b]])
                            si = j % 2
                            rs_ = R("f_sg", si)
                            for hh in range(nh):
                                hn = min(512, n - hh * 512)
                                self.act(sg[si][:, hh * 512:hh * 512 + hn], psb(set_ + hh, hn), AF.Silu, [PB[set_ + hh]], [rs_])
                                self.tt("dve", actT[:, j, hh * 512:hh * 512 + hn], psb(set_ + 2 + hh, hn), sg[si][:, hh * 512:hh * 512 + hn], ALU.mult,
                                        [PB[set_ + 2 + hh], rs_], [R("f_actT", j)])
                        if CUT == "A":
                            return
                        for dc in range(8):
                            if CUT == "B1" and dc == 1:
                                return
                            slot = kd % 3
                            rw = R("f_wd", slot)
                            self.dma("sp", [(wdr[slot][:, :, :], wds[l][dc])], [R("wds", l, dc)], [rw], stream="wd")
                            set_ = (kd % 4) * 2
                            kd += 1
                            for hh in range(nh):
                                hn = min(512, n - hh * 512)
                                b = set_ + hh
                                for j in range(NJ):
                                    self.mm(psb(b, hn), wdr[slot][:, j, :], actT[:, j, hh * 512:hh * 512 + hn], j == 0, j == NJ - 1,
                                            [rw, R("f_actT", j)], [PB[b]])
                                tt0 = t0 + hh * 512
                                self.stt("dve", xT[:, dc, tt0:tt0 + hn], psb(b, hn), self.MV(l, 5, cnd, dc), xT[:, dc, tt0:tt0 + hn], ALU.mult, ALU.add,
                                         [PB[b]] + RMOD + xr(tt0, hn, [dc]), xr(tt0, hn, [dc]))
                    P.barrier()

            LAT512 = [(g * 512, 512, 0) for g in range(4)]
            CTXG = (S, CTX, 1)

            def tile_of(t):
                return t // 128

            def tiles_of(groups):
                return [(gi, t0, n, cnd, ti) for gi, (t0, n, cnd) in enumerate(groups) for ti in range(n // 128)]

            def run_pass1(groups, l, hTs, mtmp, projA, postB):
                tl = tiles_of(groups)
                first_of = {}
                for k_, t_ in enumerate(tl):
                    first_of.setdefault(t_[0], k_)
                modgen = {}

                def start_mod(gi):
                    if gi < len(groups) and gi not in modgen:
                        t0, n, cnd = groups[gi]
                        bg_step(4)
                        modgen[gi] = modulate_g(t0, n, l, 0, cnd, hTs[gi % len(hTs)], R("hT", gi % len(hTs)), mtmp)

                def finish_mod(gi):
                    start_mod(gi)
                    for _ in modgen[gi]:
                        pass

                def A(k):
                    gi, t0, n, cnd, ti = tl[k]
                    if ti == 0:
                        finish_mod(gi)
                    projA(tl[k], hTs[gi % len(hTs)], R("hT", gi % len(hTs)), k % 2)
                active = []
                flags = {}

                def adv(g_):
                    try:
                        v = next(g_)
                        if v == "psum_done":
                            flags[id(g_)] = True
                    except StopIteration:
                        flags[id(g_)] = True
                        if g_ in active:
                            active.remove(g_)

                def adv_mod():
                    for mg in modgen.values():
                        try:
                            next(mg)
                        except StopIteration:
                            pass

                def step_all(until_len):
                    while len(active) > until_len:
                        for g_ in list(active):
                            adv(g_)
                        adv_mod()
                A(0)
                if len(tl) > 1:
                    A(1)
                for k in range(len(tl)):
                    gi, t0, n, cnd, ti = tl[k]
                    if ti == 0:
                        start_mod(gi + 1)
                    gk = postB(tl[k], k % 2)
                    active.append(gk)
                    step_all(1)
                    while not flags.get(id(gk), False):
                        adv(gk)
                        adv_mod()
                    if k + 2 < len(tl):
                        A(k + 2)
                step_all(0)

            def run_pass2(groups, l, hT, mtmp, tile_g, make_jobs, attend, rate=1, par_tiles=False):
                rh = R("hT", 0)

                def qproc_g(gi):
                    t0, n, cnd = groups[gi]
                    bg_step(4)
                    yield from modulate_g(t0, n, l, 0, cnd, hT, rh, mtmp, on_dve=True)
                    for ti in range(n // 128):
                        yield from tile_g((gi, t0, n, cnd, ti), hT, rh)
                if par_tiles:
                    t0, n, cnd = groups[0]
                    bg_step(4)
                    for _ in modulate_g(t0, n, l, 0, cnd, hT, rh, mtmp, on_dve=True):
                        pass
                    act_ = []
                    for ti in range(n // 128):
                        act_.append(tile_g((0, t0, n, cnd, ti), hT, rh, ti % 2))
                        while len(act_) > 1:
                            for g_ in list(act_):
                                try:
                                    next(g_)
                                except StopIteration:
                                    act_.remove(g_)
                    for g_ in act_:
                        for _ in g_:
                            pass
                else:
                    for _ in qproc_g(0):
                        pass
                for gi in range(len(groups)):
                    gen = qproc_g(gi + 1) if gi + 1 < len(groups) else None

                    def stepper(gen=gen):
                        if gen is None:
                            return
                        for _ in range(rate):
                            try:
                                next(gen)
                            except StopIteration:
                                return
                    attend(make_jobs(gi), groups[gi], stepper)
                    if gen is not None:
                        for _ in gen:
                            pass

            def with_hooks(njobs, hooks):
                pos = [(h + 1) * njobs // (len(hooks) + 1) for h in range(len(hooks))]
                st = {"h": 0}

                def before(j):
                    while st["h"] < len(hooks) and pos[st["h"]] <= j:
                        hooks[st["h"]]()
                        st["h"] += 1

                def flush():
                    while st["h"] < len(hooks):
                        hooks[st["h"]]()
                        st["h"] += 1
                return before, flush

            def attention_h(jobs, n, scale, PT, recs, hooks):
                before, flush = with_hooks(len(jobs), hooks)
                for j, job in enumerate(jobs):
                    before(j)
                    attention([job], n, scale, PT, recs)
                flush()

            def layer0():
                with AR.scope():
                    T = AR.alloc
                    gains = T([128, 128], F32)
                    cos0 = T([128, 16, 64], F32)
                    sin0 = T([128, 16, 64], F32)
                    kT = T([128, 2, NT], BF16)
                    Vp = T([128, 18, 2, 3, 64], BF16)
                    hT = T([128, 8, 512], BF16)
                    mtmp = mod_tmp()
                    hn_tmp = hn_alloc(768)
                    qkbf = T([128, 768], BF16)
                    rl0 = R("l0c")
                    rg0 = R("gains0")
                    self.dma("sp", [(cos0[:, :, :], cf_d[:, CF_COS0:CF_COS0 + 1024].rearrange("p (t d) -> p t d", d=64)),
                                    (sin0[:, :, :], cf_d[:, CF_SIN0:CF_SIN0 + 1024].rearrange("p (t d) -> p t d", d=64))], (), [rl0], stream="c")
                    self.dma("sp", [(gains[:, :], gains_d[:, 0:128].partition_broadcast(128))], (), [rg0], stream="c")
                    self.memset("pool", Vp[:, :, :, 1, :], 1.0, [R("Vp1")])
                    groups = LAT512 + [CTXG]
                    with AR.scope():
                        band = T([128, 20, 128], BF16)
                        wblk = T([128, 2, 128], BF16)
                        dsb = T([128, 2, 128], BF16)
                        hT2 = T([128, 8, 512], BF16)
                        hnP = [hn_alloc(256) for i in range(2)]
                        qkP = [T([128, 256], BF16) for i in range(2)]
                        a_sb = AR.view_at(MIX_OFF + 2 * NT * 2, [128, 18, 256], BF16)
                        wkv = AR.view_at(MIX_OFF + 4 * NT * 2, [128, 8, 768], BF16)
                        rband = R("band")
                        self.dma("sp", [(band[:, :, :], cb_d[:, CB_BAND:CB_BAND + 2560].rearrange("p (t d) -> p t d", d=128))], (), [rband], stream="c")
                        self.memset("pool", wblk[:, :, :], 0.0, [R("wblk")])
                        self.dma("pool", [(wblk[(g % 2) * 64:(g % 2 + 1) * 64, g // 2, (g % 2) * 64:(g % 2 + 1) * 64], poolw_d[g]) for g in range(4)],
                                 (), [R("wblk")], stream="w")
                        rwkv = R("wkv")
                        load_w(wkv[:, :, :], w_in_kv0, rwkv)

                        def projA(tile, hTg, rh, par):
                            gi, t0, n, cnd, ti = tile
                            for nb, (c0, c1) in enumerate(((0, 512), (512, 768))):
                                b = 2 * par + nb
                                for c in range(8):
                                    self.mm(ps[:, b, 0:c1 - c0], hTg[:, c, ti * 128:(ti + 1) * 128], wkv[:, c, c0:c1], c == 0, c == 7, [rh, rwkv], [PB[b]])

                        def postB(tile, par):
                            gi, t0, n, cnd, ti = tile
                            tt = tile_of(t0) + ti
                            b0, b1 = 2 * par, 2 * par + 1
                            self.cp("act", a_sb[:, tt, :], ps[:, b0, 0:256], [PB[b0]], [R("a_sb", tt)])
                            yield
                            v_store(Vp, tt, ps[:, b1, 0:256].rearrange("p (h d) -> p h d", d=64), 2, [PB[b1]], [R("Vp", tt)])
                            yield
                            rope = None if cnd == 1 else (cos0[:, tt, :], sin0[:, tt, :], 0, 64, rl0)
                            kout = qkP[par][:, 0:256].rearrange("p (h d) -> p h d", d=64)
                            yield from headnorm_g([(ps[:, b0, 256:512].rearrange("p (h d) -> p h d", d=64), 0, 4, [PB[b0]])], 4, 64,
                                                  gains[:, 64:128], kout, hnP[par], [R("qkbf", par)], rg0, rope=rope, tag=str(par))
                            pb = transposes_to(qkP[par][:, 0:256], 2, 128, [R("qkbf", par)], 4 + par)
                            yield
                            self.cp("act", kT[:, :, tt * 128:(tt + 1) * 128], pb[:, 0:256].rearrange("p (m n) -> p m n", n=128), [PB[4 + par]], [R("kT", tt)])
                            yield
                        run_pass1(groups, 0, [hT, hT2], mtmp, projA, postB)
                        for tt in range(18):
                            first = tt in (0, 16)
                            last = tt in (15, 17)
                            bA, bB = (5, 6) if tt % 2 == 0 else (0, 1)
                            for g in range(4):
                                terms = [(tt, 3 if first else (4 if last else 0))]
                                if not first:
                                    terms.append((tt - 1, 1))
                                if not last:
                                    terms.append((tt + 1, 2))
                                for k, (ts_, var) in enumerate(terms):
                                    self.mm(ps[(g % 2) * 64:(g % 2 + 1) * 64, bA, (g // 2) * 128:(g // 2 + 1) * 128],
                                            a_sb[:, ts_, g * 64:(g + 1) * 64], band[:, g * 5 + var, :], k == 0, k == len(terms) - 1,
                                            [R("a_sb", ts_), rband], [PB[bA]])
                            self.cp("act", dsb[:, :, :], ps[:, bA, 0:256].rearrange("p (a n) -> p a n", n=128), [PB[bA]], [R("dsb")])
                            for pr in range(2):
                                self.mm(ps[:, bB, pr * 128:(pr + 1) * 128], wblk[:, pr, :], dsb[:, pr, :], True, True, [R("dsb"), R("wblk")], [PB[bB]])
                            for pr in range(2):
                                self.ts("dve", mixT[:, pr, tt * 128:(tt + 1) * 128], ps[:, bB, pr * 128:(pr + 1) * 128], vecs[:, V_PSC + pr:V_PSC + pr + 1],
                                        None, ALU.mult, None, [PB[bB], rc], [mres(pr, min(tt // 4, 4))])
                        P.barrier()
                    with AR.scope():
                        wq = T([128, 8, 768], BF16)
                        qT = [T([128, 6, 2, 512], BF16) for i in range(2)]
                        PT = [T([128, 512], BF16) for i in range(3)]
                        recs = T([128, 512], F32)
                        rwq = R("wq")
                        load_w(wq[:, :, :], w_in_q0, rwq)
                        for i in range(2):
                            self.memset("pool", qT[i][:, :, :, :], 0.0, [R("qT", i)])

                        def tile_g(tile, hTg, rh):
                            gi, t0, n, cnd, ti = tile
                            tt = tile_of(t0) + ti
                            for nb, (c0, c1) in enumerate(((0, 512), (512, 768))):
                                for c in range(8):
                                    self.mm(ps[:, 5 + nb, 0:c1 - c0], hTg[:, c, ti * 128:(ti + 1) * 128], wq[:, c, c0:c1], c == 0, c == 7, [rh, rwq], [PB[5 + nb]])
                                yield
                            rope = None if cnd == 1 else (cos0[:, tt, :], sin0[:, tt, :], 0, 64, rl0)
                            qout = qkbf[:, 0:768].rearrange("p (h d) -> p h d", d=64)
                            yield from headnorm_g([(ps[:, 5, 0:512].rearrange("p (h d) -> p h d", d=64), 0, 8, [PB[5]]),
                                                   (ps[:, 6, 0:256].rearrange("p (h d) -> p h d", d=64), 8, 12, [PB[6]])], 12, 64,
                                                  gains[:, 0:64], qout, hn_tmp, [R("qkbf")], rg0, rope=rope, tag="0")
                            qTg, rq = qT[gi % 2], R("qT", gi % 2)
                            pb = transposes_to(qkbf[:, 0:768], 6, 128, [R("qkbf")], 7)
                            yield
                            for hf in range(2):
                                hp_ = slice(hf * 64, hf * 64 + 64)
                                self.cp("dve", qTg[hp_, :, hf, ti * 128:(ti + 1) * 128], pb[hp_, 0:768].rearrange("p (m n) -> p m n", n=128), [PB[7]], [rq])
                                yield

                        def make_jobs(gi):
                            t0, n, cnd = groups[gi]
                            qTg, rq = qT[gi % 2], R("qT", gi % 2)
                            ktiles = [16, 17] + (list(range(16)) if cnd == 0 else [])
                            jobs = []
                            for hs in range(12):
                                g = Q_PERM[hs] // 3
                                half = hs % 2
                                assert g % 2 == half
                                pr = slice(half * 64, half * 64 + 64)
                                keys = []
                                for kt in ktiles:
                                    vl, oh = v_lhsT(Vp, kt, g)
                                    keys.append((kT[:, g // 2, kt * 128:(kt + 1) * 128], vl, [R("kT", kt), R("Vp", kt), R("Vp1")]))
                                jobs.append(dict(q=qTg[:, hs // 2, half, 0:n], keys=keys, rq=[rq], ohalf=oh,
                                                 dst=mixT[pr, 2 + hs // 2, t0:t0 + n], rdst=[mres(2 + hs // 2, min(t0 // 512, 4))]))
                            return jobs

                        def attend(jobs, grp, stepper):
                            attention(jobs, grp[1], 0.125, PT, recs, stepper)
                        run_pass2(groups, 0, hT, mtmp, tile_g, make_jobs, attend, rate=1)
                        P.barrier()
                final_proj(0, LAT512 + [CTXG])

            def layer1():
                with AR.scope():
                    T = AR.alloc
                    GO = 128
                    gains = T([128, G_N - GO], F32)
                    hT = T([128, 8, 512], BF16)
                    mtmp = mod_tmp()
                    hn_tmp = hn_alloc(384)
                    qkbf = T([128, 384], BF16)
                    hnP = [hn_tmp, hn_alloc(384)]
                    qkP = [qkbf, T([128, 384], BF16)]
                    PT = [T([128, 512], BF16) for i in range(3)]
                    recs = T([128, 512], F32)
                    rg1 = R("gains1")
                    self.dma("sp", [(gains[:, :], gains_d[:, GO:G_N].partition_broadcast(128))], (), [rg1], stream="c")
                    groups_all = LAT512 + [CTXG]

                    with AR.scope():
                        j2 = T([128, 128], BF16)
                        msk = T([128, NMASK, 128], BF16)
                        rna = R("nac")
                        self.dma("sp", [(j2[:, :], cb_d[:, CB_J2:CB_J2 + 128]),
                                        (msk[:, :, :], cb_d[:, CB_MASK:CB_MASK + NMASK * 128].rearrange("p (t d) -> p t d", d=128))], (), [rna], stream="c")
                        XH = T([128, 7, 4, 128], BF16)
                        kTn = T([128, 2, NT], BF16)
                        Vn = T([128, 18, 2, 3, 64], BF16)
                        self.memset("pool", Vn[:, :, :, 1, :], 1.0, [R("Vn1")])
                        for hh in range(2):
                            rxh = R("XH")
                            for off in range(-3, 4):
                                prs = []
                                for a in range(2):
                                    for b in range(2):
                                        dr = 2 * off + a - b
                                        base = (hh * 4 * 15 + dr + 7) * 127
                                        src = bass.AP(rpb_d.tensor, base, [[1, 64], [15 * 127, 4], [1, 64]])
                                        prs.append((XH[a * 64:(a + 1) * 64, off + 3, :, b * 64:(b + 1) * 64], src))
                                self.dma("pool", prs, (), [rxh], stream="w")
                            with AR.scope():
                                wkv = T([128, 8, 512], BF16)
                                hT2 = T([128, 8, 512], BF16)
                                rw = R("wnkv")
                                self.dma("pool", [(wkv[:, :, 0:256], w_in1[:, 1184 + hh * 256:1184 + (hh + 1) * 256].rearrange("(c p) n -> p c n", p=128)),
                                                  (wkv[:, :, 256:512], w_in1[:, 1696 + hh * 256:1696 + (hh + 1) * 256].rearrange("(c p) n -> p c n", p=128))],
                                         (), [rw], stream="w")

                                def projA(tile, hTg, rh, par):
                                    gi, t0, n, cnd, ti = tile
                                    for c in range(8):
                                        self.mm(ps[:, par, :], hTg[:, c, ti * 128:(ti + 1) * 128], wkv[:, c, :], c == 0, c == 7, [rh, rw], [PB[par]])

                                def postB(tile, par):
                                    gi, t0, n, cnd, ti = tile
                                    tt = tile_of(t0) + ti
                                    v_store(Vn, tt, ps[:, par, 256:512].rearrange("p (h d) -> p h d", d=64), 2, [PB[par]], [R("Vn", tt)])
                                    yield
                                    kout = qkP[par][:, 0:256].rearrange("p (h d) -> p h d", d=64)
                                    yield from headnorm_g([(ps[:, par, 0:256].rearrange("p (h d) -> p h d", d=64), 0, 4, [PB[par]])], 4, 64,
                                                          gains[:, G_NK - GO:G_NK - GO + 64], kout, hnP[par], [R("qkbf", par)], rg1, tag=str(par))
                                    pb = transposes_to(qkP[par][:, 0:256], 2, 128, [R("qkbf", par)], 4 + par)
                                    yield
                                    self.cp("act", kTn[:, :, tt * 128:(tt + 1) * 128], pb[:, 0:256].rearrange("p (m n) -> p m n", n=128), [PB[4 + par]], [R("kTn", tt)])
                                    yield
                                run_pass1(groups_all, 1, [hT, hT2], mtmp, projA, postB)
                                P.barrier()
                            with AR.scope():
                                wq = T([128, 8, 256], BF16)
                                qT = [T([128, 2, 2, 512], BF16) for i in range(2)]
                                rw = R("wnq")
                                load_w(wq[:, :, :], w_in1[:, 672 + hh * 256:672 + (hh + 1) * 256], rw)
                                for i in range(2):
                                    self.memset("pool", qT[i][:, :, :, :], 0.0, [R("qTn", i)])

                                def tile_g(tile, hTg, rh, par=0):
                                    gi, t0, n, cnd, ti = tile
                                    pbk = 5 if par == 0 else 0
                                    tbk = 7 if par == 0 else 1
                                    qk_, rqk = qkP[par], (R("qkbf") if par == 0 else R("qkbf", 1))
                                    for c in range(8):
                                        self.mm(ps[:, pbk, 0:256], hTg[:, c, ti * 128:(ti + 1) * 128], wq[:, c, :], c == 0, c == 7, [rh, rw], [PB[pbk]])
                                    yield
                                    qout = qk_[:, 0:256].rearrange("p (h d) -> p h d", d=64)
                                    yield from headnorm_g([(ps[:, pbk, 0:256].rearrange("p (h d) -> p h d", d=64), 0, 4, [PB[pbk]])], 4, 64,
                                                          gains[:, G_NQ - GO:G_NQ - GO + 64], qout, hnP[par], [rqk], rg1, tag=str(par))
                                    qTg, rq = qT[gi % 2], R("qTn", gi % 2)
                                    pb = transposes_to(qk_[:, 0:256], 2, 128, [rqk], tbk)
                                    yield
                                    for hf in range(2):
                                        hp_ = slice(hf * 64, hf * 64 + 64)
                                        self.cp("dve", qTg[hp_, :, hf, ti * 128:(ti + 1) * 128], pb[hp_, 0:256].rearrange("p (m n) -> p m n", n=128), [PB[tbk]], [rq])
                                        yield

                                def make_jobs(gi):
                                    return gi

                                def attend(gi, grp, stepper):
                                    t0, n, cnd = grp
                                    qTg, rq = qT[gi % 2], R("qTn", gi % 2)
                                    for hl in range(4):
                                        h = hh * 4 + hl
                                        half = hl % 2
                                        m = hl // 2
                                        ob = OB_S[state["o"] % 2]
                                        state["o"] += 1
                                        oh = hl % 2
                                        for qt in range(4):
                                            i = gi * 4 + qt
                                            kts = [(16, None), (17, None)] + NA_PLAN[i]
                                            qa = qTg[:, m, half, qt * 128:(qt + 1) * 128]
                                            oc = ps[:, ob, qt * 128:(qt + 1) * 128]
                                            nkt = len(kts)
                                            chunks = [kts[c0:c0 + 4] for c0 in range(0, nkt, 4)]
                                            banks = []
                                            for chunk in chunks:
                                                b = SB_S[state["s"] % 3]
                                                state["s"] += 1
                                                banks.append(b)
                                                for bi, (j, mi) in enumerate(chunk):
                                                    sc = ps[:, b, bi * 128:(bi + 1) * 128]
                                                    self.mm(sc, kTn[:, m, j * 128:(j + 1) * 128], qa, True, mi is None, [R("kTn", j), rq], [PB[b]])
                                                    if mi is not None:
                                                        self.mm(sc, j2[:, :], XH[:, j - i + 3, hl, :], False, False, [rna, rxh], [PB[b]])
                                                        self.mm(sc, identb[:, :], msk[:, mi, :], False, True, [rna, rc], [PB[b]])
                                            done = 0
                                            for chunk, b in zip(chunks, banks):
                                                w_ = len(chunk) * 128
                                                pi = state["pt"] % 3
                                                state["pt"] += 1
                                                rp = R("PT", pi)
                                                self.act(PT[pi][:, 0:w_], psb(b, w_), AF.Exp, [PB[b]], [rp], scale=0.125)
                                                for bi, (j, mi) in enumerate(chunk):
                                                    vl, _ = v_lhsT(Vn, j, hl)
                                                    self.mm(oc, vl, PT[pi][:, bi * 128:(bi + 1) * 128], done == 0, done == nkt - 1,
                                                            [rp, R("Vn", j), R("Vn1")], [PB[ob]])
                                                    done += 1
                                                stepper()
                                        hp = slice((h % 2) * 64, (h % 2) * 64 + 64)
                                        finish_head(ob, 512, dict(ohalf=oh, dst=mixT[hp, 4 + h // 2, t0:t0 + 512], rdst=[mres(4 + h // 2, gi)]), recs)
                                run_pass2(LAT512, 1, hT, mtmp, tile_g, make_jobs, attend, rate=3, par_tiles=True)
                                P.barrier()

                    with AR.scope():
                        cos1 = T([128, 16, 32], F32)
                        sin1 = T([128, 16, 32], F32)
                        kTm = T([128, 4, NT], BF16)
                        Vm = T([128, 18, 2, 3, 64], BF16)
                        kfullP = [T([128, 4, 96], F32) for i in range(2)]
                        latbP = [T([128, 384], BF16) for i in range(2)]
                        latTP = [T([128, 3, 128], BF16) for i in range(2)]
                        rl1 = R("l1c")
                        self.dma("sp", [(cos1[:, :, :], cf_d[:, CF_COS1:CF_COS1 + 512].rearrange("p (t d) -> p t d", d=32)),
                                        (sin1[:, :, :], cf_d[:, CF_SIN1:CF_SIN1 + 512].rearrange("p (t d) -> p t d", d=32))], (), [rl1], stream="c")
                        self.memset("pool", Vm[:, :, :, 1, :], 1.0, [R("Vm1")])
                        ss1P = [T([128, 1], F32) for i in range(2)]
                        sd1P = [T([128, 1], F32) for i in range(2)]

                        def lora_norm_g(src_ps, width, gain_ap, rsrc, tbank, par):
                            sp_ = str(par)
                            sqv = hnP[par][0][:, 0:width]
                            ss1, sd1, lat_bf, latT = ss1P[par], sd1P[par], latbP[par], latTP[par]
                            self.act(sqv, src_ps, AF.Square, rsrc, [R("hn_sq" + sp_)])
                            yield
                            self.red(ss1[:, :], sqv, [R("hn_sq" + sp_)], [R("ss1", par)])
                            yield
                            self.act(sd1[:, :], ss1[:, :], AF.Ln, [R("ss1", par)], [R("sd1", par)], bias=EPS, scale=1.0 / width)
                            self.act(sd1[:, :], sd1[:, :], AF.Exp, [R("sd1", par)], [R("sd1", par)], scale=-0.5)
                            yield
                            self.stt("dve", lat_bf[:, 0:width], src_ps, sd1[:, 0:1], gain_ap, ALU.mult, ALU.mult, rsrc + [R("sd1", par), rg1], [R("lat_bf", par)])
                            yield "psum_done"
                            nb = width // 128
                            pb = transposes_to(lat_bf[:, 0:width], nb, 128, [R("lat_bf", par)], tbank)
                            yield
                            self.cp("act", latT[:, 0:nb, :], pb[:, 0:width].rearrange("p (m n) -> p m n", n=128), [PB[tbank]], [R("latT", par)])
                            yield

                        def lora_norm(src_ps, width, gain_ap, rsrc, tbank):
                            for _ in lora_norm_g(src_ps, width, gain_ap, rsrc, tbank, 0):
                                pass
                        latT = latTP[0]

                        for hh in range(2):
                            with AR.scope():
                                wkv = T([128, 8, 288], BF16)
                                hT2 = T([128, 8, 512], BF16)
                                wukv = T([128, 2, 512], BF16)
                                rw, rwu = R("wmkv"), R("wukv")
                                load_w(wkv[:, :, :], w_in1[:, 384:672], rw)
                                load_w(wukv[:, :, :], wukv_d[:, hh * 512:(hh + 1) * 512], rwu)

                                def projA(tile, hTg, rh, par):
                                    gi, t0, n, cnd, ti = tile
                                    for c in range(8):
                                        self.mm(ps[:, par, 0:288], hTg[:, c, ti * 128:(ti + 1) * 128], wkv[:, c, :], c == 0, c == 7, [rh, rw], [PB[par]])

                                def postB(tile, par):
                                    gi, t0, n, cnd, ti = tile
                                    tt = tile_of(t0) + ti
                                    kfull = kfullP[par]
                                    rkf = R("kfull", par)
                                    self.cp("act", kfull[:, :, 64:96], ps[:, par, 256:288].unsqueeze(1).to_broadcast([128, 4, 32]), [PB[par]], [rkf])
                                    yield
                                    yield from lora_norm_g(ps[:, par, 0:256], 256, gains[:, G_KVA - GO:G_KVA - GO + 256], [PB[par]], 2 + par, par)
                                    kb = 4 + par
                                    for c in range(2):
                                        self.mm(ps[:, kb, :], latTP[par][:, c, :], wukv[:, c, :], c == 0, c == 1, [R("latT", par), rwu], [PB[kb]])
                                    yield
                                    kvv = ps[:, kb, :].rearrange("p (h d) -> p h d", d=128)
                                    self.cp("act", kfull[:, :, 0:64], kvv[:, :, 0:64], [PB[kb]], [rkf])
                                    yield
                                    v_store(Vm, tt, kvv, 2, [PB[kb]], [R("Vm", tt)], d0=64)
                                    yield
                                    rope = None if cnd == 1 else (cos1[:, tt, :], sin1[:, tt, :], 64, 32, rl1)
                                    kout = qkP[par][:, 0:384].rearrange("p (h d) -> p h d", d=96)
                                    yield from headnorm_g([(kfull[:, :, :], 0, 4, [rkf])], 4, 96, gains[:, G_MK - GO:G_MK - GO + 96], kout, hnP[par],
                                                          [R("qkbf", par)], rg1, rope=rope, tag=str(par))
                                    pb = transposes_to(qkP[par][:, 0:384], 4, 96, [R("qkbf", par)], 6)
                                    yield
                                    self.cp("act", kTm[0:96, :, tt * 128:(tt + 1) * 128], pb[0:96, 0:512].rearrange("p (m n) -> p m n", n=128), [PB[6]], [R("kTm", tt)])
                                    yield
                                run_pass1(groups_all, 1, [hT, hT2], mtmp, projA, postB)
                                P.barrier()
                            with AR.scope():
                                wcq = T([128, 8, 384], BF16)
                                wuq = T([128, 3, 384], BF16)
                                qT = [T([128, 4, 512], BF16) for i in range(2)]
                                rw, rwu = R("wcq"), R("wuq")
                                load_w(wcq[:, :, :], w_in1[:, 0:384], rw)
                                load_w(wuq[:, :, :], wuq_d[:, hh * 384:(hh + 1) * 384], rwu)

                                def tile_g(tile, hTg, rh, par=0):
                                    gi, t0, n, cnd, ti = tile
                                    tt = tile_of(t0) + ti
                                    pbk, ubk, tbk = (5, 6, 7) if par == 0 else (0, 2, 1)
                                    qk_, rqk = qkP[par], (R("qkbf") if par == 0 else R("qkbf", 1))
                                    for c in range(8):
                                        self.mm(ps[:, pbk, 0:384], hTg[:, c, ti * 128:(ti + 1) * 128], wcq[:, c, :], c == 0, c == 7, [rh, rw], [PB[pbk]])
                                    yield
                                    yield from lora_norm_g(ps[:, pbk, 0:384], 384, gains[:, G_QA - GO:G_QA - GO + 384], [PB[pbk]], tbk, par)
                                    for c in range(3):
                                        self.mm(ps[:, ubk, 0:384], latTP[par][:, c, :], wuq[:, c, :], c == 0, c == 2, [R("latT", par), rwu], [PB[ubk]])
                                    yield
                                    qout = qk_[:, 0:384].rearrange("p (h d) -> p h d", d=96)
                                    yield from headnorm_g([(ps[:, ubk, 0:384].rearrange("p (h d) -> p h d", d=96), 0, 4, [PB[ubk]])], 4, 96,
                                                          gains[:, G_MQ - GO:G_MQ - GO + 96], qout, hnP[par], [rqk], rg1,
                                                          rope=(cos1[:, tt, :], sin1[:, tt, :], 64, 32, rl1), tag=str(par))
                                    qTg, rq = qT[gi % 2], R("qTm", gi % 2)
                                    pb = transposes_to(qk_[:, 0:384], 4, 96, [rqk], tbk)
                                    yield
                                    self.cp("dve", qTg[0:96, :, ti * 128:(ti + 1) * 128], pb[0:96, 0:512].rearrange("p (m n) -> p m n", n=128), [PB[tbk]], [rq])
                                    yield

                                def make_jobs(gi):
                                    t0, n, cnd = LAT512[gi]
                                    qTg, rq = qT[gi % 2], R("qTm", gi % 2)
                                    ktiles = [16, 17] + list(range(16))
                                    jobs = []
                                    for hl in range(4):
                                        h = hh * 4 + hl
                                        hp = slice((h % 2) * 64, (h % 2) * 64 + 64)
                                        keys = []
                                        for kt in ktiles:
                                            vl, oh = v_lhsT(Vm, kt, hl)
                                            keys.append((kTm[0:96, hl, kt * 128:(kt + 1) * 128], vl, [R("kTm", kt), R("Vm", kt), R("Vm1")]))
                                        jobs.append(dict(q=qTg[0:96, hl, 0:512], keys=keys, rq=[rq], ohalf=oh,
                                                         dst=mixT[hp, h // 2, t0:t0 + 512], rdst=[mres(h // 2, gi)]))
                                    return jobs

                                def attend(jobs, grp, stepper):
                                    attention(jobs, 512, 96.0 ** -0.5, PT, recs, stepper)
                                run_pass2(LAT512, 1, hT, mtmp, tile_g, make_jobs, attend, rate=2, par_tiles=True)
                                P.barrier()
                final_proj(1, LAT512)


            if stage != "load":
                layer0()
                if stage != "l0mix":
                    ffn(0, [(0, 1024, 0), (1024, 1024, 0), CTXG])
                    if stage != "l0":
                        layer1()
                        if stage != "l1mix":
                            ffn(1, [(0, 1024, 0), (1024, 1024, 0)])

            with AR.scope() as s1:
                yst = [s1.enter_context(AR.alloc([128, D], F32)) for i in range(2)]
                for tt in range(nout // 128):
                    slot = tt % 2
                    b0 = 2 * (tt % 2)
                    ry = R("yst", slot)
                    for c in range(8):
                        self.tr(ps[:, b0 + c // 4, (c % 4) * 128:(c % 4 + 1) * 128], xT[:, c, tt * 128:(tt + 1) * 128], identf[:, :],
                                xr(tt * 128, 128, [c]) + [rc], [PB[b0 + c // 4]])
                    for hh in range(2):
                        self.cp("act" if hh == 0 else "dve", yst[slot][:, hh * 512:(hh + 1) * 512], ps[:, b0 + hh, :], [PB[b0 + hh]], [ry])
                    self.dma("sp", [(y_d[tt * 128:(tt + 1) * 128, :], yst[slot][:, :])], [ry], [R("ydram", tt)], stream="out")
                self.nwait = P.emit()
                self.nops = len(P.ops)
        return nc


_CONSTS = None


def _prep_shared(inputs):
    global _CONSTS
    if _CONSTS is None:
        _CONSTS = _const_arrays()
    cb, cf = _CONSTS
    f = lambda a: np.ascontiguousarray(np.asarray(a, dtype=np.float32))
    fm = lambda v: f(v).reshape(-1, 128).T
    vecs = np.zeros((128, V_N), np.float32)
    vecs[:, 0:48] = fm(inputs["l0_ada_b"])
    vecs[:, 48:96] = fm(inputs["l1_ada_b"])
    vecs[:, V_NMIX0:V_NMIX0 + 8] = fm(inputs["l0_norm_mix"])
    vecs[:, V_NFFN0:V_NFFN0 + 8] = fm(inputs["l0_norm_ffn"])
    vecs[:, V_NMIX1:V_NMIX1 + 8] = fm(inputs["l1_norm_mix"])
    vecs[:, V_NFFN1:V_NFFN1 + 8] = fm(inputs["l1_norm_ffn"])
    vecs[:, V_PSC:V_PSC + 2] = fm(inputs["l0_pool_scale"])
    gains = np.concatenate([f(inputs[k]).reshape(-1) for k in
                            ("l0_q_gain", "l0_k_gain", "l1_mla_q_a_gain", "l1_mla_kv_a_gain", "l1_mla_q_gain", "l1_mla_k_gain",
                             "l1_na_q_gain", "l1_na_k_gain")]).reshape(1, G_N)
    w_in0 = f(inputs["l0_w_in"])
    qcols = np.concatenate([np.arange(256 + h * 64, 256 + (h + 1) * 64) for h in Q_PERM])
    w_in_q = np.ascontiguousarray(w_in0[:, qcols])
    w_in_kv = np.ascontiguousarray(np.concatenate([w_in0[:, 0:256], w_in0[:, 1024:1536]], axis=1))
    w_out0 = f(inputs["l0_w_out"])
    orow = np.concatenate([np.arange(256)] + [np.arange(256 + h * 64, 256 + (h + 1) * 64) for h in Q_PERM])
    w_out_p = np.ascontiguousarray(w_out0[orow, :])
    rpb = f(inputs["l1_na_rpb"])
    rp = np.zeros((8, 15, 127), np.float32)
    rp[:, :, 48:79] = rpb[:, :, ::-1]
    rpbR = np.concatenate([rp.reshape(-1), np.zeros(128, np.float32)])
    shared = {
        "vecs": vecs, "gains": gains, "cbf": cb, "cf32": cf, "rpbR": rpbR,
        "l0_ada_w": f(inputs["l0_ada_w"]), "l1_ada_w": f(inputs["l1_ada_w"]),
        "l0_w_in_kv": w_in_kv, "l0_w_in_q": w_in_q, "l0_pool_w": f(inputs["l0_pool_w"]),
        "l0_w_out_p": w_out_p, "l1_w_out": f(inputs["l1_w_out"]), "l1_w_in": f(inputs["l1_w_in"]),
        "l1_mla_w_uq": f(inputs["l1_mla_w_uq"]), "l1_mla_w_ukv": f(inputs["l1_mla_w_ukv"]),
    }
    for l in range(2):
        for nm in ("gate", "up", "down"):
            shared[f"l{l}_ffn_w_{nm}"] = f(inputs[f"l{l}_ffn_w_{nm}"])
    return shared


def _run(inputs, stage="full", cores=8, trace=False):
    shared = _prep_shared(inputs)
    x = np.asarray(inputs["x"], dtype=np.float32)
    c = np.asarray(inputs["c"], dtype=np.float32)
    ctx = np.asarray(inputs["ctx"], dtype=np.float32)
    c_ctx = np.asarray(inputs["c_ctx"], dtype=np.float32)
    in_maps = []
    for b in range(cores):
        cond = np.concatenate([c[b].reshape(8, 128).T, c_ctx.reshape(8, 128).T], axis=1)
        m = dict(shared)
        m["x"] = np.ascontiguousarray(x[b])
        m["ctx"] = np.ascontiguousarray(ctx[b])
        m["condT"] = np.ascontiguousarray(cond)
        in_maps.append(m)
    bld = Builder(stage)
    nc = bld.build()
    used = set()
    res = run_bass_kernel_spmd(nc, in_maps, core_ids=list(range(cores)), **({"trace": True} if trace else {}))
    return res, bld


def kernel(**inputs):
    res, _ = _run(inputs, STAGE, 8)
    return np.stack([np.asarray(r["y"], dtype=np.float32) for r in res.results], axis=0)
```

```python
import numpy as np
import ml_dtypes
import concourse.bass as bass
import concourse.mybir as mybir
from concourse.bass_utils import run_bass_kernel_spmd

F32 = mybir.dt.float32
BF16 = mybir.dt.bfloat16
ALU = mybir.AluOpType
AF = mybir.ActivationFunctionType
AX = mybir.AxisListType

D = 1024
S = 2048
CTX = 256
NT = S + CTX
FFN = 2816
NJ = FFN // 128
EPS = 1e-6
STAGE = "full"
Q_PERM = [0, 3, 1, 4, 2, 5, 6, 9, 7, 10, 8, 11]


class Res:
    __slots__ = ("name", "w", "r", "excl")

    def __init__(self, name):
        self.name = name
        self.w = None
        self.r = {}
        self.excl = False


class Op:
    __slots__ = ("eng", "fn", "deps", "idx", "signal", "semval", "dma", "dsem", "dval", "dprev", "selfsig", "bg")


class Prog:
    ENG = ("pe", "act", "dve", "pool", "sp")

    def __init__(self, nc, esems, dsems, streams):
        self.nc = nc
        self.h = {"pe": nc.tensor, "act": nc.scalar, "dve": nc.vector, "pool": nc.gpsimd, "sp": nc.sync}
        self.esem = esems
        self.dsems = dsems
        self.streams = streams
        self.rr = {k: 0 for k in streams}
        self.dtot = [0] * len(dsems)
        self.dlast = [None] * len(dsems)
        self.ops = []
        self.last = {e: None for e in self.ENG}
        self.pending = {e: [] for e in self.ENG}
        self.resd = {}

    def res(self, *key):
        r = self.resd.get(key)
        if r is None:
            r = Res(key)
            self.resd[key] = r
        return r

    def op(self, eng, fn, reads=(), writes=(), dma=0, stream="misc", extra=(), selfsig=False, bg=False):
        o = Op()
        o.eng = eng
        o.fn = fn
        o.dma = dma
        o.signal = False
        o.semval = 0
        o.selfsig = selfsig
        o.bg = bg
        o.idx = len(self.ops)
        deps = {}

        def add(d, kind):
            if d is None:
                return
            if d.dma == 0 and d.eng == eng:
                if eng == "pe" or eng == "sp":
                    return
            key = ("d", d.idx) if d.dma else ("e", d.eng)
            cur = deps.get(key)
            if cur is None or d.idx > cur.idx:
                deps[key] = d

        for r in reads:
            add(r.w, "raw")
            if r.excl:
                for rd in r.r.values():
                    if rd.eng != eng:
                        add(rd, "rar")
        for w in writes:
            add(w.w, "waw")
            for rd in w.r.values():
                add(rd, "war")
        for d in extra:
            add(d, "raw")
        for d in self.pending[eng]:
            add(d, "raw")
        self.pending[eng] = []
        for r in reads:
            key = ("d", o.idx) if dma else ("e", eng)
            r.r[key] = o
        for w in writes:
            w.w = o
            w.r = {}
        o.deps = list(deps.values())
        for d in o.deps:
            d.signal = True
        if dma:
            lst = self.streams[stream]
            si = lst[self.rr[stream] % len(lst)]
            self.rr[stream] += 1
            o.dsem = si
            o.dprev = self.dtot[si]
            self.dtot[si] += 16 * dma
            o.dval = self.dtot[si]
            self.dlast[si] = o
        self.ops.append(o)
        self.last[eng] = o
        return o

    def barrier(self):
        nc = self.nc
        extra = [self.last[e] for e in ("pe", "act", "dve", "pool") if self.last[e] is not None]
        extra += [d for d in self.dlast if d is not None and not d.bg]
        sem = self.esem["sp"]
        b = self.op("sp", lambda: nc.sync.sem_inc(sem, 1), extra=extra, selfsig=True)
        b.signal = True
        for e in ("pe", "act", "dve", "pool"):
            self.pending[e].append(b)

    def emit(self):
        cnt = {e: 0 for e in self.ENG}
        for o in self.ops:
            if o.dma == 0 and o.signal:
                cnt[o.eng] += 1
                o.semval = cnt[o.eng]
        waited = {e: {} for e in self.ENG}
        nwait = 0
        for o in self.ops:
            E = self.h[o.eng]
            wd = waited[o.eng]
            for d in o.deps:
                if d.dma:
                    key, sem, val = ("d", d.dsem), self.dsems[d.dsem], d.dval
                else:
                    key, sem, val = ("e", d.eng), self.esem[d.eng], d.semval
                if wd.get(key, 0) < val:
                    E.wait_ge(sem, val)
                    wd[key] = val
                    nwait += 1
            if o.dma:
                key = ("d", o.dsem)
                if o.dprev > 0 and wd.get(key, 0) < o.dprev:
                    E.wait_ge(self.dsems[o.dsem], o.dprev)
                    wd[key] = o.dprev
                for ins in o.fn():
                    ins.then_inc(self.dsems[o.dsem], 16)
            else:
                ins = o.fn()
                if o.signal and not o.selfsig:
                    ins.then_inc(self.esem[o.eng], 1)
        E = self.h["sp"]
        for i, t in enumerate(self.dtot):
            if t > 0 and waited["sp"].get(("d", i), 0) < t:
                E.wait_ge(self.dsems[i], t)
        return nwait


class Arena:
    def __init__(self, ap_f32, nbytes):
        self.base = ap_f32
        self.nbytes = nbytes
        self.top = 0
        self.peak = 0

    def alloc(self, shape, dt):
        esz = 2 if dt == BF16 else 4
        n = 1
        for d in shape[1:]:
            n *= d
        nb = (n * esz + 63) // 64 * 64
        off = self.top
        self.top += nb
        self.peak = max(self.peak, self.top)
        assert self.top <= self.nbytes, f"SBUF arena overflow: {self.top} > {self.nbytes}"
        self.last_off = off
        return self.view_at(off, shape, dt)

    def view_at(self, off, shape, dt):
        esz = 2 if dt == BF16 else 4
        n = 1
        for d in shape[1:]:
            n *= d
        nb = (n * esz + 63) // 64 * 64
        assert off % 4 == 0
        ap = self.base[0:shape[0], off // 4:(off + nb) // 4]
        if dt == BF16:
            ap = ap.bitcast(BF16)
        ap = ap[:, 0:n]
        if len(shape) == 3:
            ap = ap.rearrange("p (a b) -> p a b", a=shape[1])
        elif len(shape) == 4:
            ap = ap.rearrange("p (a b c) -> p a b c", a=shape[1], b=shape[2])
        elif len(shape) == 5:
            ap = ap.rearrange("p (a b c d) -> p a b c d", a=shape[1], b=shape[2], c=shape[3])
        return ap

    def scope(self):
        ar = self

        class _S:
            def __enter__(self_):
                self_.m = ar.top
                return self_

            def __exit__(self_, *a):
                ar.top = self_.m
                return False

            def enter_context(self_, x):
                return x
        return _S()


def _rope_tables(hd, n_tiles=16):
    half = hd // 2
    q = half // 2
    inv = (10000.0 ** (-np.arange(q, dtype=np.float32) * 2.0 / half)).astype(np.float32)
    tok = np.arange(S)
    row = (tok // 64).astype(np.float32)
    col = (tok % 64).astype(np.float32)
    ar = row[:, None] * inv[None, :]
    ac = col[:, None] * inv[None, :]
    cr, sr, cc, sc = np.cos(ar), np.sin(ar), np.cos(ac), np.sin(ac)
    cos = np.concatenate([cr, cr, cc, cc], axis=1).astype(np.float32)
    sin = np.concatenate([-sr, sr, -sc, sc], axis=1).astype(np.float32)
    cos = cos.reshape(n_tiles, 128, hd).transpose(1, 0, 2)
    sin = sin.reshape(n_tiles, 128, hd).transpose(1, 0, 2)
    return np.ascontiguousarray(cos), np.ascontiguousarray(sin)


def _band_tables():
    T = 384
    out = np.zeros((128, 20, 128), np.float32)
    for g, w in enumerate((2, 4, 8, 16)):
        M = np.zeros((T, T), np.float32)
        for t in range(T):
            lo = min(max(t - w // 2, 0), T)
            hi = min(max(t - w // 2 + w, 0), T)
            M[lo:hi, t] += 1.0 / (hi - lo)
            M[t, t] -= 1.0
        out[:, g * 5 + 0, :] = M[128:256, 128:256]
        out[:, g * 5 + 1, :] = M[0:128, 128:256]
        out[:, g * 5 + 2, :] = M[256:384, 128:256]
        out[:, g * 5 + 3, :] = M[0:128, 0:128]
        out[:, g * 5 + 4, :] = M[256:384, 256:384]
        assert np.array_equal(M[128:256, 0:128], out[:, g * 5 + 2, :])
        assert np.array_equal(M[128:256, 256:384], out[:, g * 5 + 1, :])
    return out


NA_NEG = -240000.0


def _na_tables():
    rows = 32
    masks = []
    keyof = {}
    plan = []
    qc = np.arange(64)
    c0 = np.clip(qc - 8, 0, 48)
    kc = np.arange(64)
    colvalid = (kc[:, None] >= c0[None, :]) & (kc[:, None] < c0[None, :] + 16)
    for i in range(16):
        r0 = [int(np.clip(2 * i + b - 4, 0, rows - 8)) for b in (0, 1)]
        jlo = min(r0) // 2
        jhi = (max(r0) + 7) // 2
        lst = []
        for j in range(jlo, jhi + 1):
            m = np.zeros((128, 128), np.float32)
            for a in (0, 1):
                for b in (0, 1):
                    kr = 2 * j + a
                    ok = (kr >= r0[b]) and (kr < r0[b] + 8)
                    blk = colvalid if ok else np.zeros((64, 64), bool)
                    m[a * 64:(a + 1) * 64, b * 64:(b + 1) * 64] = np.where(blk, 0.0, NA_NEG)
            if (m == NA_NEG).all():
                continue
            kb = m.tobytes()
            if kb not in keyof:
                keyof[kb] = len(masks)
                masks.append(m)
            lst.append((j, keyof[kb]))
        plan.append(lst)
    return plan, np.stack(masks, axis=1)


NA_PLAN, NA_MASKS = _na_tables()
NMASK = NA_MASKS.shape[1]


def _j2x8():
    m = np.zeros((128, 128), np.float32)
    for a in (0, 1):
        for k in range(64):
            m[a * 64 + k, a * 64 + 63 - k] = 8.0
    return m


CB_IDENT, CB_ONES, CB_J2, CB_BAND, CB_MASK = 0, 128, 256, 384, 384 + 2560
CB_N = CB_MASK + NMASK * 128
CF_IDENT, CF_COS0, CF_SIN0, CF_COS1, CF_SIN1 = 0, 128, 128 + 1024, 128 + 2048, 128 + 2048 + 512
CF_N = CF_SIN1 + 512
G_Q0, G_K0, G_QA, G_KVA, G_MQ, G_MK, G_NQ, G_NK, G_N = 0, 64, 128, 512, 768, 864, 960, 1024, 1088
V_ADAB, V_NMIX0, V_NFFN0, V_NMIX1, V_NFFN1, V_PSC, V_N = 0, 96, 104, 112, 120, 128, 130


def _const_arrays():
    cb = np.zeros((128, CB_N), np.float32)
    cb[:, CB_IDENT:CB_IDENT + 128] = np.eye(128, dtype=np.float32)
    cb[:, CB_ONES:CB_ONES + 128] = 1.0 / 1024.0
    cb[:, CB_J2:CB_J2 + 128] = _j2x8()
    cb[:, CB_BAND:CB_BAND + 2560] = _band_tables().reshape(128, 2560)
    cb[:, CB_MASK:] = NA_MASKS.reshape(128, NMASK * 128)
    cf = np.zeros((128, CF_N), np.float32)
    cf[:, CF_IDENT:CF_IDENT + 128] = np.eye(128, dtype=np.float32)
    c0, s0 = _rope_tables(64)
    c1, s1 = _rope_tables(32)
    cf[:, CF_COS0:CF_COS0 + 1024] = c0.reshape(128, 1024)
    cf[:, CF_SIN0:CF_SIN0 + 1024] = s0.reshape(128, 1024)
    cf[:, CF_COS1:CF_COS1 + 512] = c1.reshape(128, 512)
    cf[:, CF_SIN1:CF_SIN1 + 512] = s1.reshape(128, 512)
    return cb.astype(ml_dtypes.bfloat16), cf


class Builder:
    def __init__(self, stage):
        self.stage = stage
        self.nc = bass.Bass("TRN2", target_bir_lowering=False)

    def mm(self, out, lhsT, rhs, start, stop, r, w):
        nc = self.nc
        return self.P.op("pe", lambda: nc.tensor.matmul(out, lhsT=lhsT, rhs=rhs, start=start, stop=stop), r, w)

    def tr(self, out, in_, ident, r, w):
        nc = self.nc
        return self.P.op("pe", lambda: nc.tensor.transpose(out, in_, ident), r, w)

    def act(self, out, in_, func, r, w, bias=None, scale=None):
        nc = self.nc
        kw = {}
        if bias is not None:
            kw["bias"] = bias
        if scale is not None:
            kw["scale"] = scale
        return self.P.op("act", lambda: nc.scalar.activation(out=out, in_=in_, func=func, **kw), r, w)

    def _ve(self, eng):
        return {"dve": self.nc.vector, "pool": self.nc.gpsimd}[eng]

    def cp(self, eng, out, in_, r, w):
        nc = self.nc
        if eng == "act":
            return self.P.op("act", lambda: nc.scalar.copy(out=out, in_=in_), r, w)
        E = self._ve(eng)
        return self.P.op(eng, lambda: E.tensor_copy(out=out, in_=in_), r, w)

    def tt(self, eng, out, in0, in1, op, r, w):
        E = self._ve(eng)
        return self.P.op(eng, lambda: E.tensor_tensor(out=out, in0=in0, in1=in1, op=op), r, w)

    def stt(self, eng, out, in0, scalar, in1, op0, op1, r, w):
        E = self._ve(eng)
        return self.P.op(eng, lambda: E.scalar_tensor_tensor(out=out, in0=in0, scalar=scalar, in1=in1, op0=op0, op1=op1), r, w)

    def ts(self, eng, out, in0, s1, s2, op0, op1, r, w):
        E = self._ve(eng)
        if s2 is None:
            return self.P.op(eng, lambda: E.tensor_scalar(out=out, in0=in0, scalar1=s1, scalar2=None, op0=op0), r, w)
        return self.P.op(eng, lambda: E.tensor_scalar(out=out, in0=in0, scalar1=s1, scalar2=s2, op0=op0, op1=op1), r, w)

    def red(self, out, in_, r, w):
        nc = self.nc
        return self.P.op("dve", lambda: nc.vector.reduce_sum(out=out, in_=in_, axis=AX.X), r, w)

    def rcp(self, out, in_, r, w):
        nc = self.nc
        return self.P.op("dve", lambda: nc.vector.reciprocal(out=out, in_=in_), r, w)

    def memset(self, eng, ap, val, w):
        E = self._ve(eng)
        return self.P.op(eng, lambda: E.memset(ap, val), (), w)

    def dma(self, eng, pairs, r, w, stream="misc", bg=False):
        E = self.P.h[eng]
        return self.P.op(eng, lambda: [E.dma_start(out=o, in_=i) for (o, i) in pairs], r, w, dma=len(pairs), stream=stream, bg=bg)

    def build(self):
        nc = self.nc
        stage = self.stage
        dbg = stage != "full"
        di = {}

        def inp(name, shape, dt=F32):
            di[name] = nc.dram_tensor(name, list(shape), dt, kind="ExternalInput").ap()
            return di[name]

        x_d = inp("x", [S, D])
        ctx_d = inp("ctx", [CTX, D])
        cond_d = inp("condT", [128, 16])
        vecs_d = inp("vecs", [128, V_N])
        gains_d = inp("gains", [1, G_N])
        cb_d = inp("cbf", [128, CB_N], BF16)
        cf_d = inp("cf32", [128, CF_N])
        rpb_d = inp("rpbR", [8 * 15 * 127 + 128])
        adaw_d = [inp("l0_ada_w", [D, 6 * D]), inp("l1_ada_w", [D, 6 * D])]
        w_in_kv0 = inp("l0_w_in_kv", [D, 768])
        w_in_q0 = inp("l0_w_in_q", [D, 768])
        poolw_d = inp("l0_pool_w", [4, 64, 64])
        wout_d = [inp("l0_w_out_p", [D, D]), inp("l1_w_out", [D, D])]
        w_in1 = inp("l1_w_in", [D, 2208])
        wuq_d = inp("l1_mla_w_uq", [384, 768])
        wukv_d = inp("l1_mla_w_ukv", [256, 1024])
        wg_d = [inp("l0_ffn_w_gate", [D, FFN]), inp("l1_ffn_w_gate", [D, FFN])]
        wu_d = [inp("l0_ffn_w_up", [D, FFN]), inp("l1_ffn_w_up", [D, FFN])]
        wd_d = [inp("l0_ffn_w_down", [FFN, D]), inp("l1_ffn_w_down", [FFN, D])]
        nout = NT if dbg else S
        y_d = nc.dram_tensor("y", [nout, D], F32, kind="ExternalOutput").ap()
        wgs = [nc.dram_tensor(f"wgs{l}", [NJ, 128, 8, 128], BF16).ap() for l in range(2)]
        wus = [nc.dram_tensor(f"wus{l}", [NJ, 128, 8, 128], BF16).ap() for l in range(2)]
        wds = [nc.dram_tensor(f"wds{l}", [8, 128, NJ, 128], BF16).ap() for l in range(2)]

        from contextlib import ExitStack
        with ExitStack() as es:
            def sb(name, shape, dt):
                return AR.alloc(list(shape), dt)

            esems = {e: es.enter_context(nc.semaphore("s_" + e)) for e in Prog.ENG}
            NDS = 30
            dsems = [es.enter_context(nc.semaphore(f"d{i}")) for i in range(NDS)]
            streams = {"misc": [0, 1, 2, 3], "w": [4, 5, 6, 7], "gu": [8, 9, 10, 11], "wd": [12, 13, 14],
                       "xin": [15, 16], "out": [17, 18], "bg": [19, 20, 21, 22, 23, 24], "ada": [25, 26], "c": [27, 28, 29]}
            P = Prog(nc, esems, dsems, streams)
            self.P = P
            R = P.res

            ARENA_BYTES = 207 * 1024
            AR = Arena(es.enter_context(nc.sbuf_tensor("arena", [128, ARENA_BYTES // 4], F32)), ARENA_BYTES)
            self.AR = AR
            ps = es.enter_context(nc.psum_tensor("ps", [128, 8, 512], F32))
            PB = [R("psb", b) for b in range(8)]
            for r_ in PB:
                r_.excl = True

            def psb(b, n=512):
                return ps[:, b, 0:n]

            def psb16(b):
                return ps[:, b, :].bitcast(BF16)

            xT = sb("xT", [128, 8, NT], F32)
            mixT = sb("mixT", [128, 8, NT], BF16)
            self.mix_off = AR.last_off
            vecs = sb("vecs", [128, V_N], F32)
            mvec = sb("mvec", [128, 96, 2], F32)
            avec = sb("avec", [128, 2, 2, 2, 8], F32)
            identb = sb("identb", [128, 128], BF16)
            onesm = sb("onesm", [128, 128], BF16)
            identf = sb("identf", [128, 128], F32)

            def xres(dc, blk):
                return R("xT", dc, blk)

            def blks_of(t0, n):
                return sorted(set(min(t // 512, 4) for t in (t0, t0 + n - 1)))

            def xr(t0, n, dcs=range(8)):
                return [xres(dc, b) for dc in dcs for b in blks_of(t0, n)]

            def mres(c, blk):
                return R("mixT", c, blk)

            bgq = []

            def convert_ffn(l):
                wgv = wg_d[l].rearrange("(c p) (j n) -> j p c n", p=128, n=128)
                wuv = wu_d[l].rearrange("(c p) (j n) -> j p c n", p=128, n=128)
                wdv = wd_d[l].rearrange("(j p) (c n) -> c p j n", p=128, n=128)

                def mk(dst, src, res):
                    return lambda: self.dma("pool", [(dst, src)], (), [res], stream="bg", bg=True)
                for j in range(NJ):
                    bgq.append(mk(wgs[l][j], wgv[j], R("wgs", l, j)))
                    bgq.append(mk(wus[l][j], wuv[j], R("wus", l, j)))
                for c in range(8):
                    bgq.append(mk(wds[l][c], wdv[c], R("wds", l, c)))

            def bg_step(k):
                for _ in range(k):
                    if bgq:
                        bgq.pop(0)()

            rc = R("consts")
            self.dma("sp", [(vecs[:, :], vecs_d), (identb[:, :], cb_d[:, CB_IDENT:CB_IDENT + 128]),
                            (onesm[:, :], cb_d[:, CB_ONES:CB_ONES + 128]), (identf[:, :], cf_d[:, CF_IDENT:CF_IDENT + 128]),
                            ], (), [rc], stream="c")

            with AR.scope() as s0:
                condT = s0.enter_context(AR.alloc([128, 16], F32))
                scT = s0.enter_context(AR.alloc([128, 8, 2], BF16))
                adaw = [s0.enter_context(AR.alloc([128, 8, 1024], BF16)) for i in range(2)]
                rcond = R("condT")
                self.dma("sp", [(condT[:, :], cond_d)], (), [rcond], stream="c")
                self.act(scT[:, :, :], condT[:, :].rearrange("p (c k) -> p k c", c=2), AF.Silu, [rcond], [R("scT")])
                k = 0
                for l in range(2):
                    src = adaw_d[l].rearrange("(c p) n -> p c n", p=128)
                    for v in range(6):
                        slot = k % 2
                        k += 1
                        ra = R("adaw", slot)
                        self.dma("pool", [(adaw[slot][:, :, :], src[:, :, v * 1024:(v + 1) * 1024])], (), [ra], stream="ada")
                        for m in range(8):
                            col = (l * 48 + v * 8 + m) * 2
                            for c in range(8):
                                self.mm(ps[:, 7, col:col + 2], adaw[slot][:, c, m * 128:(m + 1) * 128], scT[:, c, :],
                                        c == 0, c == 7, [ra, R("scT")], [PB[7]])
                rmv = R("mvec")
                self.tt("dve", mvec[:, :, :], ps[:, 7, 0:192].rearrange("p (a b) -> p a b", b=2),
                        vecs[:, V_ADAB:V_ADAB + 96].unsqueeze(2).to_broadcast([128, 96, 2]), ALU.add, [PB[7], rc], [rmv])
                for l in range(2):
                    for f in range(2):
                        v = 1 + 3 * f
                        ng = vecs[:, V_NMIX0 + 16 * l + 8 * f: V_NMIX0 + 16 * l + 8 * f + 8]
                        for cnd in range(2):
                            self.stt("dve", avec[:, l, f, cnd, :], mvec[:, l * 48 + v * 8: l * 48 + v * 8 + 8, cnd], 1.0, ng,
                                     ALU.add, ALU.mult, [rmv, rc], [R("avec")])

                def MV(l, v, cnd, c):
                    return mvec[:, l * 48 + v * 8 + c, cnd:cnd + 1]

                def AV(l, f, cnd, c):
                    return avec[:, l, f, cnd, c:c + 1]
                self.MV, self.AV = MV, AV
                RMOD = [rmv, R("avec")]

                xin = [s0.enter_context(AR.alloc([128, D], F32)) for i in range(2)]
                for tt in range(18):
                    slot = tt % 2
                    rx = R("xin", slot)
                    src = x_d[tt * 128:(tt + 1) * 128, :] if tt < 16 else ctx_d[(tt - 16) * 128:(tt - 15) * 128, :]
                    self.dma("sp", [(xin[slot][:, :], src)], (), [rx], stream="xin")
                    b0 = 2 * (tt % 2)
                    for c in range(8):
                        self.tr(ps[:, b0 + c // 4, (c % 4) * 128:(c % 4 + 1) * 128], xin[slot][:, c * 128:(c + 1) * 128], identf[:, :],
                                [rx, rc], [PB[b0 + c // 4]])
                    dst = xT[:, :, tt * 128:(tt + 1) * 128]
                    for hh in range(2):
                        self.cp("act" if hh == 0 else "dve", dst[:, hh * 4:(hh + 1) * 4, :],
                                ps[:, b0 + hh, :].rearrange("p (c n) -> p c n", n=128), [PB[b0 + hh]], xr(tt * 128, 128, range(hh * 4, hh * 4 + 4)))
                P.barrier()
            import os as _os
            if stage != "load" and not _os.environ.get("K_SKIP_CONVERT"):
                convert_ffn(0)

            MIX_OFF = self.mix_off

            def modulate_g(t0, n, l, f, cnd, hT, rh, tmp, on_dve=False, sq8=None):
                sqb, sdt, tm = tmp
                rsq = [R("m_sq", i) for i in range(2)]
                if sq8 is not None:
                    for c in range(8):
                        self.act(sq8[c][:, 0:n], xT[:, c, t0:t0 + n], AF.Square, xr(t0, n, [c]), [R("m_sq8", c)])
                        yield
                    for c in range(8):
                        self.mm(psb(7, n), onesm[:, :], sq8[c][:, 0:n], c == 0, c == 7, [R("m_sq8", c), rc], [PB[7]])
                        if c % 2 == 1:
                            yield
                for c in range(8 if sq8 is None else 0):
                    i = c % 2
                    if on_dve:
                        self.tt("dve", sqb[i][:, 0:n], xT[:, c, t0:t0 + n], xT[:, c, t0:t0 + n], ALU.mult, xr(t0, n, [c]), [rsq[i]])
                    else:
                        self.act(sqb[i][:, 0:n], xT[:, c, t0:t0 + n], AF.Square, xr(t0, n, [c]), [rsq[i]])
                    self.mm(psb(7, n), onesm[:, :], sqb[i][:, 0:n], c == 0, c == 7, [rsq[i], rc], [PB[7]])
                    yield
                self.act(sdt[:, 0:n], psb(7, n), AF.Ln, [PB[7]], [R("m_sd")], bias=EPS, scale=1.0)
                yield
                self.act(sdt[:, 0:n], sdt[:, 0:n], AF.Exp, [R("m_sd")], [R("m_sd")], scale=-0.5)
                yield
                shift_v = 0 if f == 0 else 3
                for c in range(8):
                    i = c % 2
                    rt = R("m_tm", i)
                    self.stt("dve", tm[i][:, 0:n], xT[:, c, t0:t0 + n], self.AV(l, f, cnd, c), sdt[:, 0:n], ALU.mult, ALU.mult,
                             xr(t0, n, [c]) + [R("m_sd")] + RMOD, [rt])
                    yield
                    if on_dve:
                        self.ts("dve", hT[:, c, 0:n], tm[i][:, 0:n], self.MV(l, shift_v, cnd, c), None, ALU.add, None, [rt] + RMOD, [rh])
                    else:
                        self.act(hT[:, c, 0:n], tm[i][:, 0:n], AF.Identity, [rt] + RMOD, [rh], bias=self.MV(l, shift_v, cnd, c))
                    yield

            def modulate(*a):
                for _ in modulate_g(*a):
                    pass

            def mod_tmp():
                return ([AR.alloc([128, 512], BF16) for i in range(2)], AR.alloc([128, 512], F32),
                        [AR.alloc([128, 512], F32) for i in range(2)])

            def hn_alloc(w):
                t2_ = AR.alloc([128, w], F32)
                sq_ = AR.view_at(AR.last_off, [128, w], BF16)
                return (sq_, AR.alloc([128, w], F32), t2_, AR.alloc([128, 16], F32), AR.alloc([128, 16], F32))

            def headnorm_g(segs, H, Dh, gain_ap, out_bf, tmp, rout, rg, rope=None, tag=""):
                sqt, kg, t2, ss, sd = tmp
                rsq, rkg, rt2, rss, rsd = R("hn_sq" + tag), R("hn_kg" + tag), R("hn_sq" + tag), R("hn_ss" + tag), R("hn_sd" + tag)
                kgv = kg[:, 0:H * Dh].rearrange("p (h d) -> p h d", d=Dh)
                sqv = sqt[:, 0:H * Dh].rearrange("p (h d) -> p h d", d=Dh)
                for (src, h0, h1, rsrc) in segs:
                    self.act(sqv[:, h0:h1, :], src, AF.Square, rsrc, [rsq])
                    yield
                    self.tt("dve", kgv[:, h0:h1, :], src, gain_ap.unsqueeze(1).to_broadcast([128, h1 - h0, Dh]), ALU.mult, rsrc + [rg], [rkg])
                    yield
                yield "psum_done"
                self.red(ss[:, 0:H], sqv, [rsq], [rss])
                yield
                self.act(sd[:, 0:H], ss[:, 0:H], AF.Ln, [rss], [rsd], bias=EPS, scale=1.0 / Dh)
                self.act(sd[:, 0:H], sd[:, 0:H], AF.Exp, [rsd], [rsd], scale=-0.5)
                yield
                rsb = sd[:, 0:H].unsqueeze(2).to_broadcast([128, H, Dh])
                if rope is None:
                    self.tt("dve", out_bf, kgv, rsb, ALU.mult, [rkg, rsd], rout)
                    yield
                    return
                cos, sin, r0, Dr, rtab = rope
                q4 = Dr // 4
                self.tt("dve", kgv, kgv, rsb, ALU.mult, [rkg, rsd], [rkg])
                yield
                knr = kgv[:, :, r0:r0 + Dr].rearrange("p h (a b c) -> p h a b c", a=2, b=2)
                t2v = t2[:, 0:H * Dr].rearrange("p (h a b c) -> p h a b c", a=2, b=2, c=q4)
                sinv = sin.rearrange("p (a b c) -> p a b c", a=2, b=2)
                cosv = cos.unsqueeze(1).to_broadcast([128, H, Dr])
                for b in range(2):
                    self.tt("dve", t2v[:, :, :, b, :], knr[:, :, :, 1 - b, :], sinv[:, :, b, :].unsqueeze(1).to_broadcast([128, H, 2, q4]),
                            ALU.mult, [rkg, rtab], [rt2])
                    yield
                if r0 > 0:
                    self.cp("act", out_bf[:, :, 0:r0], kgv[:, :, 0:r0], [rkg], rout)
                self.tt("dve", kgv[:, :, r0:r0 + Dr], kgv[:, :, r0:r0 + Dr], cosv, ALU.mult, [rkg, rtab], [rkg])
                yield
                self.tt("dve", out_bf[:, :, r0:r0 + Dr], kgv[:, :, r0:r0 + Dr], t2[:, 0:H * Dr].rearrange("p (h d) -> p h d", d=Dr),
                        ALU.add, [rkg, rt2], rout)
                yield

            def headnorm(*a, **k):
                k.setdefault("tag", "0")
                for _ in headnorm_g(*a, **k):
                    pass

            SB_S = [0, 1, 2]
            OB_S = [3, 4]
            state = {"s": 0, "o": 0, "pt": 0}

            def v_lhsT(V, kt, hl):
                pr_, odd = hl // 2, hl % 2
                return V[:, kt, pr_, odd:odd + 2, :].rearrange("p a b -> p (a b)"), odd

            def v_store(V, tt, src_ps, npair, rsrc, rdst, d0=0, dstep=64):
                sv = src_ps.rearrange("p (a b) d -> p a b d", b=2)
                for odd in range(2):
                    self.cp("act", V[:, tt, :, 2 * odd, :], sv[:, :, odd, d0:d0 + 64], rsrc, rdst)

            def attention(jobs, n, scale, PT, recs, stepper=None):
                for job in jobs:
                    keys = job["keys"]
                    nk = len(keys)
                    ob = OB_S[state["o"] % 2]
                    state["o"] += 1
                    sl = []

                    def emit_s(i):
                        b = SB_S[state["s"] % 3]
                        state["s"] += 1
                        kl, vl, rkv = keys[i]
                        self.mm(psb(b, n), kl, job["q"], True, True, rkv + job["rq"], [PB[b]])
                        sl.append(b)
                    emit_s(0)
                    if nk > 1:
                        emit_s(1)
                    for i in range(nk):
                        pi = state["pt"] % 3
                        state["pt"] += 1
                        rp = R("PT", pi)
                        self.act(PT[pi][:, 0:n], psb(sl[i], n), AF.Exp, [PB[sl[i]]], [rp], scale=scale)
                        if i + 2 < nk:
                            emit_s(i + 2)
                        self.mm(psb(ob, n), keys[i][1], PT[pi][:, 0:n], i == 0, i == nk - 1, [rp] + keys[i][2], [PB[ob]])
                        if stepper is not None:
                            stepper()
                    finish_head(ob, n, job, recs)

            def finish_head(ob, n, job, recs):
                oh = job["ohalf"]
                po = slice(oh * 64, oh * 64 + 64)
                psum_ = slice((1 - oh) * 64, (1 - oh) * 64 + 64)
                rr = R("recs")
                self.act(recs[psum_, 0:n], ps[psum_, ob, 0:n], AF.Ln, [PB[ob]], [rr])
                self.act(recs[psum_, 0:n], recs[psum_, 0:n], AF.Exp, [rr], [rr], scale=-1.0)
                self.tt("dve", job["dst"], ps[po, ob, 0:n], recs[psum_, 0:n], ALU.mult, [PB[ob], rr], job["rdst"])

            def transposes_to(src_bf, nblk, width, rsrc, bank):
                pb = psb16(bank)
                for i in range(nblk):
                    self.tr(pb[0:width, i * 128:(i + 1) * 128], src_bf[:, i * width:(i + 1) * width], identb[:, :], rsrc + [rc], [PB[bank]])
                return pb

            def load_w(dst3, src2, rw, eng="pool"):
                self.dma(eng, [(dst3, src2.rearrange("(c p) n -> p c n", p=128))], (), [rw], stream="w")

            def final_proj(l, groups):
                with AR.scope():
                    wo = AR.alloc([128, 8, D], BF16)
                    rwo = R("wo")
                    load_w(wo[:, :, :], wout_d[l], rwo)
                    k = 0
                    for (t0, n, cnd) in groups:
                        for dc in range(8):
                            b = k % 8
                            k += 1
                            for mc in range(8):
                                self.mm(psb(b, n), wo[:, mc, dc * 128:(dc + 1) * 128], mixT[:, mc, t0:t0 + n], mc == 0, mc == 7,
                                        [rwo] + [mres(mc, bb) for bb in blks_of(t0, n)], [PB[b]])
                            self.stt("dve", xT[:, dc, t0:t0 + n], psb(b, n), self.MV(l, 2, cnd, dc), xT[:, dc, t0:t0 + n], ALU.mult, ALU.add,
                                     [PB[b]] + RMOD + xr(t0, n, [dc]), xr(t0, n, [dc]))
                    P.barrier()

            def ffn(l, groups):
                with AR.scope():
                    hT = AR.view_at(MIX_OFF, [128, 8, 1024], BF16)
                    gu = [AR.view_at(MIX_OFF + 16384 + i * 4096, [128, 2, 8, 128], BF16) for i in range(4)]
                    actT = AR.alloc([128, NJ, 1024], BF16)
                    wdr = [AR.alloc([128, NJ, 128], BF16) for i in range(3)]
                    sg = [AR.alloc([128, 1024], F32) for i in range(2)]
                    tmp = mod_tmp()
                    kg = 0
                    kd = 0
                    if l == 0:
                        bg_step(len(bgq))
                        if stage not in ("l0mix", "l0"):
                            convert_ffn(1)
                    else:
                        bg_step(len(bgq))
                    sq8 = [AR.alloc([128, 512], BF16) for i in range(8)]
                    for gidx, (t0, n, cnd) in enumerate(groups):
                        rh = R("f_hT")
                        if gidx == 0:
                            for h0 in range(0, n, 512):
                                hn = min(512, n - h0)
                                modulate(t0 + h0, hn, l, 1, cnd, hT[:, :, h0:h0 + hn], rh, tmp)
                        nxt = groups[gidx + 1] if gidx + 1 < len(groups) else None

                        def modnext_g(nxt=nxt, rh=rh):
                            t0n, nn, cndn = nxt
                            for h0 in range(0, nn, 512):
                                hn_ = min(512, nn - h0)
                                yield from modulate_g(t0n + h0, hn_, l, 1, cndn, hT[:, :, h0:h0 + hn_], rh, tmp, sq8=sq8)
                        mgen = modnext_g() if nxt is not None else None
                        nh = (n + 511) // 512
                        CUT = _os.environ.get("K_CUT", "")
                        if CUT == "m":
                            return
                        for j in range(NJ):
                            if CUT == "A1" and j == 1:
                                return
                            if CUT == "A3" and j == 3:
                                return
                            bg_step(1)
                            slot = kg % 4
                            rg = R("f_gu", slot)
                            self.dma("sp", [(gu[slot][:, 0, :, :], wgs[l][j]), (gu[slot][:, 1, :, :], wus[l][j])],
                                     [R("wgs", l, j), R("wus", l, j)], [rg], stream="gu")
                            set_ = (kg % 2) * 4
                            kg += 1
                            for gi in range(2):
                                for hh in range(nh):
                                    hn = min(512, n - hh * 512)
                                    b = set_ + gi * 2 + hh
                                    for c in range(8):
                                        self.mm(psb(b, hn), gu[slot][:, gi, c, :], hT[:, c, hh * 512:hh * 512 + hn], c == 0, c == 7, [rg, rh], [PB[b]])
                            si = j % 2
                            rs_ = R("f_sg", si)
                            for hh in range(nh):
                                hn = min(512, n - hh * 512)
                                self.act(sg[si][:, hh * 512:hh * 512 + hn], psb(set_ + hh, hn), AF.Silu, [PB[set_ + hh]], [rs_])
                                self.tt("dve", actT[:, j, hh * 512:hh * 512 + hn], psb(set_ + 2 + hh, hn), sg[si][:, hh * 512:hh * 512 + hn], ALU.mult,
                                        [PB[set_ + 2 + hh], rs_], [R("f_actT", j)])
                        if CUT == "A":
                            return
                        for dc in range(8):
                            if CUT == "B1" and dc == 1:
                                return
                            slot = kd % 3
                            rw = R("f_wd", slot)
                            self.dma("sp", [(wdr[slot][:, :, :], wds[l][dc])], [R("wds", l, dc)], [rw], stream="wd")
                            set_ = (kd % 3) * 2
                            kd += 1
                            for hh in range(nh):
                                hn = min(512, n - hh * 512)
                                b = set_ + hh
                                for j in range(NJ):
                                    self.mm(psb(b, hn), wdr[slot][:, j, :], actT[:, j, hh * 512:hh * 512 + hn], j == 0, j == NJ - 1,
                                            [rw, R("f_actT", j)], [PB[b]])
                                    if mgen is not None and j % 6 == 5:
                                        try:
                                            next(mgen)
                                        except StopIteration:
                                            pass
                                tt0 = t0 + hh * 512
                                self.stt("dve", xT[:, dc, tt0:tt0 + hn], psb(b, hn), self.MV(l, 5, cnd, dc), xT[:, dc, tt0:tt0 + hn], ALU.mult, ALU.add,
                                         [PB[b]] + RMOD + xr(tt0, hn, [dc]), xr(tt0, hn, [dc]))
                        if mgen is not None:
                            for _ in mgen:
                                pass
                    P.barrier()

            LAT512 = [(g * 512, 512, 0) for g in range(4)]
            CTXG = (S, CTX, 1)

            def tile_of(t):
                return t // 128

            def tiles_of(groups):
                return [(gi, t0, n, cnd, ti) for gi, (t0, n, cnd) in enumerate(groups) for ti in range(n // 128)]

            def run_pass1(groups, l, hTs, mtmp, projA, postB):
                tl = tiles_of(groups)
                first_of = {}
                for k_, t_ in enumerate(tl):
                    first_of.setdefault(t_[0], k_)
                modgen = {}

                def start_mod(gi):
                    if gi < len(groups) and gi not in modgen:
                        t0, n, cnd = groups[gi]
                        bg_step(4)
                        modgen[gi] = modulate_g(t0, n, l, 0, cnd, hTs[gi % len(hTs)], R("hT", gi % len(hTs)), mtmp)

                def finish_mod(gi):
                    start_mod(gi)
                    for _ in modgen[gi]:
                        pass

                def A(k):
                    gi, t0, n, cnd, ti = tl[k]
                    if ti == 0:
                        finish_mod(gi)
                    projA(tl[k], hTs[gi % len(hTs)], R("hT", gi % len(hTs)), k % 2)
                active = []
                flags = {}

                def adv(g_):
                    try:
                        v = next(g_)
                        if v == "psum_done":
                            flags[id(g_)] = True
                    except StopIteration:
                        flags[id(g_)] = True
                        if g_ in active:
                            active.remove(g_)

                def adv_mod():
                    for mg in modgen.values():
                        try:
                            next(mg)
                        except StopIteration:
                            pass

                def step_all(until_len):
                    while len(active) > until_len:
                        for g_ in list(active):
                            adv(g_)
                        adv_mod()
                A(0)
                if len(tl) > 1:
                    A(1)
                for k in range(len(tl)):
                    gi, t0, n, cnd, ti = tl[k]
                    if ti == 0:
                        start_mod(gi + 1)
                    gk = postB(tl[k], k % 2)
                    active.append(gk)
                    step_all(1)
                    while not flags.get(id(gk), False):
                        adv(gk)
                        adv_mod()
                    if k + 2 < len(tl):
                        A(k + 2)
                step_all(0)

            def run_pass2(groups, l, hT, mtmp, tile_g, make_jobs, attend, rate=1):
                rh = R("hT", 0)

                def qproc_g(gi):
                    t0, n, cnd = groups[gi]
                    bg_step(4)
                    yield from modulate_g(t0, n, l, 0, cnd, hT, rh, mtmp, on_dve=True)
                    for ti in range(n // 128):
                        yield from tile_g((gi, t0, n, cnd, ti), hT, rh)
                for _ in qproc_g(0):
                    pass
                for gi in range(len(groups)):
                    gen = qproc_g(gi + 1) if gi + 1 < len(groups) else None

                    def stepper(gen=gen):
                        if gen is None:
                            return
                        for _ in range(rate):
                            try:
                                next(gen)
                            except StopIteration:
                                return
                    attend(make_jobs(gi), groups[gi], stepper)
                    if gen is not None:
                        for _ in gen:
                            pass

            def with_hooks(njobs, hooks):
                pos = [(h + 1) * njobs // (len(hooks) + 1) for h in range(len(hooks))]
                st = {"h": 0}

                def before(j):
                    while st["h"] < len(hooks) and pos[st["h"]] <= j:
                        hooks[st["h"]]()
                        st["h"] += 1

                def flush():
                    while st["h"] < len(hooks):
                        hooks[st["h"]]()
                        st["h"] += 1
                return before, flush

            def attention_h(jobs, n, scale, PT, recs, hooks):
                before, flush = with_hooks(len(jobs), hooks)
                for j, job in enumerate(jobs):
                    before(j)
                    attention([job], n, scale, PT, recs)
                flush()

            def layer0():
                with AR.scope():
                    T = AR.alloc
                    gains = T([128, 128], F32)
                    cos0 = T([128, 16, 64], F32)
                    sin0 = T([128, 16, 64], F32)
                    kT = T([128, 2, NT], BF16)
                    Vp = T([128, 18, 2, 3, 64], BF16)
                    hT = T([128, 8, 512], BF16)
                    mtmp = mod_tmp()
                    hn_tmp = hn_alloc(768)
                    qkbf = T([128, 768], BF16)
                    rl0 = R("l0c")
                    rg0 = R("gains0")
                    self.dma("sp", [(cos0[:, :, :], cf_d[:, CF_COS0:CF_COS0 + 1024].rearrange("p (t d) -> p t d", d=64)),
                                    (sin0[:, :, :], cf_d[:, CF_SIN0:CF_SIN0 + 1024].rearrange("p (t d) -> p t d", d=64))], (), [rl0], stream="c")
                    self.dma("sp", [(gains[:, :], gains_d[:, 0:128].partition_broadcast(128))], (), [rg0], stream="c")
                    self.memset("pool", Vp[:, :, :, 1, :], 1.0, [R("Vp1")])
                    groups = LAT512 + [CTXG]
                    with AR.scope():
                        band = T([128, 20, 128], BF16)
                        wblk = T([128, 2, 128], BF16)
                        dsb = T([128, 2, 128], BF16)
                        hT2 = T([128, 8, 512], BF16)
                        hnP = [hn_alloc(256) for i in range(2)]
                        qkP = [T([128, 256], BF16) for i in range(2)]
                        a_sb = AR.view_at(MIX_OFF + 2 * NT * 2, [128, 18, 256], BF16)
                        wkv = AR.view_at(MIX_OFF + 4 * NT * 2, [128, 8, 768], BF16)
                        rband = R("band")
                        self.dma("sp", [(band[:, :, :], cb_d[:, CB_BAND:CB_BAND + 2560].rearrange("p (t d) -> p t d", d=128))], (), [rband], stream="c")
                        self.memset("pool", wblk[:, :, :], 0.0, [R("wblk")])
                        self.dma("pool", [(wblk[(g % 2) * 64:(g % 2 + 1) * 64, g // 2, (g % 2) * 64:(g % 2 + 1) * 64], poolw_d[g]) for g in range(4)],
                                 (), [R("wblk")], stream="w")
                        rwkv = R("wkv")
                        load_w(wkv[:, :, :], w_in_kv0, rwkv)

                        def projA(tile, hTg, rh, par):
                            gi, t0, n, cnd, ti = tile
                            for nb, (c0, c1) in enumerate(((0, 512), (512, 768))):
                                b = 2 * par + nb
                                for c in range(8):
                                    self.mm(ps[:, b, 0:c1 - c0], hTg[:, c, ti * 128:(ti + 1) * 128], wkv[:, c, c0:c1], c == 0, c == 7, [rh, rwkv], [PB[b]])

                        def postB(tile, par):
                            gi, t0, n, cnd, ti = tile
                            tt = tile_of(t0) + ti
                            b0, b1 = 2 * par, 2 * par + 1
                            self.cp("act", a_sb[:, tt, :], ps[:, b0, 0:256], [PB[b0]], [R("a_sb", tt)])
                            yield
                            v_store(Vp, tt, ps[:, b1, 0:256].rearrange("p (h d) -> p h d", d=64), 2, [PB[b1]], [R("Vp", tt)])
                            yield
                            rope = None if cnd == 1 else (cos0[:, tt, :], sin0[:, tt, :], 0, 64, rl0)
                            kout = qkP[par][:, 0:256].rearrange("p (h d) -> p h d", d=64)
                            yield from headnorm_g([(ps[:, b0, 256:512].rearrange("p (h d) -> p h d", d=64), 0, 4, [PB[b0]])], 4, 64,
                                                  gains[:, 64:128], kout, hnP[par], [R("qkbf", par)], rg0, rope=rope, tag=str(par))
                            pb = transposes_to(qkP[par][:, 0:256], 2, 128, [R("qkbf", par)], 4 + par)
                            yield
                            self.cp("act", kT[:, :, tt * 128:(tt + 1) * 128], pb[:, 0:256].rearrange("p (m n) -> p m n", n=128), [PB[4 + par]], [R("kT", tt)])
                            yield
                        run_pass1(groups, 0, [hT, hT2], mtmp, projA, postB)
                        for tt in range(18):
                            first = tt in (0, 16)
                            last = tt in (15, 17)
                            bA, bB = (5, 6) if tt % 2 == 0 else (0, 1)
                            for g in range(4):
                                terms = [(tt, 3 if first else (4 if last else 0))]
                                if not first:
                                    terms.append((tt - 1, 1))
                                if not last:
                                    terms.append((tt + 1, 2))
                                for k, (ts_, var) in enumerate(terms):
                                    self.mm(ps[(g % 2) * 64:(g % 2 + 1) * 64, bA, (g // 2) * 128:(g // 2 + 1) * 128],
                                            a_sb[:, ts_, g * 64:(g + 1) * 64], band[:, g * 5 + var, :], k == 0, k == len(terms) - 1,
                                            [R("a_sb", ts_), rband], [PB[bA]])
                            self.cp("act", dsb[:, :, :], ps[:, bA, 0:256].rearrange("p (a n) -> p a n", n=128), [PB[bA]], [R("dsb")])
                            for pr in range(2):
                                self.mm(ps[:, bB, pr * 128:(pr + 1) * 128], wblk[:, pr, :], dsb[:, pr, :], True, True, [R("dsb"), R("wblk")], [PB[bB]])
                            for pr in range(2):
                                self.ts("dve", mixT[:, pr, tt * 128:(tt + 1) * 128], ps[:, bB, pr * 128:(pr + 1) * 128], vecs[:, V_PSC + pr:V_PSC + pr + 1],
                                        None, ALU.mult, None, [PB[bB], rc], [mres(pr, min(tt // 4, 4))])
                        P.barrier()
                    with AR.scope():
                        wq = T([128, 8, 768], BF16)
                        qT = [T([128, 6, 2, 512], BF16) for i in range(2)]
                        PT = [T([128, 512], BF16) for i in range(3)]
                        recs = T([128, 512], F32)
                        rwq = R("wq")
                        load_w(wq[:, :, :], w_in_q0, rwq)
                        for i in range(2):
                            self.memset("pool", qT[i][:, :, :, :], 0.0, [R("qT", i)])

                        def tile_g(tile, hTg, rh):
                            gi, t0, n, cnd, ti = tile
                            tt = tile_of(t0) + ti
                            for nb, (c0, c1) in enumerate(((0, 512), (512, 768))):
                                for c in range(8):
                                    self.mm(ps[:, 5 + nb, 0:c1 - c0], hTg[:, c, ti * 128:(ti + 1) * 128], wq[:, c, c0:c1], c == 0, c == 7, [rh, rwq], [PB[5 + nb]])
                                yield
                            rope = None if cnd == 1 else (cos0[:, tt, :], sin0[:, tt, :], 0, 64, rl0)
                            qout = qkbf[:, 0:768].rearrange("p (h d) -> p h d", d=64)
                            yield from headnorm_g([(ps[:, 5, 0:512].rearrange("p (h d) -> p h d", d=64), 0, 8, [PB[5]]),
                                                   (ps[:, 6, 0:256].rearrange("p (h d) -> p h d", d=64), 8, 12, [PB[6]])], 12, 64,
                                                  gains[:, 0:64], qout, hn_tmp, [R("qkbf")], rg0, rope=rope, tag="0")
                            qTg, rq = qT[gi % 2], R("qT", gi % 2)
                            pb = transposes_to(qkbf[:, 0:768], 6, 128, [R("qkbf")], 7)
                            yield
                            for hf in range(2):
                                hp_ = slice(hf * 64, hf * 64 + 64)
                                self.cp("dve", qTg[hp_, :, hf, ti * 128:(ti + 1) * 128], pb[hp_, 0:768].rearrange("p (m n) -> p m n", n=128), [PB[7]], [rq])
                                yield

                        def make_jobs(gi):
                            t0, n, cnd = groups[gi]
                            qTg, rq = qT[gi % 2], R("qT", gi % 2)
                            ktiles = [16, 17] + (list(range(16)) if cnd == 0 else [])
                            jobs = []
                            for hs in range(12):
                                g = Q_PERM[hs] // 3
                                half = hs % 2
                                assert g % 2 == half
                                pr = slice(half * 64, half * 64 + 64)
                                keys = []
                                for kt in ktiles:
                                    vl, oh = v_lhsT(Vp, kt, g)
                                    keys.append((kT[:, g // 2, kt * 128:(kt + 1) * 128], vl, [R("kT", kt), R("Vp", kt), R("Vp1")]))
                                jobs.append(dict(q=qTg[:, hs // 2, half, 0:n], keys=keys, rq=[rq], ohalf=oh,
                                                 dst=mixT[pr, 2 + hs // 2, t0:t0 + n], rdst=[mres(2 + hs // 2, min(t0 // 512, 4))]))
                            return jobs

                        def attend(jobs, grp, stepper):
                            attention(jobs, grp[1], 0.125, PT, recs, stepper)
                        run_pass2(groups, 0, hT, mtmp, tile_g, make_jobs, attend, rate=1)
                        P.barrier()
                final_proj(0, LAT512 + [CTXG])

            def layer1():
                with AR.scope():
                    T = AR.alloc
                    GO = 128
                    gains = T([128, G_N - GO], F32)
                    hT = T([128, 8, 512], BF16)
                    mtmp = mod_tmp()
                    hn_tmp = hn_alloc(384)
                    qkbf = T([128, 384], BF16)
                    hnP = [hn_tmp, hn_alloc(384)]
                    qkP = [qkbf, T([128, 384], BF16)]
                    PT = [T([128, 512], BF16) for i in range(3)]
                    recs = T([128, 512], F32)
                    rg1 = R("gains1")
                    self.dma("sp", [(gains[:, :], gains_d[:, GO:G_N].partition_broadcast(128))], (), [rg1], stream="c")
                    groups_all = LAT512 + [CTXG]

                    with AR.scope():
                        j2 = T([128, 128], BF16)
                        msk = T([128, NMASK, 128], BF16)
                        rna = R("nac")
                        self.dma("sp", [(j2[:, :], cb_d[:, CB_J2:CB_J2 + 128]),
                                        (msk[:, :, :], cb_d[:, CB_MASK:CB_MASK + NMASK * 128].rearrange("p (t d) -> p t d", d=128))], (), [rna], stream="c")
                        XH = T([128, 7, 4, 128], BF16)
                        kTn = T([128, 2, NT], BF16)
                        Vn = T([128, 18, 2, 3, 64], BF16)
                        self.memset("pool", Vn[:, :, :, 1, :], 1.0, [R("Vn1")])
                        for hh in range(2):
                            rxh = R("XH")
                            for off in range(-3, 4):
                                prs = []
                                for a in range(2):
                                    for b in range(2):
                                        dr = 2 * off + a - b
                                        base = (hh * 4 * 15 + dr + 7) * 127
                                        src = bass.AP(rpb_d.tensor, base, [[1, 64], [15 * 127, 4], [1, 64]])
                                        prs.append((XH[a * 64:(a + 1) * 64, off + 3, :, b * 64:(b + 1) * 64], src))
                                self.dma("pool", prs, (), [rxh], stream="w")
                            with AR.scope():
                                wkv = T([128, 8, 512], BF16)
                                hT2 = T([128, 8, 512], BF16)
                                rw = R("wnkv")
                                self.dma("pool", [(wkv[:, :, 0:256], w_in1[:, 1184 + hh * 256:1184 + (hh + 1) * 256].rearrange("(c p) n -> p c n", p=128)),
                                                  (wkv[:, :, 256:512], w_in1[:, 1696 + hh * 256:1696 + (hh + 1) * 256].rearrange("(c p) n -> p c n", p=128))],
                                         (), [rw], stream="w")

                                def projA(tile, hTg, rh, par):
                                    gi, t0, n, cnd, ti = tile
                                    for c in range(8):
                                        self.mm(ps[:, par, :], hTg[:, c, ti * 128:(ti + 1) * 128], wkv[:, c, :], c == 0, c == 7, [rh, rw], [PB[par]])

                                def postB(tile, par):
                                    gi, t0, n, cnd, ti = tile
                                    tt = tile_of(t0) + ti
                                    v_store(Vn, tt, ps[:, par, 256:512].rearrange("p (h d) -> p h d", d=64), 2, [PB[par]], [R("Vn", tt)])
                                    yield
                                    kout = qkP[par][:, 0:256].rearrange("p (h d) -> p h d", d=64)
                                    yield from headnorm_g([(ps[:, par, 0:256].rearrange("p (h d) -> p h d", d=64), 0, 4, [PB[par]])], 4, 64,
                                                          gains[:, G_NK - GO:G_NK - GO + 64], kout, hnP[par], [R("qkbf", par)], rg1, tag=str(par))
                                    pb = transposes_to(qkP[par][:, 0:256], 2, 128, [R("qkbf", par)], 4 + par)
                                    yield
                                    self.cp("act", kTn[:, :, tt * 128:(tt + 1) * 128], pb[:, 0:256].rearrange("p (m n) -> p m n", n=128), [PB[4 + par]], [R("kTn", tt)])
                                    yield
                                run_pass1(groups_all, 1, [hT, hT2], mtmp, projA, postB)
                                P.barrier()
                            with AR.scope():
                                wq = T([128, 8, 256], BF16)
                                qT = [T([128, 2, 2, 512], BF16) for i in range(2)]
                                rw = R("wnq")
                                load_w(wq[:, :, :], w_in1[:, 672 + hh * 256:672 + (hh + 1) * 256], rw)
                                for i in range(2):
                                    self.memset("pool", qT[i][:, :, :, :], 0.0, [R("qTn", i)])

                                def tile_g(tile, hTg, rh):
                                    gi, t0, n, cnd, ti = tile
                                    for c in range(8):
                                        self.mm(ps[:, 5, 0:256], hTg[:, c, ti * 128:(ti + 1) * 128], wq[:, c, :], c == 0, c == 7, [rh, rw], [PB[5]])
                                    yield
                                    qout = qkbf[:, 0:256].rearrange("p (h d) -> p h d", d=64)
                                    yield from headnorm_g([(ps[:, 5, 0:256].rearrange("p (h d) -> p h d", d=64), 0, 4, [PB[5]])], 4, 64,
                                                          gains[:, G_NQ - GO:G_NQ - GO + 64], qout, hn_tmp, [R("qkbf")], rg1, tag="0")
                                    qTg, rq = qT[gi % 2], R("qTn", gi % 2)
                                    pb = transposes_to(qkbf[:, 0:256], 2, 128, [R("qkbf")], 7)
                                    yield
                                    for hf in range(2):
                                        hp_ = slice(hf * 64, hf * 64 + 64)
                                        self.cp("dve", qTg[hp_, :, hf, ti * 128:(ti + 1) * 128], pb[hp_, 0:256].rearrange("p (m n) -> p m n", n=128), [PB[7]], [rq])
                                        yield

                                def make_jobs(gi):
                                    return gi

                                def attend(gi, grp, stepper):
                                    t0, n, cnd = grp
                                    qTg, rq = qT[gi % 2], R("qTn", gi % 2)
                                    for hl in range(4):
                                        h = hh * 4 + hl
                                        half = hl % 2
                                        m = hl // 2
                                        ob = OB_S[state["o"] % 2]
                                        state["o"] += 1
                                        oh = hl % 2
                                        for qt in range(4):
                                            i = gi * 4 + qt
                                            kts = [(16, None), (17, None)] + NA_PLAN[i]
                                            qa = qTg[:, m, half, qt * 128:(qt + 1) * 128]
                                            oc = ps[:, ob, qt * 128:(qt + 1) * 128]
                                            nkt = len(kts)
                                            chunks = [kts[c0:c0 + 4] for c0 in range(0, nkt, 4)]
                                            banks = []
                                            for chunk in chunks:
                                                b = SB_S[state["s"] % 3]
                                                state["s"] += 1
                                                banks.append(b)
                                                for bi, (j, mi) in enumerate(chunk):
                                                    sc = ps[:, b, bi * 128:(bi + 1) * 128]
                                                    self.mm(sc, kTn[:, m, j * 128:(j + 1) * 128], qa, True, mi is None, [R("kTn", j), rq], [PB[b]])
                                                    if mi is not None:
                                                        self.mm(sc, j2[:, :], XH[:, j - i + 3, hl, :], False, False, [rna, rxh], [PB[b]])
                                                        self.mm(sc, identb[:, :], msk[:, mi, :], False, True, [rna, rc], [PB[b]])
                                            done = 0
                                            for chunk, b in zip(chunks, banks):
                                                w_ = len(chunk) * 128
                                                pi = state["pt"] % 3
                                                state["pt"] += 1
                                                rp = R("PT", pi)
                                                self.act(PT[pi][:, 0:w_], psb(b, w_), AF.Exp, [PB[b]], [rp], scale=0.125)
                                                for bi, (j, mi) in enumerate(chunk):
                                                    vl, _ = v_lhsT(Vn, j, hl)
                                                    self.mm(oc, vl, PT[pi][:, bi * 128:(bi + 1) * 128], done == 0, done == nkt - 1,
                                                            [rp, R("Vn", j), R("Vn1")], [PB[ob]])
                                                    done += 1
                                                stepper()
                                        hp = slice((h % 2) * 64, (h % 2) * 64 + 64)
                                        finish_head(ob, 512, dict(ohalf=oh, dst=mixT[hp, 4 + h // 2, t0:t0 + 512], rdst=[mres(4 + h // 2, gi)]), recs)
                                run_pass2(LAT512, 1, hT, mtmp, tile_g, make_jobs, attend, rate=3)
                                P.barrier()

                    with AR.scope():
                        cos1 = T([128, 16, 32], F32)
                        sin1 = T([128, 16, 32], F32)
                        kTm = T([128, 4, NT], BF16)
                        Vm = T([128, 18, 2, 3, 64], BF16)
                        kfullP = [T([128, 4, 96], F32) for i in range(2)]
                        latbP = [T([128, 384], BF16) for i in range(2)]
                        latTP = [T([128, 3, 128], BF16) for i in range(2)]
                        rl1 = R("l1c")
                        self.dma("sp", [(cos1[:, :, :], cf_d[:, CF_COS1:CF_COS1 + 512].rearrange("p (t d) -> p t d", d=32)),
                                        (sin1[:, :, :], cf_d[:, CF_SIN1:CF_SIN1 + 512].rearrange("p (t d) -> p t d", d=32))], (), [rl1], stream="c")
                        self.memset("pool", Vm[:, :, :, 1, :], 1.0, [R("Vm1")])
                        ss1P = [T([128, 1], F32) for i in range(2)]
                        sd1P = [T([128, 1], F32) for i in range(2)]

                        def lora_norm_g(src_ps, width, gain_ap, rsrc, tbank, par):
                            sp_ = str(par)
                            sqv = hnP[par][0][:, 0:width]
                            ss1, sd1, lat_bf, latT = ss1P[par], sd1P[par], latbP[par], latTP[par]
                            self.act(sqv, src_ps, AF.Square, rsrc, [R("hn_sq" + sp_)])
                            yield
                            self.red(ss1[:, :], sqv, [R("hn_sq" + sp_)], [R("ss1", par)])
                            yield
                            self.act(sd1[:, :], ss1[:, :], AF.Ln, [R("ss1", par)], [R("sd1", par)], bias=EPS, scale=1.0 / width)
                            self.act(sd1[:, :], sd1[:, :], AF.Exp, [R("sd1", par)], [R("sd1", par)], scale=-0.5)
                            yield
                            self.stt("dve", lat_bf[:, 0:width], src_ps, sd1[:, 0:1], gain_ap, ALU.mult, ALU.mult, rsrc + [R("sd1", par), rg1], [R("lat_bf", par)])
                            yield "psum_done"
                            nb = width // 128
                            pb = transposes_to(lat_bf[:, 0:width], nb, 128, [R("lat_bf", par)], tbank)
                            yield
                            self.cp("act", latT[:, 0:nb, :], pb[:, 0:width].rearrange("p (m n) -> p m n", n=128), [PB[tbank]], [R("latT", par)])
                            yield

                        def lora_norm(src_ps, width, gain_ap, rsrc, tbank):
                            for _ in lora_norm_g(src_ps, width, gain_ap, rsrc, tbank, 0):
                                pass
                        latT = latTP[0]

                        for hh in range(2):
                            with AR.scope():
                                wkv = T([128, 8, 288], BF16)
                                hT2 = T([128, 8, 512], BF16)
                                wukv = T([128, 2, 512], BF16)
                                rw, rwu = R("wmkv"), R("wukv")
                                load_w(wkv[:, :, :], w_in1[:, 384:672], rw)
                                load_w(wukv[:, :, :], wukv_d[:, hh * 512:(hh + 1) * 512], rwu)

                                def projA(tile, hTg, rh, par):
                                    gi, t0, n, cnd, ti = tile
                                    for c in range(8):
                                        self.mm(ps[:, par, 0:288], hTg[:, c, ti * 128:(ti + 1) * 128], wkv[:, c, :], c == 0, c == 7, [rh, rw], [PB[par]])

                                def postB(tile, par):
                                    gi, t0, n, cnd, ti = tile
                                    tt = tile_of(t0) + ti
                                    kfull = kfullP[par]
                                    rkf = R("kfull", par)
                                    self.cp("act", kfull[:, :, 64:96], ps[:, par, 256:288].unsqueeze(1).to_broadcast([128, 4, 32]), [PB[par]], [rkf])
                                    yield
                                    yield from lora_norm_g(ps[:, par, 0:256], 256, gains[:, G_KVA - GO:G_KVA - GO + 256], [PB[par]], 2 + par, par)
                                    kb = 4 + par
                                    for c in range(2):
                                        self.mm(ps[:, kb, :], latTP[par][:, c, :], wukv[:, c, :], c == 0, c == 1, [R("latT", par), rwu], [PB[kb]])
                                    yield
                                    kvv = ps[:, kb, :].rearrange("p (h d) -> p h d", d=128)
                                    self.cp("act", kfull[:, :, 0:64], kvv[:, :, 0:64], [PB[kb]], [rkf])
                                    yield
                                    v_store(Vm, tt, kvv, 2, [PB[kb]], [R("Vm", tt)], d0=64)
                                    yield
                                    rope = None if cnd == 1 else (cos1[:, tt, :], sin1[:, tt, :], 64, 32, rl1)
                                    kout = qkP[par][:, 0:384].rearrange("p (h d) -> p h d", d=96)
                                    yield from headnorm_g([(kfull[:, :, :], 0, 4, [rkf])], 4, 96, gains[:, G_MK - GO:G_MK - GO + 96], kout, hnP[par],
                                                          [R("qkbf", par)], rg1, rope=rope, tag=str(par))
                                    pb = transposes_to(qkP[par][:, 0:384], 4, 96, [R("qkbf", par)], 6)
                                    yield
                                    self.cp("act", kTm[0:96, :, tt * 128:(tt + 1) * 128], pb[0:96, 0:512].rearrange("p (m n) -> p m n", n=128), [PB[6]], [R("kTm", tt)])
                                    yield
                                run_pass1(groups_all, 1, [hT, hT2], mtmp, projA, postB)
                                P.barrier()
                            with AR.scope():
                                wcq = T([128, 8, 384], BF16)
                                wuq = T([128, 3, 384], BF16)
                                qT = [T([128, 4, 512], BF16) for i in range(2)]
                                rw, rwu = R("wcq"), R("wuq")
                                load_w(wcq[:, :, :], w_in1[:, 0:384], rw)
                                load_w(wuq[:, :, :], wuq_d[:, hh * 384:(hh + 1) * 384], rwu)

                                def tile_g(tile, hTg, rh):
                                    gi, t0, n, cnd, ti = tile
                                    tt = tile_of(t0) + ti
                                    for c in range(8):
                                        self.mm(ps[:, 5, 0:384], hTg[:, c, ti * 128:(ti + 1) * 128], wcq[:, c, :], c == 0, c == 7, [rh, rw], [PB[5]])
                                    yield
                                    yield from lora_norm_g(ps[:, 5, 0:384], 384, gains[:, G_QA - GO:G_QA - GO + 384], [PB[5]], 7, 0)
                                    for c in range(3):
                                        self.mm(ps[:, 6, 0:384], latT[:, c, :], wuq[:, c, :], c == 0, c == 2, [R("latT", 0), rwu], [PB[6]])
                                    yield
                                    qout = qkbf[:, 0:384].rearrange("p (h d) -> p h d", d=96)
                                    yield from headnorm_g([(ps[:, 6, 0:384].rearrange("p (h d) -> p h d", d=96), 0, 4, [PB[6]])], 4, 96,
                                                          gains[:, G_MQ - GO:G_MQ - GO + 96], qout, hn_tmp, [R("qkbf")], rg1,
                                                          rope=(cos1[:, tt, :], sin1[:, tt, :], 64, 32, rl1), tag="0")
                                    qTg, rq = qT[gi % 2], R("qTm", gi % 2)
                                    pb = transposes_to(qkbf[:, 0:384], 4, 96, [R("qkbf")], 7)
                                    yield
                                    self.cp("dve", qTg[0:96, :, ti * 128:(ti + 1) * 128], pb[0:96, 0:512].rearrange("p (m n) -> p m n", n=128), [PB[7]], [rq])
                                    yield

                                def make_jobs(gi):
                                    t0, n, cnd = LAT512[gi]
                                    qTg, rq = qT[gi % 2], R("qTm", gi % 2)
                                    ktiles = [16, 17] + list(range(16))
                                    jobs = []
                                    for hl in range(4):
                                        h = hh * 4 + hl
                                        hp = slice((h % 2) * 64, (h % 2) * 64 + 64)
                                        keys = []
                                        for kt in ktiles:
                                            vl, oh = v_lhsT(Vm, kt, hl)
                                            keys.append((kTm[0:96, hl, kt * 128:(kt + 1) * 128], vl, [R("kTm", kt), R("Vm", kt), R("Vm1")]))
                                        jobs.append(dict(q=qTg[0:96, hl, 0:512], keys=keys, rq=[rq], ohalf=oh,
                                                         dst=mixT[hp, h // 2, t0:t0 + 512], rdst=[mres(h // 2, gi)]))
                                    return jobs

                                def attend(jobs, grp, stepper):
                                    attention(jobs, 512, 96.0 ** -0.5, PT, recs, stepper)
                                run_pass2(LAT512, 1, hT, mtmp, tile_g, make_jobs, attend, rate=2)
                                P.barrier()
                final_proj(1, LAT512)


            if stage != "load":
                layer0()
                if stage != "l0mix":
                    ffn(0, [(0, 1024, 0), (1024, 1024, 0), CTXG])
                    if stage != "l0":
                        layer1()
                        if stage != "l1mix":
                            ffn(1, [(0, 1024, 0), (1024, 1024, 0)])

            with AR.scope() as s1:
                yst = [s1.enter_context(AR.alloc([128, D], F32)) for i in range(2)]
                for tt in range(nout // 128):
                    slot = tt % 2
                    b0 = 2 * (tt % 2)
                    ry = R("yst", slot)
                    for c in range(8):
                        self.tr(ps[:, b0 + c // 4, (c % 4) * 128:(c % 4 + 1) * 128], xT[:, c, tt * 128:(tt + 1) * 128], identf[:, :],
                                xr(tt * 128, 128, [c]) + [rc], [PB[b0 + c // 4]])
                    for hh in range(2):
                        self.cp("act" if hh == 0 else "dve", yst[slot][:, hh * 512:(hh + 1) * 512], ps[:, b0 + hh, :], [PB[b0 + hh]], [ry])
                    self.dma("sp", [(y_d[tt * 128:(tt + 1) * 128, :], yst[slot][:, :])], [ry], [R("ydram", tt)], stream="out")
                self.nwait = P.emit()
                self.nops = len(P.ops)
        return nc


_CONSTS = None


def _prep_shared(inputs):
    global _CONSTS
    if _CONSTS is None:
        _CONSTS = _const_arrays()
    cb, cf = _CONSTS
    f = lambda a: np.ascontiguousarray(np.asarray(a, dtype=np.float32))
    fm = lambda v: f(v).reshape(-1, 128).T
    vecs = np.zeros((128, V_N), np.float32)
    vecs[:, 0:48] = fm(inputs["l0_ada_b"])
    vecs[:, 48:96] = fm(inputs["l1_ada_b"])
    vecs[:, V_NMIX0:V_NMIX0 + 8] = fm(inputs["l0_norm_mix"])
    vecs[:, V_NFFN0:V_NFFN0 + 8] = fm(inputs["l0_norm_ffn"])
    vecs[:, V_NMIX1:V_NMIX1 + 8] = fm(inputs["l1_norm_mix"])
    vecs[:, V_NFFN1:V_NFFN1 + 8] = fm(inputs["l1_norm_ffn"])
    vecs[:, V_PSC:V_PSC + 2] = fm(inputs["l0_pool_scale"])
    gains = np.concatenate([f(inputs[k]).reshape(-1) for k in
                            ("l0_q_gain", "l0_k_gain", "l1_mla_q_a_gain", "l1_mla_kv_a_gain", "l1_mla_q_gain", "l1_mla_k_gain",
                             "l1_na_q_gain", "l1_na_k_gain")]).reshape(1, G_N)
    w_in0 = f(inputs["l0_w_in"])
    qcols = np.concatenate([np.arange(256 + h * 64, 256 + (h + 1) * 64) for h in Q_PERM])
    w_in_q = np.ascontiguousarray(w_in0[:, qcols])
    w_in_kv = np.ascontiguousarray(np.concatenate([w_in0[:, 0:256], w_in0[:, 1024:1536]], axis=1))
    w_out0 = f(inputs["l0_w_out"])
    orow = np.concatenate([np.arange(256)] + [np.arange(256 + h * 64, 256 + (h + 1) * 64) for h in Q_PERM])
    w_out_p = np.ascontiguousarray(w_out0[orow, :])
    rpb = f(inputs["l1_na_rpb"])
    rp = np.zeros((8, 15, 127), np.float32)
    rp[:, :, 48:79] = rpb[:, :, ::-1]
    rpbR = np.concatenate([rp.reshape(-1), np.zeros(128, np.float32)])
    shared = {
        "vecs": vecs, "gains": gains, "cbf": cb, "cf32": cf, "rpbR": rpbR,
        "l0_ada_w": f(inputs["l0_ada_w"]), "l1_ada_w": f(inputs["l1_ada_w"]),
        "l0_w_in_kv": w_in_kv, "l0_w_in_q": w_in_q, "l0_pool_w": f(inputs["l0_pool_w"]),
        "l0_w_out_p": w_out_p, "l1_w_out": f(inputs["l1_w_out"]), "l1_w_in": f(inputs["l1_w_in"]),
        "l1_mla_w_uq": f(inputs["l1_mla_w_uq"]), "l1_mla_w_ukv": f(inputs["l1_mla_w_ukv"]),
    }
    for l in range(2):
        for nm in ("gate", "up", "down"):
            shared[f"l{l}_ffn_w_{nm}"] = f(inputs[f"l{l}_ffn_w_{nm}"])
    return shared


def _run(inputs, stage="full", cores=8, trace=False):
    shared = _prep_shared(inputs)
    x = np.asarray(inputs["x"], dtype=np.float32)
    c = np.asarray(inputs["c"], dtype=np.float32)
    ctx = np.asarray(inputs["ctx"], dtype=np.float32)
    c_ctx = np.asarray(inputs["c_ctx"], dtype=np.float32)
    in_maps = []
    for b in range(cores):
        cond = np.concatenate([c[b].reshape(8, 128).T, c_ctx.reshape(8, 128).T], axis=1)
        m = dict(shared)
        m["x"] = np.ascontiguousarray(x[b])
        m["ctx"] = np.ascontiguousarray(ctx[b])
        m["condT"] = np.ascontiguousarray(cond)
        in_maps.append(m)
    bld = Builder(stage)
    nc = bld.build()
    used = set()
    res = run_bass_kernel_spmd(nc, in_maps, core_ids=list(range(cores)), **({"trace": True} if trace else {}))
    return res, bld


def kernel(**inputs):
    res, _ = _run(inputs, STAGE, 8)
    return np.stack([np.asarray(r["y"], dtype=np.float32) for r in res.results], axis=0)
```
